# Optimizing a Trainium2 kernel written in Bass

```python
import math
import jax, jax.numpy as jnp
from jax import lax
import numpy as np

D_MODEL = 1024
BATCH = 8
SEQ = 4096
DEPTH = 2

GRID_W = 64
CTX_LEN = 256
Q_BLOCK = 128
ROPE_THETA = 10000.0
NORM_EPS = 1e-6
MLA_HEADS = 8
MLA_Q_RANK = 384
MLA_KV_RANK = 256
MLA_NOPE = 64
MLA_ROPE = 32
MLA_V = 64
MLA_SCALE = (MLA_NOPE + MLA_ROPE) ** -0.5
GQA_HEADS = 8
GQA_KV_HEADS = 2
GQA_HEAD_DIM = 64
GQA_SCALE = GQA_HEAD_DIM ** -0.5
RWKV_HEADS = 8
RWKV_HEAD = 64
RWKV_WIDTH = RWKV_HEADS * RWKV_HEAD
DECAY_LORA = 64
ICLR_LORA = 64
GATE_LORA = 128
N_DIR = 2
CONV_WIDTH = 3
GN_EPS = 64e-5
N_BRANCH = 3
BRANCH_WIDTH = 512
FFN_HIDDEN = -(-8 * D_MODEL // (3 * 256)) * 256
IN_SPLITS = (MLA_Q_RANK, MLA_KV_RANK, MLA_ROPE,
             GQA_HEADS * GQA_HEAD_DIM, GQA_KV_HEADS * GQA_HEAD_DIM, GQA_KV_HEADS * GQA_HEAD_DIM,
             3 * RWKV_WIDTH, N_DIR * DECAY_LORA, N_DIR * ICLR_LORA, GATE_LORA,
             N_BRANCH * D_MODEL)
IN_WIDTH = sum(IN_SPLITS)
IN_OFFSETS = tuple(np.cumsum(IN_SPLITS)[:-1].tolist())

kernel_name = 'hybrid_mla_gqa_rwkv7_prefix_dit_block'

F32 = jnp.float32


def rms_norm(x, g, eps=NORM_EPS):
    xf = x.astype(F32)
    y = xf * lax.rsqrt(jnp.mean(xf * xf, axis=-1, keepdims=True) + eps)
    return (y * g.astype(F32)).astype(x.dtype)


def axial_rope_tables(row, col, dim):
    quarter = dim // 4
    freqs = ROPE_THETA ** (-jnp.arange(quarter, dtype=F32) / quarter)
    ang = jnp.concatenate([row[:, None] * freqs, col[:, None] * freqs], axis=-1)
    return jnp.cos(ang), jnp.sin(ang)


def apply_rope(x, cos, sin):
    half = x.shape[-1] // 2
    c = cos[None, :, None, :].astype(x.dtype)
    s = sin[None, :, None, :].astype(x.dtype)
    x1, x2 = x[..., :half], x[..., half:]
    return jnp.concatenate([x1 * c - x2 * s, x1 * s + x2 * c], axis=-1)


def short_conv(x, taps):
    t = x.shape[1]
    xp = jnp.pad(x, ((0, 0), (1, 1), (0, 0)))
    return taps[0] * xp[:, :t] + taps[1] * xp[:, 1:t + 1] + taps[2] * xp[:, 2:t + 2]


def block_attention(q, k, v, scale):
    b, t, hk, g, dk = q.shape
    nb = t // Q_BLOCK
    qb = jnp.moveaxis(q.reshape(b, nb, Q_BLOCK, hk, g, dk), 1, 0)

    def one_block(qi):
        s = jnp.einsum('bqhgd,bkhd->bhgqk', qi, k).astype(F32) * scale
        p = jax.nn.softmax(s, axis=-1).astype(v.dtype)
        return jnp.einsum('bhgqk,bkhd->bqhgd', p, v)

    o = lax.map(one_block, qb)
    return jnp.moveaxis(o, 0, 1).reshape(b, t, hk * g * v.shape[-1])


def mla_heads(cq, ckv, kr, lp, rope):
    b, t, _ = cq.shape
    q = (rms_norm(cq, lp['mla_q_norm']) @ lp['mla_w_uq']).reshape(b, t, MLA_HEADS, MLA_NOPE + MLA_ROPE)
    kv = (rms_norm(ckv, lp['mla_kv_norm']) @ lp['mla_w_ukv']).reshape(b, t, MLA_HEADS, MLA_NOPE + MLA_V)
    q_nope, q_rot = q[..., :MLA_NOPE], q[..., MLA_NOPE:]
    k_nope, v = kv[..., :MLA_NOPE], kv[..., MLA_NOPE:]
    k_rot = kr[:, :, None, :]
    if rope is not None:
        q_rot = apply_rope(q_rot, *rope)
        k_rot = apply_rope(k_rot, *rope)
    k = jnp.concatenate([k_nope, jnp.broadcast_to(k_rot, (b, t, MLA_HEADS, MLA_ROPE))], axis=-1)
    q = jnp.concatenate([q_nope, q_rot], axis=-1)
    return q[:, :, :, None, :], k, v


def gqa_heads(qr, kr, vr, lp, rope):
    b, t, _ = qr.shape
    q = rms_norm(qr.reshape(b, t, GQA_HEADS, GQA_HEAD_DIM), lp['gqa_q_norm'])
    k = rms_norm(kr.reshape(b, t, GQA_KV_HEADS, GQA_HEAD_DIM), lp['gqa_k_norm'])
    v = vr.reshape(b, t, GQA_KV_HEADS, GQA_HEAD_DIM)
    if rope is not None:
        q = apply_rope(q, *rope)
        k = apply_rope(k, *rope)
    q = q.reshape(b, t, GQA_KV_HEADS, GQA_HEADS // GQA_KV_HEADS, GQA_HEAD_DIM)
    return q, k, v


def rwkv_inputs(rkv_raw, w_lo, a_lo, g_lo, lp):
    b, t, _ = rkv_raw.shape
    rkv = short_conv(rkv_raw, lp['rwkv_conv'])
    r, k, v = jnp.split(rkv, 3, axis=-1)
    heads = lambda z: z.reshape(z.shape[:-1] + (RWKV_HEADS, RWKV_HEAD))
    w_h = jnp.tanh(w_lo.reshape(b, t, N_DIR, DECAY_LORA))
    w_raw = (jnp.einsum('btdr,drc->dbtc', w_h, lp['rwkv_w2']) + lp['rwkv_w0'][:, None, None, :]).astype(F32)
    decay = jnp.exp(-jnp.exp(-jax.nn.softplus(-w_raw) - 0.5))
    a = jax.nn.sigmoid((jnp.einsum('btdr,drc->dbtc', a_lo.reshape(b, t, N_DIR, ICLR_LORA), lp['rwkv_a2'])
                        + lp['rwkv_a0'][:, None, None, :]).astype(F32))
    g = jax.nn.sigmoid(g_lo) @ lp['rwkv_g2']
    kk = heads(k * lp['rwkv_k_k']).astype(F32)
    kk = kk / jnp.maximum(jnp.sqrt(jnp.sum(kk * kk, axis=-1, keepdims=True)), 1e-12)
    k_mod = k.astype(F32)[None] * (1.0 + (a - 1.0) * lp['rwkv_k_a'].astype(F32))
    return {'r': heads(r).astype(F32), 'w': heads(decay), 'k': heads(k_mod), 'v': heads(v).astype(F32),
            'kk': kk, 'a': heads(a), 'g': g}


def rwkv_scan(state0, r, w, k, v, kk, a):
    def orient(z):
        return jnp.stack([z[0], jnp.flip(z[1], axis=1)])

    shared = lambda z: jnp.broadcast_to(z[None], w.shape)
    xs = tuple(jnp.moveaxis(orient(z), 2, 0) for z in (shared(r), w, k, shared(v), shared(kk), a))

    def step(s, inp):
        r_t, w_t, k_t, v_t, kk_t, a_t = inp
        sa = jnp.einsum('dbhij,dbhj->dbhi', s, -kk_t)
        s = s * w_t[..., None, :] + sa[..., :, None] * (kk_t * a_t)[..., None, :] + v_t[..., :, None] * k_t[..., None, :]
        return s, jnp.einsum('dbhij,dbhj->dbhi', s, r_t)

    s_final, o = lax.scan(step, state0, xs)
    return orient(jnp.moveaxis(o, 0, 2)), s_final


def rwkv_branch(inp, state0, lp):
    o, s_final = rwkv_scan(state0, inp['r'], inp['w'], inp['k'], inp['v'], inp['kk'], inp['a'])
    y = jnp.sum(o, axis=0)
    b, t = y.shape[:2]
    mu = jnp.mean(y, axis=-1, keepdims=True)
    var = jnp.mean(jnp.square(y - mu), axis=-1, keepdims=True)
    y = ((y - mu) * lax.rsqrt(var + GN_EPS)).reshape(b, t, RWKV_WIDTH)
    y = y * lp['rwkv_ln_g'].astype(F32) + lp['rwkv_ln_b'].astype(F32)
    bonus = jnp.sum(jnp.sum(inp['r'][None] * inp['k'] * lp['rwkv_r_k'].astype(F32), axis=-1, keepdims=True)
                    * inp['v'][None], axis=0).reshape(b, t, RWKV_WIDTH)
    out = (y + bonus) * inp['g'].astype(F32)
    return out.astype(inp['g'].dtype), s_final


def merge_branches(branches, gate_raw, w_branch, w_out):
    b, t, _ = gate_raw.shape
    gates = jax.nn.sigmoid(gate_raw.reshape(b, t, N_BRANCH, D_MODEL))
    acc = gates[:, :, 0] * (branches[0] @ w_branch[0])
    for n in range(1, N_BRANCH):
        acc = acc + gates[:, :, n] * (branches[n] @ w_branch[n])
    return acc @ w_out


def token_mixer(h_lat, h_ctx, lp, rope_mla, rope_gqa, need_ctx_out):
    pl = jnp.split(h_lat @ lp['w_in'], IN_OFFSETS, axis=-1)
    pc = jnp.split(h_ctx @ lp['w_in'], IN_OFFSETS, axis=-1)
    qa_l, ka_l, va_l = mla_heads(pl[0], pl[1], pl[2], lp, rope_mla)
    qa_c, ka_c, va_c = mla_heads(pc[0], pc[1], pc[2], lp, None)
    oa_l = block_attention(qa_l, jnp.concatenate([ka_l, ka_c], axis=1), jnp.concatenate([va_l, va_c], axis=1), MLA_SCALE)
    qb_l, kb_l, vb_l = gqa_heads(pl[3], pl[4], pl[5], lp, rope_gqa)
    qb_c, kb_c, vb_c = gqa_heads(pc[3], pc[4], pc[5], lp, None)
    ob_l = block_attention(qb_l, jnp.concatenate([kb_l, kb_c], axis=1), jnp.concatenate([vb_l, vb_c], axis=1), GQA_SCALE)
    b = h_lat.shape[0]
    state0 = jnp.zeros((N_DIR, b, RWKV_HEADS, RWKV_HEAD, RWKV_HEAD), F32)
    oc_c, s_ctx = rwkv_branch(rwkv_inputs(pc[6], pc[7], pc[8], pc[9], lp), state0, lp)
    oc_l, _ = rwkv_branch(rwkv_inputs(pl[6], pl[7], pl[8], pl[9], lp), s_ctx, lp)
    y_lat = merge_branches([oa_l, ob_l, oc_l], pl[10], lp['w_branch'], lp['w_out'])
    if not need_ctx_out:
        return y_lat, None
    oa_c = block_attention(qa_c, ka_c, va_c, MLA_SCALE)
    ob_c = block_attention(qb_c, kb_c, vb_c, GQA_SCALE)
    y_ctx = merge_branches([oa_c, ob_c, oc_c], pc[10], lp['w_branch'], lp['w_out'])
    return y_lat, y_ctx


def swiglu(h, w_in, w_out):
    gate, up = jnp.split(h @ w_in, 2, axis=-1)
    return (jax.nn.silu(gate) * up) @ w_out


def setup_inputs(seed: int = 0) -> dict:
    key = jax.random.key(seed)
    counter = iter(range(1000))
    L, D = DEPTH, D_MODEL

    def nrm(shape, scale):
        return scale * jax.random.normal(jax.random.fold_in(key, next(counter)), shape, F32)

    def gain(*shape):
        return 1.0 + nrm((L,) + shape, 0.05)

    return {
        'x': nrm((BATCH, SEQ, D), 1.0),
        'c': nrm((BATCH, D), 1.0),
        'ctx': nrm((BATCH, CTX_LEN, D), 1.0),
        'c_ctx': nrm((D,), 1.0),
        'ada_w': nrm((L, D, 6 * D), 0.5 * D ** -0.5),
        'ada_b': nrm((L, 6 * D), 0.02),
        'attn_pre_g': gain(D),
        'attn_post_g': gain(D),
        'ffn_pre_g': gain(D),
        'ffn_post_g': gain(D),
        'w_in': nrm((L, D, IN_WIDTH), D ** -0.5),
        'mla_q_norm': gain(MLA_Q_RANK),
        'mla_kv_norm': gain(MLA_KV_RANK),
        'mla_w_uq': nrm((L, MLA_Q_RANK, MLA_HEADS * (MLA_NOPE + MLA_ROPE)), MLA_Q_RANK ** -0.5),
        'mla_w_ukv': nrm((L, MLA_KV_RANK, MLA_HEADS * (MLA_NOPE + MLA_V)), MLA_KV_RANK ** -0.5),
        'gqa_q_norm': gain(GQA_HEAD_DIM),
        'gqa_k_norm': gain(GQA_HEAD_DIM),
        'rwkv_conv': nrm((L, CONV_WIDTH, 3 * RWKV_WIDTH), 0.1) + jnp.array([0.25, 0.5, 0.25], F32)[None, :, None],
        'rwkv_w0': nrm((L, N_DIR, RWKV_WIDTH), 1.0) - 2.0,
        'rwkv_w2': nrm((L, N_DIR, DECAY_LORA, RWKV_WIDTH), 0.1),
        'rwkv_a0': nrm((L, N_DIR, RWKV_WIDTH), 0.5),
        'rwkv_a2': nrm((L, N_DIR, ICLR_LORA, RWKV_WIDTH), 0.1),
        'rwkv_g2': nrm((L, GATE_LORA, RWKV_WIDTH), GATE_LORA ** -0.5),
        'rwkv_k_k': 0.85 + nrm((L, RWKV_WIDTH), 0.05),
        'rwkv_k_a': 1.0 + nrm((L, RWKV_WIDTH), 0.05),
        'rwkv_r_k': nrm((L, RWKV_HEADS, RWKV_HEAD), 0.1),
        'rwkv_ln_g': gain(RWKV_WIDTH),
        'rwkv_ln_b': nrm((L, RWKV_WIDTH), 0.02),
        'w_branch': nrm((L, N_BRANCH, BRANCH_WIDTH, D), BRANCH_WIDTH ** -0.5),
        'w_out': nrm((L, D, D), D ** -0.5),
        'ffn_w_in': nrm((L, D, 2 * FFN_HIDDEN), D ** -0.5),
        'ffn_w_out': nrm((L, FFN_HIDDEN, D), FFN_HIDDEN ** -0.5),
    }


def reference(x, c, ctx, c_ctx, ada_w, ada_b, attn_pre_g, attn_post_g, ffn_pre_g, ffn_post_g,
              w_in, mla_q_norm, mla_kv_norm, mla_w_uq, mla_w_ukv, gqa_q_norm, gqa_k_norm,
              rwkv_conv, rwkv_w0, rwkv_w2, rwkv_a0, rwkv_a2, rwkv_g2, rwkv_k_k, rwkv_k_a, rwkv_r_k,
              rwkv_ln_g, rwkv_ln_b, w_branch, w_out, ffn_w_in, ffn_w_out):
    n = x.shape[1]
    rows = n // GRID_W
    ri, ci = jnp.meshgrid(jnp.arange(rows, dtype=F32), jnp.arange(GRID_W, dtype=F32), indexing='ij')
    row, col = ri.reshape(-1), ci.reshape(-1)
    rope_mla = axial_rope_tables(row, col, MLA_ROPE)
    rope_gqa = axial_rope_tables(row, col, GQA_HEAD_DIM)
    silu_c = jax.nn.silu(c)
    silu_cc = jax.nn.silu(c_ctx)
    for l in range(DEPTH):
        last = l == DEPTH - 1
        lp = dict(w_in=w_in[l], mla_q_norm=mla_q_norm[l], mla_kv_norm=mla_kv_norm[l], mla_w_uq=mla_w_uq[l],
                  mla_w_ukv=mla_w_ukv[l], gqa_q_norm=gqa_q_norm[l], gqa_k_norm=gqa_k_norm[l],
                  rwkv_conv=rwkv_conv[l], rwkv_w0=rwkv_w0[l], rwkv_w2=rwkv_w2[l], rwkv_a0=rwkv_a0[l],
                  rwkv_a2=rwkv_a2[l], rwkv_g2=rwkv_g2[l], rwkv_k_k=rwkv_k_k[l], rwkv_k_a=rwkv_k_a[l],
                  rwkv_r_k=rwkv_r_k[l], rwkv_ln_g=rwkv_ln_g[l], rwkv_ln_b=rwkv_ln_b[l],
                  w_branch=w_branch[l], w_out=w_out[l])
        sh1, sc1, g1, sh2, sc2, g2 = jnp.split((silu_c @ ada_w[l] + ada_b[l])[:, None, :], 6, axis=-1)
        csh1, csc1, cg1, csh2, csc2, cg2 = jnp.split(silu_cc @ ada_w[l] + ada_b[l], 6, axis=-1)
        h_lat = rms_norm(x, attn_pre_g[l]) * (1.0 + sc1) + sh1
        h_ctx = rms_norm(ctx, attn_pre_g[l]) * (1.0 + csc1) + csh1
        y_lat, y_ctx = token_mixer(h_lat, h_ctx, lp, rope_mla, rope_gqa, not last)
        x = x + g1 * rms_norm(y_lat, attn_post_g[l])
        f_lat = swiglu(rms_norm(x, ffn_pre_g[l]) * (1.0 + sc2) + sh2, ffn_w_in[l], ffn_w_out[l])
        x = x + g2 * rms_norm(f_lat, ffn_post_g[l])
        if not last:
            ctx = ctx + cg1 * rms_norm(y_ctx, attn_post_g[l])
            f_ctx = swiglu(rms_norm(ctx, ffn_pre_g[l]) * (1.0 + csc2) + csh2, ffn_w_in[l], ffn_w_out[l])
            ctx = ctx + cg2 * rms_norm(f_ctx, ffn_post_g[l])
    return x
```

```python
import numpy as np
from contextlib import ExitStack
import concourse.bass as bass
import concourse.mybir as mybir
from concourse.bass_utils import run_bass_kernel_spmd

F32 = mybir.dt.float32
BF16 = mybir.dt.bfloat16
AF = mybir.ActivationFunctionType
ALU = mybir.AluOpType
AX = mybir.AxisListType

ENGS = ("pe", "dve", "act", "pool", "sp")
EPOCH = 12000
N_DMA_SLOTS = 8

D = 1024
NCTX = 256
NLAT = 4096
T = NCTX + NLAT
DEPTH = 2
GRID_W = 64
EPS = 1e-6
GN_EPS = 64e-5
MLA_SCALE = 96 ** -0.5
GQA_SCALE = 64 ** -0.5
FFH = 2816
IN_W = 6432
O_CQ, O_CKV, O_KR, O_GQ, O_GK, O_GV, O_RKV, O_WLO, O_ALO, O_GLO, O_GATE = (
    0, 384, 640, 672, 1184, 1312, 1440, 2976, 3104, 3232, 3360)
SPANS = [(0, 256)] + [(256 + 512 * i, 512) for i in range(8)]
CH = 64
RW_DT = F32
NCHUNK = T // CH

NF = 96
F_PRE, F_FPRE, F_QN, F_KVN, F_GQ, F_GK, F_CONV, F_W0, F_A0, F_KK, F_KA, F_RK, F_LNG, F_LNB = (
    0, 8, 16, 19, 21, 22, 23, 59, 67, 75, 79, 83, 87, 91)
NR = 8192
R_ADAB, R_POST, R_FPOST = 0, 6144, 7168


PSUM_PREFIXES = ("mps", "tp", "ips", "pss", "pq", "pr", "pso", "pw", "pa", "pn", "pb", "Xps", "tps", "ptr", "pg", "pm",
                 "mpy", "fpy")


class Prog:
    def __init__(self, nc, stack, same_engine_sync=False):
        self.nc = nc
        self.stack = stack
        self.same_engine_sync = same_engine_sync
        self.q = {e: [] for e in ENGS}
        self.cnt = {e: 0 for e in ENGS}
        self.esems = {e: [] for e in ENGS}
        self.waited = {e: {} for e in ENGS}
        self.state = {}
        self.dslots = {}
        self.dnext = {}
        for e in ("sp", "act", "pool"):
            self.dslots[e] = [[self._newsem(f"d{e}{i}"), 0] for i in range(N_DMA_SLOTS)]
            self.dnext[e] = 0
        self.ninstr = 0

    def _newsem(self, name):
        return self.stack.enter_context(self.nc.semaphore(name))

    def _my_event(self, eng):
        c = self.cnt[eng]
        ep = c // EPOCH
        while len(self.esems[eng]) <= ep:
            self.esems[eng].append(self._newsem(f"e{eng}{len(self.esems[eng])}"))
        self.cnt[eng] = c + 1
        return (self.esems[eng][ep], c - ep * EPOCH + 1, eng)

    def _deps(self, eng, reads, writes, self_sync=False):
        need = {}

        def add(ev):
            if ev is None:
                return
            sem, val, src = ev
            if src == eng and (eng == "pe" or not (self.same_engine_sync or self_sync)):
                return
            k = id(sem)
            if k not in need or need[k][1] < val:
                need[k] = (sem, val)

        for k in reads:
            st = self.state.get(k)
            if st is not None:
                add(st["w"])
                if isinstance(k, str) and k.startswith(PSUM_PREFIXES):
                    for ev in st["r"].values():
                        if ev[2] != eng:
                            add(ev)
        for k in writes:
            st = self.state.get(k)
            if st is not None:
                add(st["w"])
                for ev in st["r"].values():
                    add(ev)
        out = []
        wd = self.waited[eng]
        for k, (sem, val) in need.items():
            if wd.get(k, 0) >= val:
                continue
            wd[k] = val
            out.append((sem, val))
        return out

    def _commit(self, ev, reads, writes):
        for k in writes:
            self.state[k] = {"w": ev, "r": {}}
        for k in reads:
            if k in writes:
                continue
            st = self.state.setdefault(k, {"w": None, "r": {}})
            st["r"][id(ev[0])] = ev

    def op(self, eng, fn, r=(), w=(), ss=False):
        r = list(r)
        w = list(w)
        waits = self._deps(eng, r, w, self_sync=ss)
        ev = self._my_event(eng)
        sem = ev[0]

        def run(e, fn=fn, waits=waits, sem=sem):
            for s, v in waits:
                e.wait_ge(s, v)
            fn(e).then_inc(sem, 1)

        self.q[eng].append(run)
        self._commit(ev, r, w)
        self.ninstr += 1

    def dma(self, qeng, out, in_, r=(), w=(), **kw):
        r = list(r)
        w = list(w)
        waits = self._deps(qeng, r, w)
        i = self.dnext[qeng]
        self.dnext[qeng] = (i + 1) % N_DMA_SLOTS
        slot = self.dslots[qeng][i]
        sem, tot = slot
        wd = self.waited[qeng]
        if tot > 0 and wd.get(id(sem), 0) < tot:
            waits.append((sem, tot))
            wd[id(sem)] = tot
        slot[1] = tot + 16
        ev = (sem, tot + 16, "dma")

        def run(e, waits=waits, sem=sem, out=out, in_=in_, kw=kw):
            for s, v in waits:
                e.wait_ge(s, v)
            e.dma_start(out=out, in_=in_, **kw).then_inc(sem, 16)

        self.q[qeng].append(run)
        self._commit(ev, r, w)
        self.ninstr += 1

    def barrier(self):
        evs = []
        for e in ENGS:
            c = self.cnt[e]
            if c > 0:
                ep = (c - 1) // EPOCH
                evs.append((self.esems[e][ep], c - ep * EPOCH, e))
        for q in self.dslots:
            for sem, tot in self.dslots[q]:
                if tot > 0:
                    evs.append((sem, tot, "dma"))
        for e in ENGS:
            wd = self.waited[e]
            waits = []
            for s, v, src in evs:
                if src == e:
                    continue
                if wd.get(id(s), 0) < v:
                    wd[id(s)] = v
                    waits.append((s, v))

            def run(eng, waits=waits):
                for s, v in waits:
                    eng.wait_ge(s, v)

            self.q[e].append(run)
        self.state = {}

    def emit(self):
        nc = self.nc
        with nc.Block() as block:
            @block.tensor
            def _(e):
                for f in self.q["pe"]:
                    f(e)

            @block.vector
            def _(e):
                for f in self.q["dve"]:
                    f(e)

            @block.scalar
            def _(e):
                for f in self.q["act"]:
                    f(e)

            @block.gpsimd
            def _(e):
                for f in self.q["pool"]:
                    f(e)

            @block.sync
            def _(e):
                for f in self.q["sp"]:
                    f(e)


def _host_consts():
    c = {}
    c["ident"] = np.eye(128, dtype=np.float32)
    c["ones"] = np.ones((128, 128), np.float32)
    bo = np.zeros((128, 128), np.float32)
    bo[:64, :64] = 1
    bo[64:, 64:] = 1
    c["bones"] = bo
    pm = np.zeros((128, 128), np.float32)
    for m in range(128):
        b, i = divmod(m, 64)
        if i < 32:
            pm[b * 64 + i + 32, m] = -1.0
        else:
            pm[b * 64 + i - 32, m] = 1.0
    c["perm"] = pm
    c["stackI"] = np.concatenate([np.eye(64, dtype=np.float32)] * 2, axis=0)
    s = np.arange(64)[:, None]
    t = np.arange(64)[None, :]

    def bd(m):
        z = np.zeros((128, 128), np.float32)
        z[:64, :64] = m
        z[64:, 64:] = m
        return z
    c["m_sl"] = bd((s < t).astype(np.float32))
    c["m_sl_T"] = bd((s > t).astype(np.float32))
    c["m_il"] = bd((s <= t).astype(np.float32))
    c["m_il_T"] = bd((s >= t).astype(np.float32))
    tt = np.arange(512)
    c["rmask_f"] = np.tile(((tt % 64) != 0).astype(np.float32)[None, :], (128, 1))
    c["rmask_b"] = np.tile(((tt % 64) != 63).astype(np.float32)[None, :], (128, 1))
    pos = np.arange(NLAT)
    row = (pos // GRID_W).astype(np.float32)
    col = (pos % GRID_W).astype(np.float32)

    def tables(dim, reps):
        quarter = dim // 4
        freqs = (np.float32(10000.0) ** (-np.arange(quarter, dtype=np.float32) / np.float32(quarter))).astype(np.float32)
        ang = np.concatenate([row[:, None] * freqs, col[:, None] * freqs], axis=-1).astype(np.float32)
        cs, sn = np.cos(ang).astype(np.float32), np.sin(ang).astype(np.float32)
        cs = np.concatenate([cs, cs], axis=1).T
        sn = np.concatenate([sn, sn], axis=1).T
        cs = np.concatenate([np.ones((dim, NCTX), np.float32), cs], axis=1)
        sn = np.concatenate([np.zeros((dim, NCTX), np.float32), sn], axis=1)
        return np.tile(cs, (reps, 1)), np.tile(sn, (reps, 1))
    c["cosA"], c["sinA"] = tables(32, 4)
    c["cosB"], c["sinB"] = tables(64, 2)
    return {k: np.ascontiguousarray(v, dtype=np.float32) for k, v in c.items()}


def _fm(v, nch):
    return np.asarray(v, np.float32).reshape(nch, 128).T


def _host_vecs(inp):
    vf = np.zeros((128, DEPTH * NF), np.float32)
    vr = np.zeros((1, DEPTH * NR), np.float32)
    for l in range(DEPTH):
        b = l * NF
        vf[:, b + F_PRE:b + F_PRE + 8] = _fm(inp["attn_pre_g"][l], 8)
        vf[:, b + F_FPRE:b + F_FPRE + 8] = _fm(inp["ffn_pre_g"][l], 8)
        vf[:, b + F_QN:b + F_QN + 3] = _fm(inp["mla_q_norm"][l], 3)
        vf[:, b + F_KVN:b + F_KVN + 2] = _fm(inp["mla_kv_norm"][l], 2)
        vf[:, b + F_GQ] = np.tile(inp["gqa_q_norm"][l], 2)
        vf[:, b + F_GK] = np.tile(inp["gqa_k_norm"][l], 2)
        for j in range(3):
            vf[:, b + F_CONV + j * 12:b + F_CONV + j * 12 + 12] = _fm(inp["rwkv_conv"][l, j], 12)
        for d in range(2):
            vf[:, b + F_W0 + d * 4:b + F_W0 + d * 4 + 4] = _fm(inp["rwkv_w0"][l, d], 4)
            vf[:, b + F_A0 + d * 4:b + F_A0 + d * 4 + 4] = _fm(inp["rwkv_a0"][l, d], 4)
        vf[:, b + F_KK:b + F_KK + 4] = _fm(inp["rwkv_k_k"][l], 4)
        vf[:, b + F_KA:b + F_KA + 4] = _fm(inp["rwkv_k_a"][l], 4)
        vf[:, b + F_RK:b + F_RK + 4] = _fm(inp["rwkv_r_k"][l].reshape(-1), 4)
        vf[:, b + F_LNG:b + F_LNG + 4] = _fm(inp["rwkv_ln_g"][l], 4)
        vf[:, b + F_LNB:b + F_LNB + 4] = _fm(inp["rwkv_ln_b"][l], 4)
        rb = l * NR
        vr[0, rb + R_ADAB:rb + R_ADAB + 6144] = inp["ada_b"][l]
        vr[0, rb + R_POST:rb + R_POST + 1024] = inp["attn_post_g"][l]
        vr[0, rb + R_FPOST:rb + R_FPOST + 1024] = inp["ffn_post_g"][l]
    return vf, vr


WEIGHTS = {
    "ada_w": [DEPTH, D, 6 * D], "w_in": [DEPTH, D, IN_W], "mla_w_uq": [DEPTH, 384, 768],
    "mla_w_ukv": [DEPTH, 256, 1024], "rwkv_w2": [DEPTH, 2, 64, 512], "rwkv_a2": [DEPTH, 2, 64, 512],
    "rwkv_g2": [DEPTH, 128, 512], "w_branch": [DEPTH, 3, 512, D], "w_out": [DEPTH, D, D],
    "ffn_w_in": [DEPTH, D, 2 * FFH], "ffn_w_out": [DEPTH, FFH, D],
}
CONST_SHAPES = {
    "ident": [128, 128], "ones": [128, 128], "bones": [128, 128], "perm": [128, 128], "stackI": [128, 64],
    "m_sl": [128, 128], "m_sl_T": [128, 128], "m_il": [128, 128], "m_il_T": [128, 128],
    "rmask_f": [128, 512], "rmask_b": [128, 512],
    "cosA": [128, T], "sinA": [128, T], "cosB": [128, T], "sinB": [128, T],
}


def build_program(upto="all", debug=False, nlayers=DEPTH, skip=()):
    nc = bass.Bass("TRN2", target_bir_lowering=False)
    kindS = "ExternalOutput" if debug else "Internal"

    def din(name, shape, dt=F32):
        return nc.dram_tensor(name, list(shape), dt, kind="ExternalInput").ap()

    def dsc(name, shape, dt=F32):
        return nc.dram_tensor(name, list(shape), dt, kind=kindS).ap()

    xin = din("xin", [T, D])
    csil = din("csil", [128, 16])
    vecF_d = din("vecF", [128, DEPTH * NF])
    vecR_d = din("vecR", [1, DEPTH * NR])
    Wd = {k: din(k, s) for k, s in WEIGHTS.items()}
    Cd = {k: din(k, s) for k, s in CONST_SHAPES.items()}
    out_d = nc.dram_tensor("out", [NLAT, D], F32, kind="ExternalOutput").ap()

    xres = dsc("xres", [T, D])
    cqT = dsc("cqT", [384, T], BF16)
    ckvT = dsc("ckvT", [256, T], BF16)
    krT = dsc("krT", [64, T], BF16)
    gqT = dsc("gqT", [512, T], BF16)
    gkT = dsc("gkT", [128, T], BF16)
    gvS = dsc("gvS", [T, 128], BF16)
    rkvT = dsc("rkvT", [1536, T], F32)
    loT = dsc("loT", [384, T], F32)
    gateT = dsc("gateT", [3072, T], BF16)
    QaT = dsc("QaT", [8 * 96, T], BF16)
    KaT = dsc("KaT", [8 * 96, T], BF16)
    VaS = dsc("VaS", [T, 512], BF16)
    QbT = dsc("QbT", [512, T], BF16)
    KbT = dsc("KbT", [128, T], BF16)
    oaT = dsc("oaT", [512, T], BF16)
    obT = dsc("obT", [512, T], BF16)
    ocT = dsc("ocT", [512, T], BF16)
    rwA = dsc("rwA", [2, 512, T], RW_DT); rwB = dsc("rwB", [2, 512, T], RW_DT); rwK = dsc("rwK", [2, 512, T], RW_DT)
    rwR = dsc("rwR", [2, 512, T], RW_DT); rwE = dsc("rwE", [2, 512, T])
    rwV = dsc("rwV", [512, T], RW_DT); rwBon = dsc("rwBon", [512, T])
    rwY = dsc("rwY", [2, T, 512])

    with ExitStack() as top:
        import os as _os
        P = Prog(nc, top, same_engine_sync=bool(int(_os.environ.get('SES', '0'))))

        uid = {"n": 0}

        def sbt(st, name, shape, dt=F32):
            uid["n"] += 1
            return st.enter_context(nc.sbuf_tensor(f"{name}_s{uid['n']}", list(shape), dt))

        def pst(st, name, shape, dt=F32):
            uid["n"] += 1
            return st.enter_context(nc.psum_tensor(f"{name}_p{uid['n']}", list(shape), dt))

        ident_bf = sbt(top, "ident_bf", [128, 128], BF16)
        ones_bf = sbt(top, "ones_bf", [128, 128], BF16)
        bones_bf = sbt(top, "bones_bf", [128, 128], BF16)
        perm_bf = sbt(top, "perm_bf", [128, 128], BF16)
        ones_f = sbt(top, "ones_f", [128, 128])
        bones_f = sbt(top, "bones_f", [128, 128])
        ident_f = sbt(top, "ident_f", [128, 128])
        stackI = sbt(top, "stackI", [128, 64])
        m_sl = sbt(top, "m_sl", [128, 128]); m_slT = sbt(top, "m_slT", [128, 128]); m_il = sbt(top, "m_il", [128, 128])
        vecF = sbt(top, "vecF_sb", [128, DEPTH * NF])
        colF = sbt(top, "colF", [128, 64])
        GB = [[sbt(top, f"GB{j}{i}", [128, D]) for i in range(2)] for j in range(2)]
        for name, tl in (("ident", ident_bf), ("ones", ones_bf), ("bones", bones_bf), ("perm", perm_bf)):
            P.dma("pool", tl[:], Cd[name], w=[tl.name])
        for name, tl in (("ones", ones_f), ("bones", bones_f), ("ident", ident_f), ("stackI", stackI), ("m_sl", m_sl),
                         ("m_sl_T", m_slT), ("m_il", m_il)):
            P.dma("sp", tl[:], Cd[name], w=[tl.name])
        P.dma("sp", vecF[:], vecF_d, w=["vecF"])
        P.barrier()

        rr = {"ev": 0}

        def evac(out, in_, r, w, func=None, scale=None, bias=None, eng=None):
            if func is None and scale is None and bias is None:
                if eng is None:
                    eng = "act" if rr["ev"] % 2 else "dve"
                    rr["ev"] += 1
                if eng == "act":
                    P.op("act", lambda e: e.copy(out, in_), r=r, w=w)
                else:
                    P.op(eng, lambda e: e.tensor_copy(out, in_), r=r, w=w)
            else:
                kw = {}
                if scale is not None:
                    kw["scale"] = scale
                if bias is not None:
                    kw["bias"] = bias
                f = func if func is not None else AF.Identity
                P.op("act", lambda e: e.activation(out, in_, f, **kw), r=r, w=w)

        def vF(l, off, n=1):
            return vecF[:, l * NF + off:l * NF + off + n]

        def phase_mod(l):
            with ExitStack() as ph:
                adaw = sbt(ph, "adaw", [128, 8, 3072], BF16)
                vecR = sbt(ph, "vecR", [1, NR])
                P.dma("sp", vecR[:], vecR_d[0:1, l * NR:(l + 1) * NR], w=["vecR"])
                cs = sbt(ph, "cs", [128, 16]); csb = sbt(ph, "csb", [128, 16], BF16)
                modrow = [sbt(ph, f"modrow{j}", [1, 6144]) for j in range(2)]
                rowtmp = sbt(ph, "rowtmp", [1, D])
                colraw = sbt(ph, "colraw", [128, 64])
                pss = [pst(ph, f"mps{i}", [128, 512]) for i in range(4)]
                P.dma("sp", cs[:], csil, w=["cs"])
                P.op("act", lambda e: e.activation(csb[:], cs[:], AF.Silu), r=["cs"], w=["csb"])
                n = 0
                for hf in range(2):
                    for k in range(8):
                        P.dma("pool", adaw[:, k, :], Wd["ada_w"][l, k * 128:(k + 1) * 128, hf * 3072:(hf + 1) * 3072], w=[f"adaw{k}"])
                    for j in range(2):
                        for gg in range(6):
                            g = hf * 6 + gg
                            ps = pss[n % 4]; pk = f"mps{n % 4}"; n += 1
                            for k in range(8):
                                P.op("pe", lambda e, ps=ps, k=k, j=j, gg=gg: e.matmul(
                                    ps[0:1, :], csb[:, k * 2 + j:k * 2 + j + 1], adaw[:, k, gg * 512:(gg + 1) * 512],
                                    start=(k == 0), stop=(k == 7)), r=["csb", f"adaw{k}"], w=[pk])
                            P.op("dve", lambda e, ps=ps, j=j, g=g: e.tensor_tensor(
                                modrow[j][0:1, g * 512:(g + 1) * 512], ps[0:1, :],
                                vecR[0:1, R_ADAB + g * 512:R_ADAB + (g + 1) * 512], ALU.add),
                                r=[pk, "vecR"], w=[f"modrow{j}"])
                psc = pss[0]
                segs = (0, 1, 3, 4)
                for j in range(2):
                    for si, seg in enumerate(segs):
                        for k in range(8):
                            cidx = (j * 4 + si) * 8 + k
                            P.op("pe", lambda e, j=j, seg=seg, k=k, cidx=cidx: e.matmul(
                                psc[:, cidx:cidx + 1], modrow[j][0:1, seg * 1024 + k * 128:seg * 1024 + (k + 1) * 128],
                                ones_f[0:1, 0:1], start=True, stop=True), r=[f"modrow{j}"], w=["mps0"])
                P.op("dve", lambda e: e.tensor_copy(colraw[:], psc[:, 0:64]), r=["mps0"], w=["colraw"])
                for j in range(2):
                    b = j * 32
                    P.op("dve", lambda e, b=b: e.scalar_tensor_tensor(
                        colF[:, b:b + 8], colraw[:, b + 8:b + 16], 1.0, vF(l, F_PRE, 8), ALU.add, ALU.mult),
                        r=["colraw", "vecF"], w=["colF"])
                    P.op("dve", lambda e, b=b: e.tensor_copy(colF[:, b + 8:b + 16], colraw[:, b:b + 8]), r=["colraw"], w=["colF"])
                    P.op("dve", lambda e, b=b: e.scalar_tensor_tensor(
                        colF[:, b + 16:b + 24], colraw[:, b + 24:b + 32], 1.0, vF(l, F_FPRE, 8), ALU.add, ALU.mult),
                        r=["colraw", "vecF"], w=["colF"])
                    P.op("dve", lambda e, b=b: e.tensor_copy(colF[:, b + 24:b + 32], colraw[:, b + 16:b + 24]), r=["colraw"], w=["colF"])
                n = 1
                for j in range(2):
                    for i, (seg, roff) in enumerate(((2, R_POST), (5, R_FPOST))):
                        P.op("dve", lambda e, j=j, seg=seg, roff=roff: e.tensor_tensor(
                            rowtmp[0:1, :], modrow[j][0:1, seg * 1024:(seg + 1) * 1024],
                            vecR[0:1, roff:roff + 1024], ALU.mult),
                            r=[f"modrow{j}", "vecR"], w=["rowtmp"])
                        for hh in range(2):
                            ps = pss[n % 4]; pk = f"mps{n % 4}"; n += 1
                            P.op("pe", lambda e, ps=ps, hh=hh: e.matmul(
                                ps[:, :], ones_f[0:1, :], rowtmp[0:1, hh * 512:(hh + 1) * 512], start=True, stop=True),
                                r=["rowtmp"], w=[pk])
                            evac(GB[j][i][:, hh * 512:(hh + 1) * 512], ps[:, :], r=[pk], w=[f"GB{j}{i}"])
                P.barrier()

        def norm_modulate_T(ph, l, src, t0, n, kind, hT, tag):
            j = 1 if t0 < NCTX else 0
            cb = (j * 4 + kind * 2) * 8
            nst = n // 128
            for s in range(nst):
                nb_ = len(ph["xt"])
                xt = ph["xt"][s % nb_]; xk = f"xt{s % nb_}"
                P.dma("sp", xt[:], src[t0 + s * 128:t0 + (s + 1) * 128, :], r=[(tag, "x", t0 + s * 128)], w=[xk])
                junk = ph["junk"]; ss = ph["ss"][s % 2]; sk = f"ss{s % 2}"
                P.op("act", lambda e, xt=xt, ss=ss: e.activation(junk[:], xt[:], AF.Square, accum_out=ss[:, 0:1]),
                     r=[xk], w=["junk", sk])
                P.op("act", lambda e, ss=ss: e.activation(ss[:, 1:2], ss[:, 0:1], AF.Ln, scale=1.0 / D, bias=ph["epsc"][:, 0:1]),
                     r=[sk, "epsc"], w=[sk + "b"], ss=True)
                P.op("act", lambda e, ss=ss: e.activation(ss[:, 2:3], ss[:, 1:2], AF.Exp, scale=-0.5), r=[sk + "b"], w=[sk + "c"], ss=True)
                xn = ph["xn"][s % nb_]; nk = f"xn{s % nb_}"
                P.op("dve", lambda e, xt=xt, xn=xn, ss=ss: e.tensor_scalar(xn[:], xt[:], ss[:, 2:3], None, ALU.mult),
                     r=[xk, sk + "c"], w=[nk])
                for half in range(2):
                    tp = ph["tp"][half]; tk = f"tp{half}"
                    for kk in range(4):
                        k = half * 4 + kk
                        P.op("pe", lambda e, tp=tp, kk=kk, k=k, xn=xn: e.transpose(
                            tp[:, kk * 128:(kk + 1) * 128], xn[:, k * 128:(k + 1) * 128], ident_bf[:]), r=[nk], w=[tk])
                    for kk in range(4):
                        k = half * 4 + kk
                        o = hT[:, k, s * 128:(s + 1) * 128]
                        i_ = tp[:, kk * 128:(kk + 1) * 128]
                        if kk % 2 == 0:
                            P.op("act", lambda e, o=o, i_=i_, k=k: e.activation(
                                o, i_, AF.Identity, scale=colF[:, cb + k:cb + k + 1], bias=colF[:, cb + 8 + k:cb + 9 + k]),
                                r=[tk, "colF"], w=[(tag, "hT")])
                        else:
                            P.op("dve", lambda e, o=o, i_=i_, k=k: e.tensor_scalar(
                                o, i_, colF[:, cb + k:cb + k + 1], colF[:, cb + 8 + k:cb + 9 + k], ALU.mult, ALU.add),
                                r=[tk, "colF"], w=[(tag, "hT")])

        def norm_tiles(ph, nbuf=2):
            d = {}
            d["xt"] = [sbt(ph, f"xt{i}", [128, D]) for i in range(nbuf)]
            d["xn"] = [sbt(ph, f"xn{i}", [128, D], BF16) for i in range(nbuf)]
            d["junk"] = sbt(ph, "junk", [128, D], BF16)
            d["ss"] = [sbt(ph, f"ss{i}", [128, 4]) for i in range(2)]
            d["tp"] = [pst(ph, f"tp{i}", [128, 1024], BF16) for i in range(2)]
            d["epsc"] = sbt(ph, "epsc", [128, 1])
            P.op("dve", lambda e: e.memset(d["epsc"][:], EPS), w=["epsc"])
            return d

        def phase_inproj(l):
            src = xin if l == 0 else xres
            with ExitStack() as ph:
                win = sbt(ph, "win", [128, 8, IN_W + 32], BF16)
                nt = norm_tiles(ph)
                hT = [sbt(ph, f"hT{i}", [128, 8, 512], BF16) for i in range(2)]
                stg_b = [sbt(ph, f"stgb{i}", [128, 512], BF16) for i in range(3)]
                stg_f = [sbt(ph, f"stgf{i}", [128, 512]) for i in range(3)]
                pss = [pst(ph, f"ips{i}", [128, 512]) for i in range(4)]
                for k in range(8):
                    P.dma("pool", win[:, k, 0:IN_W], Wd["w_in"][l, k * 128:(k + 1) * 128, :], w=[f"win{k}"])
                for k in range(8):
                    P.op("dve", lambda e, k=k: e.tensor_scalar(
                        win[:, k, IN_W:IN_W + 16], win[:, k, O_KR + 16:O_KR + 32], -1.0, None, ALU.mult), r=[f"win{k}"], w=[f"winR{k}"])
                    P.op("dve", lambda e, k=k: e.tensor_copy(
                        win[:, k, IN_W + 16:IN_W + 32], win[:, k, O_KR:O_KR + 16]), r=[f"win{k}"], w=[f"winR{k}"])
                wkeys = [f"win{k}" for k in range(8)] + [f"winR{k}" for k in range(8)]
                chunks = []
                for c in range(3):
                    chunks.append((O_CQ + c * 128, 128, cqT, c * 128, BF16, None))
                for c in range(2):
                    chunks.append((O_CKV + c * 128, 128, ckvT, c * 128, BF16, None))
                chunks.append((O_KR, 32, krT, 0, BF16, None))
                chunks.append((IN_W, 32, krT, 32, BF16, None))
                for c in range(4):
                    chunks.append((O_GQ + c * 128, 128, gqT, c * 128, BF16, None))
                chunks.append((O_GK, 128, gkT, 0, BF16, None))
                for c in range(12):
                    chunks.append((O_RKV + c * 128, 128, rkvT, c * 128, F32, None))
                chunks.append((O_ALO, 128, loT, 128, F32, None))
                chunks.append((O_WLO, 128, loT, 0, F32, AF.Tanh))
                chunks.append((O_GLO, 128, loT, 256, F32, AF.Sigmoid))
                for c in range(24):
                    chunks.append((O_GATE + c * 128, 128, gateT, c * 128, BF16, AF.Sigmoid))
                n = 0
                nsb = 0
                nsf = 0
                for bi, (t0, nn) in enumerate(SPANS):
                    h = hT[bi % 2]; hk = ("in", "hT", bi % 2)
                    norm_modulate_T(nt, l, src, t0, nn, 0, h, ("in", bi % 2))
                    hkey = (("in", bi % 2), "hT")
                    for (c0, m, dst, r0, dt, func) in chunks:
                        ps = pss[n % 4]; pk = f"ips{n % 4}"; n += 1
                        for k in range(8):
                            P.op("pe", lambda e, ps=ps, k=k, c0=c0, m=m, h=h, nn=nn: e.matmul(
                                ps[0:m, 0:nn], win[:, k, c0:c0 + m], h[:, k, 0:nn], start=(k == 0), stop=(k == 7)),
                                r=[hkey] + wkeys, w=[pk])
                        if dt == BF16:
                            sg = stg_b[nsb % 3]; sk = f"stgb{nsb % 3}"; nsb += 1
                        else:
                            sg = stg_f[nsf % 3]; sk = f"stgf{nsf % 3}"; nsf += 1
                        evac(sg[0:m, 0:nn], ps[0:m, 0:nn], r=[pk], w=[sk], func=func)
                        P.dma("sp", dst[r0:r0 + m, t0:t0 + nn], sg[0:m, 0:nn], r=[sk], w=[(dst.tensor.name, t0)])
                    for s in range(nn // 128):
                        ps = pss[n % 4]; pk = f"ips{n % 4}"; n += 1
                        for k in range(8):
                            P.op("pe", lambda e, ps=ps, k=k, h=h, s=s: e.matmul(
                                ps[:, 0:128], h[:, k, s * 128:(s + 1) * 128], win[:, k, O_GV:O_GV + 128],
                                start=(k == 0), stop=(k == 7)), r=[hkey] + wkeys, w=[pk])
                        sg = stg_b[nsb % 3]; sk = f"stgb{nsb % 3}"; nsb += 1
                        evac(sg[:, 0:128], ps[:, 0:128], r=[pk], w=[sk])
                        P.dma("sp", gvS[t0 + s * 128:t0 + (s + 1) * 128, :], sg[:, 0:128], r=[sk], w=[("gvS", t0)])
                P.barrier()

        def rms_feat(ph, x, nk, nn, ones_t, inv_n, gcol0, l, xn_out, tag):
            sq = ph["sq"]; pss = ph["pss"]; lnv = ph["lnv"]; rstd = ph["rstd"]
            P.op("dve", lambda e: e.tensor_tensor(sq[:, 0:nk, 0:nn], x[:, 0:nk, 0:nn], x[:, 0:nk, 0:nn], ALU.mult),
                 r=[tag + "x"], w=["sq"])
            for k in range(nk):
                P.op("pe", lambda e, k=k: e.matmul(pss[:, 0:nn], ones_t[:], sq[:, k, 0:nn], start=(k == 0), stop=(k == nk - 1)),
                     r=["sq"], w=["pss"])
            P.op("act", lambda e: e.activation(lnv[:, 0:nn], pss[:, 0:nn], AF.Ln, scale=inv_n, bias=ph["epsc"][:, 0:1]),
                 r=["pss", "epsc"], w=["lnv"], ss=True)
            P.op("act", lambda e: e.activation(rstd[:, 0:nn], lnv[:, 0:nn], AF.Exp, scale=-0.5), r=["lnv"], w=["rstd"], ss=True)
            for k in range(nk):
                P.op("dve", lambda e, k=k: e.scalar_tensor_tensor(
                    xn_out[:, k, 0:nn], x[:, k, 0:nn], vF(l, gcol0 + k), rstd[:, 0:nn], ALU.mult, ALU.mult),
                    r=[tag + "x", "rstd", "vecF"], w=[tag + "xn"])

        def phase_mla(l):
            with ExitStack() as ph:
                wq_n = sbt(ph, "wq_n", [128, 3, 512], BF16); wq_r = sbt(ph, "wq_r", [128, 3, 256], BF16)
                wq_rR = sbt(ph, "wq_rR", [128, 3, 256], BF16)
                wk_n = sbt(ph, "wk_n", [128, 2, 512], BF16); wk_v = sbt(ph, "wk_v", [128, 2, 512], BF16)
                t = {}
                t["sq"] = sbt(ph, "sq", [128, 3, 512], BF16); t["lnv"] = sbt(ph, "lnv", [128, 512]); t["rstd"] = sbt(ph, "rstd", [128, 512])
                t["pss"] = pst(ph, "pss", [128, 512]); t["epsc"] = sbt(ph, "epsc", [128, 1])
                P.op("dve", lambda e: e.memset(t["epsc"][:], EPS), w=["epsc"])
                cq = sbt(ph, "cq", [128, 3, 512], BF16); cqn = sbt(ph, "cqn", [128, 3, 512], BF16)
                ckv = sbt(ph, "ckv", [128, 2, 512], BF16); ckvn = sbt(ph, "ckvn", [128, 2, 512], BF16)
                krA = sbt(ph, "krA", [32, 512], BF16); krB = sbt(ph, "krB", [32, 512], BF16)
                cA = sbt(ph, "cA", [128, 512]); sA = sbt(ph, "sA", [128, 512])
                t1 = sbt(ph, "t1", [128, 512]); t2 = sbt(ph, "t2", [128, 512])
                stg = [sbt(ph, f"stg{i}", [128, 512], BF16) for i in range(4)]
                pq = [pst(ph, f"pq{i}", [128, 512]) for i in range(4)]
                for kc in range(3):
                    src = Wd["mla_w_uq"][l, kc * 128:(kc + 1) * 128, :].rearrange("p (h d) -> p h d", d=96)
                    P.dma("pool", wq_n[:, kc, :].rearrange("p (h d) -> p h d", d=64), src[:, :, 0:64], w=["wq_n"])
                    P.dma("pool", wq_r[:, kc, :].rearrange("p (h d) -> p h d", d=32), src[:, :, 64:96], w=["wq_r"])
                    rv = wq_r[:, kc, :].rearrange("p (h d) -> p h d", d=32)
                    rRv = wq_rR[:, kc, :].rearrange("p (h d) -> p h d", d=32)
                    P.op("dve", lambda e, rv=rv, rRv=rRv: e.tensor_scalar(rRv[:, :, 0:16], rv[:, :, 16:32], -1.0, None, ALU.mult),
                         r=["wq_r"], w=["wq_rR"])
                    P.op("dve", lambda e, rv=rv, rRv=rRv: e.tensor_copy(rRv[:, :, 16:32], rv[:, :, 0:16]), r=["wq_r"], w=["wq_rR"])
                for kc in range(2):
                    src = Wd["mla_w_ukv"][l, kc * 128:(kc + 1) * 128, :].rearrange("p (h d) -> p h d", d=128)
                    P.dma("pool", wk_n[:, kc, :].rearrange("p (h d) -> p h d", d=64), src[:, :, 0:64], w=["wk_n"])
                    P.dma("pool", wk_v[:, kc, :].rearrange("p (h d) -> p h d", d=64), src[:, :, 64:128], w=["wk_v"])
                ns = 0; npq = 0
                cqT_v = cqT.rearrange("(k p) t -> p k t", p=128)
                ckvT_v = ckvT.rearrange("(k p) t -> p k t", p=128)
                for bi, (t0, nn) in enumerate(SPANS):
                    P.dma("sp", cq[:, :, 0:nn], cqT_v[:, :, t0:t0 + nn], w=["qx"])
                    P.dma("sp", ckv[:, :, 0:nn], ckvT_v[:, :, t0:t0 + nn], w=["kx"])
                    P.dma("sp", krA[:, 0:nn], krT[0:32, t0:t0 + nn], w=["krA"])
                    P.dma("sp", krB[:, 0:nn], krT[32:64, t0:t0 + nn], w=["krB"])
                    P.dma("sp", cA[:, 0:nn], Cd["cosA"][:, t0:t0 + nn], w=["cA"])
                    P.dma("sp", sA[:, 0:nn], Cd["sinA"][:, t0:t0 + nn], w=["sA"])
                    rms_feat(t, cq, 3, nn, ones_bf, 1.0 / 384, F_QN, l, cqn, "q")
                    need_q = not (l == DEPTH - 1 and t0 < NCTX)
                    if need_q:
                        for g in range(4):
                            ps = pq[npq % 4]; pk = f"pq{npq % 4}"; npq += 1
                            for k in range(3):
                                P.op("pe", lambda e, ps=ps, k=k, g=g, nn=nn: e.matmul(ps[:, 0:nn], wq_n[:, k, g * 128:(g + 1) * 128], cqn[:, k, 0:nn],
                                                                                start=(k == 0), stop=(k == 2)), r=["qxn", "wq_n"], w=[pk])
                            sg = stg[ns % 4]; sk = f"stg{ns % 4}"; ns += 1
                            evac(sg[:, 0:nn], ps[:, 0:nn], r=[pk], w=[sk])
                            for hh in range(2):
                                h = 2 * g + hh
                                P.dma("sp", QaT[h * 96:h * 96 + 64, t0:t0 + nn], sg[hh * 64:(hh + 1) * 64, 0:nn], r=[sk], w=[("QaT", t0, h)])
                        for g2 in range(2):
                            p1 = pq[npq % 4]; k1 = f"pq{npq % 4}"; npq += 1
                            p2 = pq[npq % 4]; k2 = f"pq{npq % 4}"; npq += 1
                            for k in range(3):
                                P.op("pe", lambda e, k=k, p1=p1, g2=g2, nn=nn: e.matmul(p1[:, 0:nn], wq_r[:, k, g2 * 128:(g2 + 1) * 128], cqn[:, k, 0:nn],
                                                                           start=(k == 0), stop=(k == 2)), r=["qxn", "wq_r"], w=[k1])
                            for k in range(3):
                                P.op("pe", lambda e, k=k, p2=p2, g2=g2, nn=nn: e.matmul(p2[:, 0:nn], wq_rR[:, k, g2 * 128:(g2 + 1) * 128], cqn[:, k, 0:nn],
                                                                           start=(k == 0), stop=(k == 2)), r=["qxn", "wq_rR"], w=[k2])
                            P.op("dve", lambda e, p1=p1, nn=nn: e.tensor_tensor(t1[:, 0:nn], p1[:, 0:nn], cA[:, 0:nn], ALU.mult), r=[k1, "cA"], w=["t1"])
                            P.op("dve", lambda e, p2=p2, nn=nn: e.tensor_tensor(t2[:, 0:nn], p2[:, 0:nn], sA[:, 0:nn], ALU.mult), r=[k2, "sA"], w=["t2"])
                            sg = stg[ns % 4]; sk = f"stg{ns % 4}"; ns += 1
                            P.op("pool", lambda e, sg=sg, nn=nn: e.tensor_tensor(sg[:, 0:nn], t1[:, 0:nn], t2[:, 0:nn], ALU.add), r=["t1", "t2"], w=[sk])
                            for hh in range(4):
                                h = 4 * g2 + hh
                                P.dma("sp", QaT[h * 96 + 64:h * 96 + 96, t0:t0 + nn], sg[hh * 32:(hh + 1) * 32, 0:nn], r=[sk], w=[("QaT", t0, h, "r")])
                    rms_feat(t, ckv, 2, nn, ones_bf, 1.0 / 256, F_KVN, l, ckvn, "k")
                    for g in range(4):
                        ps = pq[npq % 4]; pk = f"pq{npq % 4}"; npq += 1
                        for k in range(2):
                            P.op("pe", lambda e, ps=ps, k=k, g=g, nn=nn: e.matmul(ps[:, 0:nn], wk_n[:, k, g * 128:(g + 1) * 128], ckvn[:, k, 0:nn],
                                                                            start=(k == 0), stop=(k == 1)), r=["kxn", "wk_n"], w=[pk])
                        sg = stg[ns % 4]; sk = f"stg{ns % 4}"; ns += 1
                        evac(sg[:, 0:nn], ps[:, 0:nn], r=[pk], w=[sk])
                        for hh in range(2):
                            h = 2 * g + hh
                            P.dma("sp", KaT[h * 96:h * 96 + 64, t0:t0 + nn], sg[hh * 64:(hh + 1) * 64, 0:nn], r=[sk], w=[("KaT", t0, h)])
                    for s in range(nn // 128):
                        ps = pq[npq % 4]; pk = f"pq{npq % 4}"; npq += 1
                        for k in range(2):
                            P.op("pe", lambda e, ps=ps, k=k, s=s: e.matmul(ps[:, :], ckvn[:, k, s * 128:(s + 1) * 128], wk_v[:, k, :],
                                                                            start=(k == 0), stop=(k == 1)), r=["kxn", "wk_v"], w=[pk])
                        sg = stg[ns % 4]; sk = f"stg{ns % 4}"; ns += 1
                        evac(sg[:, :], ps[:, :], r=[pk], w=[sk])
                        P.dma("sp", VaS[t0 + s * 128:t0 + (s + 1) * 128, :], sg[:, :], r=[sk], w=[("VaS", t0, s)])
                    P.op("dve", lambda e, nn=nn: e.tensor_tensor(t1[0:32, 0:nn], krA[:, 0:nn], cA[0:32, 0:nn], ALU.mult), r=["krA", "cA"], w=["t1"])
                    P.op("dve", lambda e, nn=nn: e.tensor_tensor(t2[0:32, 0:nn], krB[:, 0:nn], sA[0:32, 0:nn], ALU.mult), r=["krB", "sA"], w=["t2"])
                    sg = stg[ns % 4]; sk = f"stg{ns % 4}"; ns += 1
                    P.op("pool", lambda e, sg=sg, nn=nn: e.tensor_tensor(sg[0:32, 0:nn], t1[0:32, 0:nn], t2[0:32, 0:nn], ALU.add), r=["t1", "t2"], w=[sk])
                    for h in range(8):
                        P.dma("sp", KaT[h * 96 + 64:h * 96 + 96, t0:t0 + nn], sg[0:32, 0:nn], r=[sk], w=[("KaT", t0, h, "r")])
                P.barrier()

        def phase_gqa(l):
            with ExitStack() as ph:
                t = {}
                t["sq"] = sbt(ph, "sq", [128, 1, 512], BF16); t["lnv"] = sbt(ph, "lnv", [128, 512]); t["rstd"] = sbt(ph, "rstd", [128, 512])
                t["pss"] = pst(ph, "pss", [128, 512]); t["epsc"] = sbt(ph, "epsc", [128, 1])
                P.op("dve", lambda e: e.memset(t["epsc"][:], EPS), w=["epsc"])
                x = [sbt(ph, f"gx{i}", [128, 1, 512], BF16) for i in range(2)]
                xn = [sbt(ph, f"gxn{i}", [128, 1, 512], BF16) for i in range(2)]
                cB = sbt(ph, "cB", [128, 512]); sB = sbt(ph, "sB", [128, 512])
                t1 = sbt(ph, "t1", [128, 512]); t2 = sbt(ph, "t2", [128, 512])
                stg = [sbt(ph, f"stg{i}", [128, 512], BF16) for i in range(2)]
                pr = [pst(ph, f"pr{i}", [128, 512]) for i in range(2)]
                n = 0
                for bi, (t0, nn) in enumerate(SPANS):
                    P.dma("sp", cB[:, 0:nn], Cd["cosB"][:, t0:t0 + nn], w=["cB"])
                    P.dma("sp", sB[:, 0:nn], Cd["sinB"][:, t0:t0 + nn], w=["sB"])
                    for c in range(5):
                        if c < 4 and l == DEPTH - 1 and t0 < NCTX:
                            continue
                        src = gqT[c * 128:(c + 1) * 128, t0:t0 + nn] if c < 4 else gkT[:, t0:t0 + nn]
                        dst = QbT[c * 128:(c + 1) * 128, t0:t0 + nn] if c < 4 else KbT[:, t0:t0 + nn]
                        i = n % 2; n += 1
                        P.dma("sp", x[i][:, 0, 0:nn], src, w=[f"g{i}x"])
                        rms_feat(t, x[i], 1, nn, bones_bf, 1.0 / 64, F_GQ if c < 4 else F_GK, l, xn[i], f"g{i}")
                        P.op("pe", lambda e, i=i, nn=nn: e.matmul(pr[i][:, 0:nn], perm_bf[:], xn[i][:, 0, 0:nn], start=True, stop=True),
                             r=[f"g{i}xn"], w=[f"pr{i}"])
                        P.op("pool", lambda e, i=i, nn=nn: e.tensor_tensor(t1[:, 0:nn], xn[i][:, 0, 0:nn], cB[:, 0:nn], ALU.mult), r=[f"g{i}xn", "cB"], w=["t1"])
                        P.op("dve", lambda e, i=i, nn=nn: e.tensor_tensor(t2[:, 0:nn], pr[i][:, 0:nn], sB[:, 0:nn], ALU.mult), r=[f"pr{i}", "sB"], w=["t2"])
                        P.op("pool", lambda e, i=i, nn=nn: e.tensor_tensor(stg[i][:, 0:nn], t1[:, 0:nn], t2[:, 0:nn], ALU.add), r=["t1", "t2"], w=[f"stg{i}"])
                        P.dma("sp", dst, stg[i][:, 0:nn], r=[f"stg{i}"], w=[("Qb", t0, c)])
                P.barrier()

        def phase_attn(l, kind):
            if kind == "a":
                QT, KT, VS, d, nkv, G, scale, OT = QaT, KaT, VaS, 96, 8, 1, MLA_SCALE, oaT
            else:
                QT, KT, VS, d, nkv, G, scale, OT = QbT, KbT, gvS, 64, 2, 4, GQA_SCALE, obT
            NKT = T // 128
            with ExitStack() as ph:
                Kt = [sbt(ph, f"Kt{i}", [128, T], BF16) for i in range(2)]
                Va = [sbt(ph, f"Va{i}", [128, NKT, 128], BF16) for i in range(2)]
                Qt = [sbt(ph, f"Qt{i}", [128, 512], BF16) for i in range(2)]
                pt = [sbt(ph, f"pt{i}", [128, 512], BF16) for i in range(4)]
                rc = sbt(ph, "rc", [128, 512]); ot = [sbt(ph, f"ot{i}", [64, 512], BF16) for i in range(2)]
                ps_s = [pst(ph, f"pss{i}", [128, 512]) for i in range(4)]
                ps_o = [pst(ph, f"pso{i}", [128, 512]) for i in range(2)]
                for i in range(2):
                    P.op("pool", lambda e, i=i: e.memset(Va[i][:, :, 64:128], 1.0), w=[f"Va{i}o"])
                VS_v = VS.rearrange("(kt p) c -> p kt c", p=128)
                jobs = []
                nq = 0; nun = 0
                for hk in range(nkv):
                    for gi in range(G):
                        h = hk * G + gi
                        for (t0, nn) in SPANS:
                            if t0 < NCTX and l == DEPTH - 1:
                                continue
                            kts = list(range(NCTX // 128)) if t0 < NCTX else list(range(NKT))
                            qi = nq % 2; nq += 1
                            oi = nun % 2; nun += 1
                            for ki, kt in enumerate(kts):
                                jobs.append(dict(hk=hk, h=h, t0=t0, nn=nn, kt=kt, ki=ki, nk=len(kts), qi=qi, oi=oi, bi=hk % 2,
                                                 first_of_head=(gi == 0 and ki == 0 and t0 == SPANS[1 if l == DEPTH - 1 else 0][0])))
                LOOK = 3

                def front(j, J):
                    bi_, qi, nn, kt, h, t0, hk = J["bi"], J["qi"], J["nn"], J["kt"], J["h"], J["t0"], J["hk"]
                    if J["first_of_head"]:
                        P.dma("sp", Kt[bi_][0:d, :], KT[hk * d:(hk + 1) * d, :], w=[f"Kt{bi_}"])
                        P.dma("sp", Va[bi_][:, :, 0:64], VS_v[:, :, hk * 64:(hk + 1) * 64], w=[f"Va{bi_}"])
                    if J["ki"] == 0:
                        P.dma("sp", Qt[qi][0:d, 0:nn], QT[h * d:(h + 1) * d, t0:t0 + nn], w=[f"Qt{qi}"])
                    si = j % 4
                    P.op("pe", lambda e: e.matmul(ps_s[si][:, 0:nn], Kt[bi_][0:d, kt * 128:(kt + 1) * 128], Qt[qi][0:d, 0:nn], start=True, stop=True),
                         r=[f"Kt{bi_}", f"Qt{qi}"], w=[f"pss{si}"])
                    P.op("act", lambda e: e.activation(pt[si][:, 0:nn], ps_s[si][:, 0:nn], AF.Exp, scale=float(scale)),
                         r=[f"pss{si}"], w=[f"pt{si}"])

                def back(j, J):
                    bi_, nn, kt, h, t0, ki, nk, oi = J["bi"], J["nn"], J["kt"], J["h"], J["t0"], J["ki"], J["nk"], J["oi"]
                    si = j % 4
                    po = ps_o[oi]; pok = f"pso{oi}"
                    P.op("pe", lambda e: e.matmul(po[:, 0:nn], Va[bi_][:, kt, :], pt[si][:, 0:nn], start=(ki == 0), stop=(ki == nk - 1)),
                         r=[f"Va{bi_}", f"Va{bi_}o", f"pt{si}"], w=[pok])
                    if ki == nk - 1:
                        P.op("dve", lambda e: e.reciprocal(rc[64:128, 0:nn], po[64:128, 0:nn]), r=[pok], w=["rc"])
                        P.op("dve", lambda e: e.tensor_tensor(ot[oi][:, 0:nn], po[0:64, 0:nn], rc[64:128, 0:nn], ALU.mult),
                             r=[pok, "rc"], w=[f"ot{oi}"])
                        P.dma("sp", OT[h * 64:(h + 1) * 64, t0:t0 + nn], ot[oi][:, 0:nn], r=[f"ot{oi}"], w=[("OT", h, t0)])

                for j in range(len(jobs) + LOOK):
                    if j < len(jobs):
                        front(j, jobs[j])
                    if j - LOOK >= 0:
                        back(j - LOOK, jobs[j - LOOK])
                P.barrier()

        def phase_rw1(l):
            with ExitStack() as ph:
                def Ft(name, w=512):
                    return sbt(ph, name, [128, w])
                raw = {x: Ft("raw" + x, 514) for x in "rkv"}
                cv = {x: Ft("cv" + x) for x in "rkv"}
                kkr = Ft("kkr"); sqk = Ft("sqk"); nrm = Ft("nrm"); kk = Ft("kk")
                low = Ft("low"); loa = Ft("loa")
                w2t = Ft("w2t"); a2t = Ft("a2t"); omka = sbt(ph, "omka", [128, 4])
                rmf = Ft("rmf"); rmb = Ft("rmb")
                sig = Ft("sig"); av = Ft("av"); cl = Ft("cl"); e2 = Ft("e2"); e3 = Ft("e3")
                tmp = Ft("tmp"); tmp2 = Ft("tmp2"); km = Ft("km")
                NB_ = 4
                At_l = [sbt(ph, f"At{i}", [128, 512], RW_DT) for i in range(NB_)]; Bt_l = [sbt(ph, f"Bt{i}", [128, 512], RW_DT) for i in range(NB_)]
                Kt_l = [sbt(ph, f"Kt_{i}", [128, 512], RW_DT) for i in range(NB_)]; Rt_l = [sbt(ph, f"Rt{i}", [128, 512], RW_DT) for i in range(NB_)]
                e1_l = [sbt(ph, f"e1{i}", [128, 512]) for i in range(NB_)]
                vb_l = [sbt(ph, f"vb{i}", [128, 512], RW_DT) for i in range(2)]; bon_l = [sbt(ph, f"bon{i}", [128, 512]) for i in range(2)]
                nob = {"n": 0, "v": 0}
                rk = Ft("rk")
                pw = pst(ph, "pw", [128, 512]); pa = pst(ph, "pa", [128, 512]); pn = pst(ph, "pn", [128, 512]); pb = pst(ph, "pb", [128, 512])
                P.dma("sp", w2t[:], Wd["rwkv_w2"][l].rearrange("d r c -> (d r) c"), w=["w2t"])
                P.dma("sp", a2t[:], Wd["rwkv_a2"][l].rearrange("d r c -> (d r) c"), w=["a2t"])
                P.dma("sp", rmf[:], Cd["rmask_f"], w=["rmf"])
                P.dma("sp", rmb[:], Cd["rmask_b"], w=["rmb"])
                P.op("dve", lambda e: e.tensor_scalar(omka[:], vF(l, F_KA, 4), -1.0, 1.0, ALU.mult, ALU.add), r=["vecF"], w=["omka"])
                for (t0, nn) in SPANS:
                    seg0, seg1 = (0, NCTX) if t0 < NCTX else (NCTX, T)
                    lo = max(t0 - 1, seg0); hi = min(t0 + nn + 1, seg1)
                    P.dma("sp", low[:, 0:nn], loT[0:128, t0:t0 + nn], w=["low"])
                    P.dma("sp", loa[:, 0:nn], loT[128:256, t0:t0 + nn], w=["loa"])
                    for c in range(4):
                        for xi, x in enumerate("rkv"):
                            rt = raw[x]
                            P.op("pool", lambda e, rt=rt: e.memset(rt[:, 0:1], 0.0), w=["raw" + x])
                            P.op("pool", lambda e, rt=rt, nn=nn: e.memset(rt[:, nn + 1:nn + 2], 0.0), w=["raw" + x])
                            r0 = xi * 512 + c * 128
                            P.dma("sp", rt[:, lo - (t0 - 1):hi - (t0 - 1)], rkvT[r0:r0 + 128, lo:hi], w=["raw" + x])
                            ch = xi * 4 + c
                            tp0, tp1, tp2 = (vF(l, F_CONV + j * 12 + ch) for j in range(3))
                            o = cv[x]
                            P.op("dve", lambda e, o=o, rt=rt, nn=nn, tp1=tp1: e.tensor_scalar(o[:, 0:nn], rt[:, 1:nn + 1], tp1, None, ALU.mult),
                                 r=["raw" + x], w=["cv" + x])
                            P.op("dve", lambda e, o=o, rt=rt, nn=nn, tp0=tp0: e.scalar_tensor_tensor(
                                o[:, 0:nn], rt[:, 0:nn], tp0, o[:, 0:nn], ALU.mult, ALU.add), r=["raw" + x], w=["cv" + x])
                            P.op("dve", lambda e, o=o, rt=rt, nn=nn, tp2=tp2: e.scalar_tensor_tensor(
                                o[:, 0:nn], rt[:, 2:nn + 2], tp2, o[:, 0:nn], ALU.mult, ALU.add), r=["raw" + x], w=["cv" + x])
                        rows = slice(c * 128, (c + 1) * 128)
                        vi_ = nob["v"] % 2; nob["v"] += 1
                        vb = vb_l[vi_]; bon = bon_l[vi_]; vbk = f"vb{vi_}"; bonk = f"bon{vi_}"
                        P.op("act", lambda e, nn=nn, vb=vb: e.copy(vb[:, 0:nn], cv["v"][:, 0:nn]), r=["cvv"], w=[vbk])
                        P.dma("sp", rwV[rows, t0:t0 + nn], vb[:, 0:nn], r=[vbk], w=[("rwV", c, t0)])
                        kkc = vF(l, F_KK + c)
                        P.op("dve", lambda e, nn=nn, kkc=kkc: e.tensor_scalar(kkr[:, 0:nn], cv["k"][:, 0:nn], kkc, None, ALU.mult), r=["cvk"], w=["kkr"])
                        P.op("pool", lambda e, nn=nn: e.tensor_tensor(sqk[:, 0:nn], kkr[:, 0:nn], kkr[:, 0:nn], ALU.mult), r=["kkr"], w=["sqk"])
                        P.op("pe", lambda e, nn=nn: e.matmul(pn[:, 0:nn], bones_f[:], sqk[:, 0:nn], start=True, stop=True), r=["sqk"], w=["pn"])
                        P.op("act", lambda e, nn=nn: e.activation(nrm[:, 0:nn], pn[:, 0:nn], AF.Sqrt), r=["pn"], w=["nrm"])
                        P.op("dve", lambda e, nn=nn: e.tensor_scalar(nrm[:, 0:nn], nrm[:, 0:nn], 1e-12, None, ALU.max), r=["nrm"], w=["nrm"])
                        P.op("dve", lambda e, nn=nn: e.reciprocal(nrm[:, 0:nn], nrm[:, 0:nn]), r=["nrm"], w=["nrm"])
                        P.op("dve", lambda e, nn=nn: e.tensor_tensor(kk[:, 0:nn], kkr[:, 0:nn], nrm[:, 0:nn], ALU.mult), r=["kkr", "nrm"], w=["kk"])
                        for d in range(2):
                            oi_ = nob["n"] % NB_; nob["n"] += 1
                            At = At_l[oi_]; Bt = Bt_l[oi_]; Kt_ = Kt_l[oi_]; Rt = Rt_l[oi_]; e1 = e1_l[oi_]
                            kAt, kBt, kKt, kRt, ke1 = f"At{oi_}", f"Bt{oi_}", f"Kt_{oi_}", f"Rt{oi_}", f"e1{oi_}"
                            ps_ = slice(64 * d, 64 * d + 64)
                            P.op("pe", lambda e, nn=nn, ps_=ps_, c=c: e.matmul(pw[:, 0:nn], w2t[ps_, c * 128:(c + 1) * 128], low[ps_, 0:nn], start=True, stop=True),
                                 r=["w2t", "low"], w=["pw"])
                            w0c = vF(l, F_W0 + d * 4 + c); a0c = vF(l, F_A0 + d * 4 + c)
                            P.op("act", lambda e, nn=nn, w0c=w0c: e.activation(sig[:, 0:nn], pw[:, 0:nn], AF.Sigmoid, bias=w0c), r=["pw", "vecF"], w=["sig"])
                            P.op("pool", lambda e, nn=nn: e.tensor_scalar(sig[:, 0:nn], sig[:, 0:nn], -0.6065306597126334, None, ALU.mult), r=["sig"], w=["sig"])
                            P.op("pe", lambda e, nn=nn, ps_=ps_, c=c: e.matmul(pa[:, 0:nn], a2t[ps_, c * 128:(c + 1) * 128], loa[ps_, 0:nn], start=True, stop=True),
                                 r=["a2t", "loa"], w=["pa"])
                            P.op("act", lambda e, nn=nn, a0c=a0c: e.activation(av[:, 0:nn], pa[:, 0:nn], AF.Sigmoid, bias=a0c), r=["pa", "vecF"], w=["av"])
                            if d == 0:
                                P.op("dve", lambda e, nn=nn: e.tensor_tensor_scan(cl[:, 0:nn], rmf[:, 0:nn], sig[:, 0:nn], 0.0, ALU.mult, ALU.add),
                                     r=["sig", "rmf"], w=["cl"])
                            else:
                                P.op("dve", lambda e, nn=nn: e.tensor_tensor_scan(cl[:, 0:nn][:, ::-1], rmb[:, 0:nn][:, ::-1], sig[:, 0:nn][:, ::-1],
                                                                                  0.0, ALU.mult, ALU.add), r=["sig", "rmb"], w=["cl"])
                            P.op("act", lambda e, nn=nn, e1=e1: e.activation(e1[:, 0:nn], cl[:, 0:nn], AF.Exp), r=["cl"], w=[ke1])
                            P.op("act", lambda e, nn=nn: e.activation(e2[:, 0:nn], cl[:, 0:nn], AF.Exp, scale=-1.0), r=["cl"], w=["e2"])
                            P.op("pool", lambda e, nn=nn: e.tensor_tensor(tmp[:, 0:nn], cl[:, 0:nn], sig[:, 0:nn], ALU.subtract), r=["cl", "sig"], w=["tmp"])
                            P.op("act", lambda e, nn=nn: e.activation(e3[:, 0:nn], tmp[:, 0:nn], AF.Exp), r=["tmp"], w=["e3"])
                            P.op("dve", lambda e, nn=nn, At=At: e.scalar_tensor_tensor(At[:, 0:nn], kk[:, 0:nn], -1.0, e3[:, 0:nn], ALU.mult, ALU.mult), r=["kk", "e3"], w=[kAt])
                            P.op("pool", lambda e, nn=nn: e.tensor_tensor(tmp[:, 0:nn], kk[:, 0:nn], av[:, 0:nn], ALU.mult), r=["kk", "av"], w=["tmp"])
                            P.op("pool", lambda e, nn=nn, Bt=Bt: e.tensor_tensor(Bt[:, 0:nn], tmp[:, 0:nn], e2[:, 0:nn], ALU.mult), r=["tmp", "e2"], w=[kBt])
                            kac = vF(l, F_KA + c); omc = omka[:, c:c + 1]
                            P.op("dve", lambda e, nn=nn, kac=kac, omc=omc: e.tensor_scalar(tmp2[:, 0:nn], av[:, 0:nn], kac, omc, ALU.mult, ALU.add),
                                 r=["av", "omka", "vecF"], w=["tmp2"])
                            P.op("pool", lambda e, nn=nn: e.tensor_tensor(km[:, 0:nn], cv["k"][:, 0:nn], tmp2[:, 0:nn], ALU.mult), r=["cvk", "tmp2"], w=["km"])
                            P.op("pool", lambda e, nn=nn, Kt_=Kt_: e.tensor_tensor(Kt_[:, 0:nn], km[:, 0:nn], e2[:, 0:nn], ALU.mult), r=["km", "e2"], w=[kKt])
                            P.op("dve", lambda e, nn=nn, Rt=Rt, e1=e1: e.tensor_tensor(Rt[:, 0:nn], cv["r"][:, 0:nn], e1[:, 0:nn], ALU.mult), r=["cvr", ke1], w=[kRt])
                            rkc = vF(l, F_RK + c)
                            P.op("dve", lambda e, nn=nn, rkc=rkc: e.scalar_tensor_tensor(rk[:, 0:nn], cv["r"][:, 0:nn], rkc, km[:, 0:nn], ALU.mult, ALU.mult),
                                 r=["cvr", "km", "vecF"], w=["rk"])
                            P.op("pe", lambda e, nn=nn, d=d: e.matmul(pb[:, 0:nn], bones_f[:], rk[:, 0:nn], start=(d == 0), stop=(d == 1)), r=["rk"], w=["pb"])
                            for (tl, dst, nm) in ((At, rwA, kAt), (Bt, rwB, kBt), (Kt_, rwK, kKt), (Rt, rwR, kRt), (e1, rwE, ke1)):
                                P.dma("sp", dst[d, rows, t0:t0 + nn], tl[:, 0:nn], r=[nm], w=[(nm, d, c, t0)])
                        P.op("dve", lambda e, nn=nn, bon=bon: e.tensor_tensor(bon[:, 0:nn], pb[:, 0:nn], cv["v"][:, 0:nn], ALU.mult), r=["pb", "cvv"], w=[bonk])
                        P.dma("sp", rwBon[rows, t0:t0 + nn], bon[:, 0:nn], r=[bonk], w=[("bon", c, t0)])
                P.barrier()

        def phase_rw2(l):
            import os
            with ExitStack() as ph:
                MK = [sbt(ph, f"MK{d}", [128, 256]) for d in range(2)]
                MK3 = [sbt(ph, f"MK3{d}", [128, 384]) for d in range(2)]
                m_ilT = sbt(ph, "m_ilT", [128, 128])
                P.dma("sp", m_ilT[:], Cd["m_il_T"], w=["m_ilT"])
                for d, (a_, b_, c_) in enumerate(((m_sl, m_slT, m_il), (m_slT, m_sl, m_ilT))):
                    P.op("dve", lambda e, d=d, a_=a_: e.tensor_copy(MK[d][:, 0:128], a_[:]), r=["m_ilT"], w=["MK"])
                    P.op("dve", lambda e, d=d, b_=b_: e.tensor_copy(MK[d][:, 128:256], b_[:]), r=["m_ilT"], w=["MK"])
                    P.op("dve", lambda e, d=d, a_=a_: e.tensor_copy(MK3[d][:, 0:128], a_[:]), r=["m_ilT"], w=["MK"])
                    P.op("dve", lambda e, d=d, c_=c_: e.tensor_copy(MK3[d][:, 128:256], c_[:]), r=["m_ilT"], w=["MK"])
                    P.op("dve", lambda e, d=d, c_=c_: e.tensor_copy(MK3[d][:, 256:384], c_[:]), r=["m_ilT"], w=["MK"])
                NCH = 4
                opnd = [[{X: sbt(ph, f"o{X}{ci}{b}", [128, 128], RW_DT) for X in "ABKRV"} for b in range(2)] for ci in range(NCH)]
                wc = [[sbt(ph, f"wc{ci}{b}", [128, 1]) for b in range(2)] for ci in range(NCH)]
                S2 = [[sbt(ph, f"S{d}{ci}", [128, 64]) for ci in range(NCH)] for d in range(2)]
                Sb2 = [[sbt(ph, f"Sb{d}{ci}", [128, 64], RW_DT) for ci in range(NCH)] for d in range(2)]
                S = list(S2[0]); Sb = list(Sb2[0])
                stackI_bf = sbt(ph, "stackI_bf", [128, 64], RW_DT)
                P.op("dve", lambda e: e.tensor_copy(stackI_bf[:], stackI[:]), w=["stackI_bf"])
                Np = [[sbt(ph, f"Np{ci}{b}", [128, 256], RW_DT) for b in range(2)] for ci in range(NCH)]
                NkB = [sbt(ph, f"NkB{ci}", [128, 384], RW_DT) for ci in range(NCH)]
                VBK = [sbt(ph, f"VBK{ci}", [128, 320], RW_DT) for ci in range(NCH)]
                Xs = [sbt(ph, f"Xs{ci}", [128, 64], RW_DT) for ci in range(NCH)]
                Us = [sbt(ph, f"Us{ci}", [128, 64], RW_DT) for ci in range(NCH)]
                Ys = [sbt(ph, f"Ys{ci}", [128, 64]) for ci in range(NCH)]
                tS = [sbt(ph, f"tS{ci}", [128, 64]) for ci in range(NCH)]
                Xps = [pst(ph, f"Xps{ci}", [128, 512]) for ci in range(NCH)]
                tps = [pst(ph, f"tps{i}", [128, 512]) for i in range(4)]
                tn = {"n": 0, "dq": 0, "cp": 0}

                def tnext():
                    i = tn["n"] % 4; tn["n"] += 1
                    return tps[i], f"tps{i}"

                def dq():
                    tn["dq"] += 1
                    return "sp"

                def cpy(out, in_, r, w):
                    tn["cp"] += 1
                    if tn["cp"] % 2:
                        P.op("act", lambda e: e.copy(out, in_), r=r, w=w)
                    else:
                        P.op("dve", lambda e: e.tensor_copy(out, in_), r=r, w=w)

                for ci in range(NCH):
                    for d_ in range(2):
                        P.op("pool", lambda e, t_=S2[d_][ci]: e.memset(t_[:], 0.0), w=[f"S{ci}"])
                        P.op("pool", lambda e, t_=Sb2[d_][ci]: e.memset(t_[:], 0.0), w=[f"Sb{ci}"])
                    for b in range(2):
                        for X in "ABKRV":
                            P.op("pool", lambda e, t_=opnd[ci][b][X]: e.memset(t_[:], 0.0), w=[f"o{X}{ci}{b}"])

                def load(ci, d, c, q, b):
                    ts = q * 64
                    O = opnd[ci][b]
                    for X, src in (("A", rwA[d]), ("B", rwB[d]), ("K", rwK[d]), ("R", rwR[d]), ("V", rwV)):
                        for h in range(2):
                            P.dma(dq(), O[X][h * 64:(h + 1) * 64, h * 64:(h + 1) * 64],
                                  src[c * 128 + h * 64:c * 128 + (h + 1) * 64, ts:ts + 64], w=[f"o{X}{ci}{b}"])
                    tl = ts + 63 if d == 0 else ts
                    if not os.environ.get("NOWC"):
                        P.dma(dq(), wc[ci][b][:, 0:1], rwE[d, c * 128:(c + 1) * 128, tl:tl + 1], w=[f"wc{ci}{b}"], allow_slow_non_contiguous=True)

                def compute(ci, d, c, q, b):
                    ts = q * 64
                    Sc = S[ci]; Sbc = Sb[ci]
                    O = opnd[ci][b]
                    ko = lambda X: f"o{X}{ci}{b}"
                    t1, k1 = tnext()
                    P.op("pe", lambda e: e.matmul(t1[:, 0:128], O["B"][:], O["A"][:], start=True, stop=True), r=[ko("B"), ko("A")], w=[k1])
                    P.op("pe", lambda e: e.matmul(t1[:, 128:256], O["A"][:], O["B"][:], start=True, stop=True), r=[ko("B"), ko("A")], w=[k1])
                    P.op("dve", lambda e: e.tensor_tensor(Np[ci][0][:, :], t1[:, 0:256], MK[d][:, :], ALU.mult), r=[k1, "MK"], w=[f"Np{ci}0N", f"Np{ci}0T"])
                    t2, k2 = tnext()
                    P.op("pe", lambda e: e.matmul(t2[:, 0:128], O["K"][:], O["A"][:], start=True, stop=True), r=[ko("K"), ko("A")], w=[k2])
                    P.op("pe", lambda e: e.matmul(t2[:, 128:256], O["B"][:], O["R"][:], start=True, stop=True), r=[ko("B"), ko("R")], w=[k2])
                    P.op("pe", lambda e: e.matmul(t2[:, 256:384], O["K"][:], O["R"][:], start=True, stop=True), r=[ko("K"), ko("R")], w=[k2])
                    P.op("dve", lambda e: e.tensor_tensor(NkB[ci][:, :], t2[:, 0:384], MK3[d][:, :], ALU.mult), r=[k2, "MK"], w=[f"NkB{ci}"])
                    t3, k3 = tnext()
                    P.op("pe", lambda e: e.matmul(t3[:, 0:64], O["V"][:], stackI_bf[:], start=True, stop=True), r=[ko("V")], w=[k3])
                    if RW_DT == F32:
                        P.op("pe", lambda e: e.transpose(t3[:, 64:192], O["B"][:], ident_f[:]), r=[ko("B")], w=[k3])
                        P.op("pe", lambda e: e.transpose(t3[:, 192:320], O["K"][:], ident_f[:]), r=[ko("K")], w=[k3])
                    else:
                        P.op("pe", lambda e: e.matmul(t3[:, 64:192], O["B"][:], ident_bf[:], start=True, stop=True), r=[ko("B")], w=[k3])
                        P.op("pe", lambda e: e.matmul(t3[:, 192:320], O["K"][:], ident_bf[:], start=True, stop=True), r=[ko("K")], w=[k3])
                    P.op("act", lambda e: e.copy(VBK[ci][:, :], t3[:, 0:320]), r=[k3], w=[f"VBK{ci}"])
                    yield
                    xp = Xps[ci]; xk = f"Xps{ci}"
                    P.op("pe", lambda e: e.matmul(xp[:, 0:64], O["A"][:], Sbc[:], start=True, stop=False), r=[ko("A"), f"Sb{ci}"], w=[xk])
                    P.op("pe", lambda e: e.matmul(xp[:, 0:64], NkB[ci][:, 0:128], VBK[ci][:, 0:64], start=False, stop=True),
                         r=[f"NkB{ci}", f"VBK{ci}"], w=[xk])
                    for k in range(6):
                        cur = Np[ci][k % 2]; ckN = f"Np{ci}{k % 2}N"; ckT = f"Np{ci}{k % 2}T"
                        cpy(Xs[ci][:, :], xp[:, 0:64], r=[xk], w=[f"Xs{ci}"])
                        yield
                        P.op("pe", lambda e: e.matmul(xp[:, 0:64], ident_f[:], Xs[ci][:, :], start=True, stop=False), r=[f"Xs{ci}"], w=[xk])
                        P.op("pe", lambda e, cur=cur, k=k: e.matmul(xp[:, 0:64], cur[:, 0:128], Xs[ci][:, :], start=False, stop=True),
                             r=[ckN, f"Xs{ci}"], w=[xk])
                        if k < 5:
                            nxt = Np[ci][(k + 1) % 2]; nkN = f"Np{ci}{(k + 1) % 2}N"; nkT = f"Np{ci}{(k + 1) % 2}T"
                            tt, kt = tnext()
                            P.op("pe", lambda e, cur=cur, tt=tt: e.matmul(tt[:, 0:128], cur[:, 128:256], cur[:, 0:128], start=True, stop=True), r=[ckN, ckT], w=[kt])
                            cpy(nxt[:, 0:128], tt[:, 0:128], r=[kt], w=[nkN])
                            if k < 4:
                                tt2, kt2 = tnext()
                                P.op("pe", lambda e, nxt=nxt, tt2=tt2: e.transpose(tt2[:, 0:128], nxt[:, 0:128], ident_f[:]), r=[nkN], w=[kt2])
                                cpy(nxt[:, 128:256], tt2[:, 0:128], r=[kt2], w=[nkT])
                        yield
                    cpy(Us[ci][:, :], xp[:, 0:64], r=[xk], w=[f"Us{ci}"])
                    yield
                    ty, ky = tnext()
                    P.op("pe", lambda e: e.matmul(ty[:, 0:64], O["R"][:], Sbc[:], start=True, stop=False), r=[ko("R"), f"Sb{ci}"], w=[ky])
                    P.op("pe", lambda e: e.matmul(ty[:, 0:64], NkB[ci][:, 128:256], Us[ci][:, :], start=False, stop=False), r=[f"NkB{ci}", f"Us{ci}"], w=[ky])
                    P.op("pe", lambda e: e.matmul(ty[:, 0:64], NkB[ci][:, 256:384], VBK[ci][:, 0:64], start=False, stop=True), r=[f"NkB{ci}", f"VBK{ci}"], w=[ky])
                    P.op("pe", lambda e: e.matmul(ty[:, 64:128], VBK[ci][:, 64:192], Us[ci][:, :], start=True, stop=False), r=[f"VBK{ci}", f"Us{ci}"], w=[ky])
                    P.op("pe", lambda e: e.matmul(ty[:, 64:128], VBK[ci][:, 192:320], VBK[ci][:, 0:64], start=False, stop=True), r=[f"VBK{ci}"], w=[ky])
                    P.op("act", lambda e: e.copy(Ys[ci][:, :], ty[:, 0:64]), r=[ky], w=[f"Ys{ci}"])
                    P.op("dve", lambda e: e.tensor_tensor(tS[ci][:, :], ty[:, 64:128], Sc[:, :], ALU.add), r=[ky, f"S{ci}"], w=[f"tS{ci}"])
                    P.op("dve", lambda e: e.tensor_scalar(Sc[:, :], tS[ci][:, :], wc[ci][b][:, 0:1], None, ALU.mult),
                         r=[f"tS{ci}", f"wc{ci}{b}"], w=[f"S{ci}"], ss=True)
                    P.op("act", lambda e: e.copy(Sbc[:, :], Sc[:, :]), r=[f"S{ci}"], w=[f"Sb{ci}"])
                    for h in range(2):
                        P.dma(dq(), rwY[d, ts:ts + 64, c * 128 + h * 64:c * 128 + (h + 1) * 64], Ys[ci][h * 64:(h + 1) * 64, :],
                              r=[f"Ys{ci}"], w=[("rwY", d, c, q, h)])
                    yield

                nctx_ch = NCTX // CH
                _lim = int(os.environ.get("RW2_STEPS", "100000"))
                for d in range(int(os.environ.get("RW2_D0", "0")), int(os.environ.get("RW2_DIRS", "2"))):
                    if d == 0:
                        qs = list(range(NCHUNK))
                    else:
                        qs = list(range(nctx_ch - 1, -1, -1)) + list(range(NCHUNK - 1, nctx_ch - 1, -1))
                    for ci in range(NCH):
                        S[ci] = S2[d][ci]; Sb[ci] = Sb2[d][ci]
                        load(ci, d, ci, qs[0], 0)
                    qs = qs[:_lim]
                    for si, q in enumerate(qs):
                        b = si % 2
                        gens = []
                        for ci in range(NCH):
                            if si + 1 < len(qs):
                                load(ci, d, ci, qs[si + 1], 1 - b)
                            gens.append(compute(ci, d, ci, q, b))
                        alive = list(gens)
                        while alive:
                            nxt_alive = []
                            for g in alive:
                                try:
                                    next(g)
                                    nxt_alive.append(g)
                                except StopIteration:
                                    pass
                            alive = nxt_alive
                P.barrier()

        def phase_rw3(l):
            with ExitStack() as ph:
                g2t = sbt(ph, "g2t", [128, 512]); epsg = sbt(ph, "epsg", [128, 1])
                y0 = [sbt(ph, f"y0{i}", [128, 512]) for i in range(2)]; y1 = [sbt(ph, f"y1{i}", [128, 512]) for i in range(2)]
                ys = sbt(ph, "ys", [128, 512]); sq = sbt(ph, "sq", [128, 512]); yn = sbt(ph, "yn", [128, 512])
                st = [sbt(ph, f"st{i}", [128, 32]) for i in range(2)]
                log_ = sbt(ph, "log", [128, 512]); z = sbt(ph, "z", [128, 512]); bn = sbt(ph, "bn", [128, 512])
                og = [sbt(ph, f"og{i}", [128, 512], BF16) for i in range(2)]
                ptr = [pst(ph, f"ptr{c}", [128, 512]) for c in range(4)]
                pg = [pst(ph, f"pg{i}", [128, 512]) for i in range(2)]
                P.dma("sp", g2t[:], Wd["rwkv_g2"][l], w=["g2t"])
                P.op("dve", lambda e: e.memset(epsg[:], GN_EPS), w=["epsg"])
                n = 0; ng = 0
                for (t0, nn) in SPANS:
                    if t0 < NCTX and l == DEPTH - 1:
                        continue
                    for s_ in range(nn // 128):
                        i = n % 2; n += 1
                        r0 = t0 + s_ * 128
                        P.dma("sp", y0[i][:], rwY[0, r0:r0 + 128, :], w=[f"y0{i}"])
                        P.dma("pool", y1[i][:], rwY[1, r0:r0 + 128, :], w=[f"y1{i}"])
                        P.op("pool", lambda e, i=i: e.tensor_tensor(ys[:], y0[i][:], y1[i][:], ALU.add), r=[f"y0{i}", f"y1{i}"], w=["ys"])
                        sti = st[i]; sk = f"st{i}"
                        P.op("dve", lambda e, sti=sti: e.tensor_reduce(sti[:, 0:8], ys[:].rearrange("p (h d) -> p h d", d=64), AX.X, ALU.add),
                             r=["ys"], w=[sk])
                        P.op("dve", lambda e, sti=sti: e.tensor_scalar(sti[:, 8:16], sti[:, 0:8], -1.0 / 64, None, ALU.mult), r=[sk], w=[sk + "m"], ss=True)
                        for h in range(8):
                            P.op("dve", lambda e, sti=sti, h=h: e.tensor_scalar(yn[:, h * 64:(h + 1) * 64], ys[:, h * 64:(h + 1) * 64],
                                                                                 sti[:, 8 + h:9 + h], None, ALU.add), r=["ys", sk + "m"], w=["yc"], ss=True)
                        P.op("pool", lambda e: e.tensor_tensor(sq[:], yn[:], yn[:], ALU.mult), r=["yc"], w=["sq"])
                        P.op("dve", lambda e, sti=sti: e.tensor_reduce(sti[:, 16:24], sq[:].rearrange("p (h d) -> p h d", d=64), AX.X, ALU.add),
                             r=["sq"], w=[sk + "v"])
                        P.op("act", lambda e, sti=sti: e.activation(sti[:, 24:32], sti[:, 16:24], AF.Ln, scale=1.0 / 64, bias=epsg[:, 0:1]),
                             r=[sk + "v", "epsg"], w=[sk + "l"], ss=True)
                        P.op("act", lambda e, sti=sti: e.activation(sti[:, 16:24], sti[:, 24:32], AF.Exp, scale=-0.5), r=[sk + "l"], w=[sk + "r"], ss=True)
                        for h in range(8):
                            P.op("dve", lambda e, sti=sti, h=h: e.tensor_scalar(yn[:, h * 64:(h + 1) * 64], yn[:, h * 64:(h + 1) * 64],
                                                                                 sti[:, 16 + h:17 + h], None, ALU.mult), r=["yc", "sq", sk + "r"], w=["yn"], ss=True)
                        for c in range(4):
                            P.op("pe", lambda e, c=c, s_=s_: e.transpose(ptr[c][:, s_ * 128:(s_ + 1) * 128], yn[:, c * 128:(c + 1) * 128], ident_f[:]),
                                 r=["yn"], w=[f"ptr{c}"])
                    P.dma("sp", log_[:, 0:nn], loT[256:384, t0:t0 + nn], w=["log"])
                    for c in range(4):
                        P.op("act", lambda e, c=c, nn=nn: e.activation(z[:, 0:nn], ptr[c][:, 0:nn], AF.Identity, scale=vF(l, F_LNG + c), bias=vF(l, F_LNB + c)),
                             r=[f"ptr{c}", "vecF"], w=["z"])
                        P.dma("sp", bn[:, 0:nn], rwBon[c * 128:(c + 1) * 128, t0:t0 + nn], w=["bn"])
                        P.op("pool", lambda e, nn=nn: e.tensor_tensor(z[:, 0:nn], z[:, 0:nn], bn[:, 0:nn], ALU.add), r=["bn", "z"], w=["z"])
                        gi = ng % 2; ng += 1
                        P.op("pe", lambda e, c=c, nn=nn, gi=gi: e.matmul(pg[gi][:, 0:nn], g2t[:, c * 128:(c + 1) * 128], log_[:, 0:nn], start=True, stop=True),
                             r=["g2t", "log"], w=[f"pg{gi}"])
                        P.op("dve", lambda e, nn=nn, gi=gi: e.tensor_tensor(og[gi][:, 0:nn], pg[gi][:, 0:nn], z[:, 0:nn], ALU.mult), r=[f"pg{gi}", "z"], w=[f"og{gi}"])
                        P.dma("sp", ocT[c * 128:(c + 1) * 128, t0:t0 + nn], og[gi][:, 0:nn], r=[f"og{gi}"], w=[("ocT", c, t0)])
                P.barrier()

        def post_residual(tl, l, py, j, which, src, r0, dst, d0, tag):
            ysb = tl["ysb"]; ssq = tl["ssq"]; xt2 = tl["xt2"]; tmp = tl["tmp"]; junk = tl["junk2"]
            for hh in range(2):
                P.op("act", lambda e, hh=hh: e.activation(junk[:, 0:512], py[hh][:, :], AF.Square, accum_out=ssq[:, hh:hh + 1]),
                     r=[tag + f"py{hh}"], w=["junk2", f"ssq{hh}"])
                P.op("dve", lambda e, hh=hh: e.tensor_copy(ysb[:, hh * 512:(hh + 1) * 512], py[hh][:, :]), r=[tag + f"py{hh}"], w=["ysb"])
            P.op("dve", lambda e: e.tensor_tensor(ssq[:, 2:3], ssq[:, 0:1], ssq[:, 1:2], ALU.add), r=["ssq0", "ssq1"], w=["ssq2"], ss=True)
            P.op("act", lambda e: e.activation(ssq[:, 3:4], ssq[:, 2:3], AF.Ln, scale=1.0 / D, bias=tl["epsc"][:, 0:1]), r=["ssq2", "epsc"], w=["ssq3"], ss=True)
            P.op("act", lambda e: e.activation(ssq[:, 4:5], ssq[:, 3:4], AF.Exp, scale=-0.5), r=["ssq3"], w=["ssq4"], ss=True)
            P.dma("sp", xt2[:], src[r0:r0 + 128, :], w=["xt2"])
            P.op("pool", lambda e: e.tensor_tensor(tmp[:], ysb[:], GB[j][which][:], ALU.mult), r=["ysb"], w=["tmp"])
            P.op("dve", lambda e: e.scalar_tensor_tensor(tmp[:], tmp[:], ssq[:, 4:5], xt2[:], ALU.mult, ALU.add), r=["tmp", "ssq4", "xt2"], w=["tmp"], ss=True)
            P.dma("sp", dst[d0:d0 + 128, :], tmp[:], r=["tmp"], w=[(tag, "out", d0)])

        def post_tiles(ph, junk=None):
            tl = {}
            tl["ysb"] = sbt(ph, "ysb", [128, D]); tl["ssq"] = sbt(ph, "ssq", [128, 8]); tl["xt2"] = sbt(ph, "xt2", [128, D])
            tl["tmp"] = sbt(ph, "tmp", [128, D]); tl["junk2"] = junk if junk is not None else sbt(ph, "junk2", [128, 512], BF16)
            return tl

        def phase_merge(l):
            src = xin if l == 0 else xres
            with ExitStack() as ph:
                wb = sbt(ph, "wb", [128, 12, D], BF16); wo = sbt(ph, "wo", [128, 8, D], BF16)
                br = sbt(ph, "br", [128, 12, 512], BF16)
                gt = [sbt(ph, f"gt{i}", [128, 512], BF16) for i in range(3)]
                acc = sbt(ph, "acc", [128, 8, 512], BF16)
                ta = sbt(ph, "ta", [128, 512]); tb = sbt(ph, "tb", [128, 512])
                tl = post_tiles(ph)
                tl["epsc"] = sbt(ph, "epsc", [128, 1])
                P.op("dve", lambda e: e.memset(tl["epsc"][:], EPS), w=["epsc"])
                pm = [pst(ph, f"pm{i}", [128, 512]) for i in range(3)]
                py = [pst(ph, f"py{i}", [128, 512]) for i in range(2)]
                for b in range(3):
                    for k in range(4):
                        P.dma("pool", wb[:, b * 4 + k, :], Wd["w_branch"][l, b, k * 128:(k + 1) * 128, :], w=["wb"])
                for k in range(8):
                    P.dma("pool", wo[:, k, :], Wd["w_out"][l, k * 128:(k + 1) * 128, :], w=["wo"])
                n = 0
                for (t0, nn) in SPANS:
                    if t0 < NCTX and l == DEPTH - 1:
                        continue
                    j = 1 if t0 < NCTX else 0
                    for b, oT in enumerate((oaT, obT, ocT)):
                        P.dma("sp", br[:, b * 4:(b + 1) * 4, 0:nn], oT.rearrange("(k p) t -> p k t", p=128)[:, :, t0:t0 + nn], w=["br"])
                    for n_ in range(8):
                        for b in range(3):
                            i = n % 3; n += 1
                            g0 = b * 1024 + n_ * 128
                            P.dma("sp", gt[i][:, 0:nn], gateT[g0:g0 + 128, t0:t0 + nn], w=[f"gt{i}"])
                            for k in range(4):
                                P.op("pe", lambda e, i=i, b=b, k=k, n_=n_, nn=nn: e.matmul(
                                    pm[i][:, 0:nn], wb[:, b * 4 + k, n_ * 128:(n_ + 1) * 128], br[:, b * 4 + k, 0:nn], start=(k == 0), stop=(k == 3)),
                                    r=["wb", "br"], w=[f"pm{i}"])
                            if b == 0:
                                P.op("dve", lambda e, i=i, nn=nn: e.tensor_tensor(ta[:, 0:nn], pm[i][:, 0:nn], gt[i][:, 0:nn], ALU.mult), r=[f"pm{i}", f"gt{i}"], w=["ta"])
                            else:
                                P.op("dve", lambda e, i=i, nn=nn: e.tensor_tensor(tb[:, 0:nn], pm[i][:, 0:nn], gt[i][:, 0:nn], ALU.mult), r=[f"pm{i}", f"gt{i}"], w=["tb"])
                                if b == 1:
                                    P.op("pool", lambda e, nn=nn: e.tensor_tensor(ta[:, 0:nn], ta[:, 0:nn], tb[:, 0:nn], ALU.add), r=["ta", "tb"], w=["ta"])
                                else:
                                    P.op("pool", lambda e, nn=nn, n_=n_: e.tensor_tensor(acc[:, n_, 0:nn], ta[:, 0:nn], tb[:, 0:nn], ALU.add), r=["ta", "tb"], w=["acc"])
                    for s_ in range(nn // 128):
                        for hh in range(2):
                            for k in range(8):
                                P.op("pe", lambda e, hh=hh, k=k, s_=s_: e.matmul(py[hh][:, :], acc[:, k, s_ * 128:(s_ + 1) * 128], wo[:, k, hh * 512:(hh + 1) * 512],
                                                                                start=(k == 0), stop=(k == 7)), r=["acc", "wo"], w=[f"mpy{hh}"])
                        r0 = t0 + s_ * 128
                        post_residual(tl, l, py, j, 0, src, r0, xres, r0, "m")
                P.barrier()

        def phase_ffn(l):
            last = l == DEPTH - 1
            with ExitStack() as ph:
                w1 = sbt(ph, "w1", [128, 8, 2 * FFH], BF16); w2 = sbt(ph, "w2", [128, 22, D], BF16)
                nt = norm_tiles(ph, 1)
                hT = sbt(ph, "hT", [128, 8, 512], BF16)
                actT = sbt(ph, "actT", [128, 22, 512], BF16)
                sil = [sbt(ph, f"sil{i}", [128, 512]) for i in range(2)]
                tl = post_tiles(ph, nt["junk"]); tl["epsc"] = nt["epsc"]
                pgu = [pst(ph, f"pgu{i}", [128, 512]) for i in range(4)]
                py = [pst(ph, f"fpy{i}", [128, 512]) for i in range(2)]
                for k in range(8):
                    P.dma("pool", w1[:, k, :], Wd["ffn_w_in"][l, k * 128:(k + 1) * 128, :], w=["w1"])
                for cc in range(22):
                    P.dma("pool", w2[:, cc, :], Wd["ffn_w_out"][l, cc * 128:(cc + 1) * 128, :], w=["w2"])
                n = 0
                for (t0, nn) in SPANS:
                    if t0 < NCTX and last:
                        continue
                    j = 1 if t0 < NCTX else 0
                    norm_modulate_T(nt, l, xres, t0, nn, 1, hT, "f")
                    for cc in range(22):
                        i = n % 2; n += 1
                        pg_, pu_ = pgu[2 * i], pgu[2 * i + 1]
                        for k in range(8):
                            P.op("pe", lambda e, pg_=pg_, k=k, cc=cc, nn=nn: e.matmul(pg_[:, 0:nn], w1[:, k, cc * 128:(cc + 1) * 128], hT[:, k, 0:nn],
                                                                                      start=(k == 0), stop=(k == 7)), r=["w1", ("f", "hT")], w=[f"pgu{2 * i}"])
                        for k in range(8):
                            P.op("pe", lambda e, pu_=pu_, k=k, cc=cc, nn=nn: e.matmul(pu_[:, 0:nn], w1[:, k, FFH + cc * 128:FFH + (cc + 1) * 128], hT[:, k, 0:nn],
                                                                                      start=(k == 0), stop=(k == 7)), r=["w1", ("f", "hT")], w=[f"pgu{2 * i + 1}"])
                        P.op("act", lambda e, pg_=pg_, i=i, nn=nn: e.activation(sil[i][:, 0:nn], pg_[:, 0:nn], AF.Silu), r=[f"pgu{2 * i}"], w=[f"sil{i}"])
                        P.op("dve", lambda e, pu_=pu_, i=i, nn=nn, cc=cc: e.tensor_tensor(actT[:, cc, 0:nn], pu_[:, 0:nn], sil[i][:, 0:nn], ALU.mult),
                             r=[f"pgu{2 * i + 1}", f"sil{i}"], w=["actT"])
                    for s_ in range(nn // 128):
                        for hh in range(2):
                            for cc in range(22):
                                P.op("pe", lambda e, hh=hh, cc=cc, s_=s_: e.matmul(py[hh][:, :], actT[:, cc, s_ * 128:(s_ + 1) * 128], w2[:, cc, hh * 512:(hh + 1) * 512],
                                                                                  start=(cc == 0), stop=(cc == 21)), r=["actT", "w2"], w=[f"fpy{hh}"])
                        r0 = t0 + s_ * 128
                        if last:
                            post_residual(tl, l, py, j, 1, xres, r0, out_d, r0 - NCTX, "f")
                        else:
                            post_residual(tl, l, py, j, 1, xres, r0, xres, r0, "f")
                P.barrier()

        PH = {"mod": phase_mod, "inproj": phase_inproj, "mla": phase_mla, "gqa": phase_gqa,
              "attna": lambda l: phase_attn(l, "a"), "attnb": lambda l: phase_attn(l, "b"),
              "rw1": phase_rw1, "rw2": phase_rw2, "rw3": phase_rw3, "merge": phase_merge, "ffn": phase_ffn}
        seq = ["mod", "inproj", "rw1", "rw2", "rw3", "mla", "gqa", "attna", "attnb", "merge", "ffn"]
        if skip:
            seq = [x for x in seq if x not in skip]
        order = []
        for l in range(nlayers):
            order += [(x, l) for x in seq]
        for name, l in order:
            PH[name](l)
            if upto == (name, l):
                break
        P.barrier()
        P.emit()
        print("instructions:", P.ninstr, {e: P.cnt[e] for e in ENGS})
    return nc


def make_in_maps(inputs, cores=range(8)):
    inp = {k: np.asarray(v) for k, v in inputs.items()}
    consts = _host_consts()
    vf, vr = _host_vecs(inp)
    maps = []
    for b in cores:
        m = {}
        m["xin"] = np.ascontiguousarray(np.concatenate([inp["ctx"][b], inp["x"][b]], axis=0), dtype=np.float32)
        cc = np.stack([inp["c"][b].reshape(8, 128).T, inp["c_ctx"].reshape(8, 128).T], axis=-1)
        m["csil"] = np.ascontiguousarray(cc.reshape(128, 16), dtype=np.float32)
        m["vecF"] = vf
        m["vecR"] = vr
        for k in WEIGHTS:
            m[k] = np.ascontiguousarray(inp[k], dtype=np.float32)
        for k in CONST_SHAPES:
            m[k] = consts[k]
        maps.append(m)
    return maps


def kernel(**inputs):
    nc = build_program()
    maps = make_in_maps(inputs)
    res = run_bass_kernel_spmd(nc, maps, core_ids=list(range(8)))
    return np.stack([r["out"] for r in res.results], axis=0).astype(np.float32)
```

```python
import numpy as np
from contextlib import ExitStack
import concourse.bass as bass
import concourse.mybir as mybir
from concourse.bass_utils import run_bass_kernel_spmd

F32 = mybir.dt.float32
BF16 = mybir.dt.bfloat16
AF = mybir.ActivationFunctionType
ALU = mybir.AluOpType
AX = mybir.AxisListType

ENGS = ("pe", "dve", "act", "pool", "sp")
EPOCH = 12000
N_DMA_SLOTS = 8

D = 1024
NCTX = 256
NLAT = 4096
T = NCTX + NLAT
DEPTH = 2
GRID_W = 64
EPS = 1e-6
GN_EPS = 64e-5
MLA_SCALE = 96 ** -0.5
GQA_SCALE = 64 ** -0.5
FFH = 2816
IN_W = 6432
O_CQ, O_CKV, O_KR, O_GQ, O_GK, O_GV, O_RKV, O_WLO, O_ALO, O_GLO, O_GATE = (
    0, 384, 640, 672, 1184, 1312, 1440, 2976, 3104, 3232, 3360)
SPANS = [(0, 256)] + [(256 + 512 * i, 512) for i in range(8)]
CH = 64
RW_DT = F32
NCHUNK = T // CH

NF = 96
F_PRE, F_FPRE, F_QN, F_KVN, F_GQ, F_GK, F_CONV, F_W0, F_A0, F_KK, F_KA, F_RK, F_LNG, F_LNB = (
    0, 8, 16, 19, 21, 22, 23, 59, 67, 75, 79, 83, 87, 91)
NR = 8192
R_ADAB, R_POST, R_FPOST = 0, 6144, 7168


PSUM_PREFIXES = ("mps", "tp", "ips", "pss", "pq", "pr", "pso", "pw", "pa", "pn", "pb", "Xps", "tps", "ptr", "pg", "pm",
                 "mpy", "fpy")


class Prog:
    def __init__(self, nc, stack, same_engine_sync=False):
        self.nc = nc
        self.stack = stack
        self.same_engine_sync = same_engine_sync
        self.q = {e: [] for e in ENGS}
        self.cnt = {e: 0 for e in ENGS}
        self.esems = {e: [] for e in ENGS}
        self.waited = {e: {} for e in ENGS}
        self.state = {}
        self.dslots = {}
        self.dnext = {}
        for e in ("sp", "act", "pool"):
            self.dslots[e] = [[self._newsem(f"d{e}{i}"), 0] for i in range(N_DMA_SLOTS)]
            self.dnext[e] = 0
        self.ninstr = 0

    def _newsem(self, name):
        return self.stack.enter_context(self.nc.semaphore(name))

    def _my_event(self, eng):
        c = self.cnt[eng]
        ep = c // EPOCH
        while len(self.esems[eng]) <= ep:
            self.esems[eng].append(self._newsem(f"e{eng}{len(self.esems[eng])}"))
        self.cnt[eng] = c + 1
        return (self.esems[eng][ep], c - ep * EPOCH + 1, eng)

    def _deps(self, eng, reads, writes, self_sync=False):
        need = {}

        def add(ev):
            if ev is None:
                return
            sem, val, src = ev
            if src == eng and (eng == "pe" or not (self.same_engine_sync or self_sync)):
                return
            k = id(sem)
            if k not in need or need[k][1] < val:
                need[k] = (sem, val)

        for k in reads:
            st = self.state.get(k)
            if st is not None:
                add(st["w"])
                if isinstance(k, str) and k.startswith(PSUM_PREFIXES):
                    for ev in st["r"].values():
                        if ev[2] != eng:
                            add(ev)
        for k in writes:
            st = self.state.get(k)
            if st is not None:
                add(st["w"])
                for ev in st["r"].values():
                    add(ev)
        out = []
        wd = self.waited[eng]
        for k, (sem, val) in need.items():
            if wd.get(k, 0) >= val:
                continue
            wd[k] = val
            out.append((sem, val))
        return out

    def _commit(self, ev, reads, writes):
        for k in writes:
            self.state[k] = {"w": ev, "r": {}}
        for k in reads:
            if k in writes:
                continue
            st = self.state.setdefault(k, {"w": None, "r": {}})
            st["r"][id(ev[0])] = ev

    def op(self, eng, fn, r=(), w=(), ss=False):
        r = list(r)
        w = list(w)
        waits = self._deps(eng, r, w, self_sync=ss)
        ev = self._my_event(eng)
        sem = ev[0]

        def run(e, fn=fn, waits=waits, sem=sem):
            for s, v in waits:
                e.wait_ge(s, v)
            fn(e).then_inc(sem, 1)

        self.q[eng].append(run)
        self._commit(ev, r, w)
        self.ninstr += 1

    def dma(self, qeng, out, in_, r=(), w=(), **kw):
        r = list(r)
        w = list(w)
        waits = self._deps(qeng, r, w)
        i = self.dnext[qeng]
        self.dnext[qeng] = (i + 1) % N_DMA_SLOTS
        slot = self.dslots[qeng][i]
        sem, tot = slot
        wd = self.waited[qeng]
        if tot > 0 and wd.get(id(sem), 0) < tot:
            waits.append((sem, tot))
            wd[id(sem)] = tot
        slot[1] = tot + 16
        ev = (sem, tot + 16, "dma")

        def run(e, waits=waits, sem=sem, out=out, in_=in_, kw=kw):
            for s, v in waits:
                e.wait_ge(s, v)
            e.dma_start(out=out, in_=in_, **kw).then_inc(sem, 16)

        self.q[qeng].append(run)
        self._commit(ev, r, w)
        self.ninstr += 1

    def barrier(self):
        evs = []
        for e in ENGS:
            c = self.cnt[e]
            if c > 0:
                ep = (c - 1) // EPOCH
                evs.append((self.esems[e][ep], c - ep * EPOCH, e))
        for q in self.dslots:
            for sem, tot in self.dslots[q]:
                if tot > 0:
                    evs.append((sem, tot, "dma"))
        for e in ENGS:
            wd = self.waited[e]
            waits = []
            for s, v, src in evs:
                if src == e:
                    continue
                if wd.get(id(s), 0) < v:
                    wd[id(s)] = v
                    waits.append((s, v))

            def run(eng, waits=waits):
                for s, v in waits:
                    eng.wait_ge(s, v)

            self.q[e].append(run)
        self.state = {}

    def emit(self):
        nc = self.nc
        with nc.Block() as block:
            @block.tensor
            def _(e):
                for f in self.q["pe"]:
                    f(e)

            @block.vector
            def _(e):
                for f in self.q["dve"]:
                    f(e)

            @block.scalar
            def _(e):
                for f in self.q["act"]:
                    f(e)

            @block.gpsimd
            def _(e):
                for f in self.q["pool"]:
                    f(e)

            @block.sync
            def _(e):
                for f in self.q["sp"]:
                    f(e)


def _host_consts():
    c = {}
    c["ident"] = np.eye(128, dtype=np.float32)
    c["ones"] = np.ones((128, 128), np.float32)
    bo = np.zeros((128, 128), np.float32)
    bo[:64, :64] = 1
    bo[64:, 64:] = 1
    c["bones"] = bo
    pm = np.zeros((128, 128), np.float32)
    for m in range(128):
        b, i = divmod(m, 64)
        if i < 32:
            pm[b * 64 + i + 32, m] = -1.0
        else:
            pm[b * 64 + i - 32, m] = 1.0
    c["perm"] = pm
    c["stackI"] = np.concatenate([np.eye(64, dtype=np.float32)] * 2, axis=0)
    s = np.arange(64)[:, None]
    t = np.arange(64)[None, :]

    def bd(m):
        z = np.zeros((128, 128), np.float32)
        z[:64, :64] = m
        z[64:, 64:] = m
        return z
    c["m_sl"] = bd((s < t).astype(np.float32))
    c["m_sl_T"] = bd((s > t).astype(np.float32))
    c["m_il"] = bd((s <= t).astype(np.float32))
    c["m_il_T"] = bd((s >= t).astype(np.float32))
    tt = np.arange(512)
    c["rmask_f"] = np.tile(((tt % 64) != 0).astype(np.float32)[None, :], (128, 1))
    c["rmask_b"] = np.tile(((tt % 64) != 63).astype(np.float32)[None, :], (128, 1))
    pos = np.arange(NLAT)
    row = (pos // GRID_W).astype(np.float32)
    col = (pos % GRID_W).astype(np.float32)

    def tables(dim, reps):
        quarter = dim // 4
        freqs = (np.float32(10000.0) ** (-np.arange(quarter, dtype=np.float32) / np.float32(quarter))).astype(np.float32)
        ang = np.concatenate([row[:, None] * freqs, col[:, None] * freqs], axis=-1).astype(np.float32)
        cs, sn = np.cos(ang).astype(np.float32), np.sin(ang).astype(np.float32)
        cs = np.concatenate([cs, cs], axis=1).T
        sn = np.concatenate([sn, sn], axis=1).T
        cs = np.concatenate([np.ones((dim, NCTX), np.float32), cs], axis=1)
        sn = np.concatenate([np.zeros((dim, NCTX), np.float32), sn], axis=1)
        return np.tile(cs, (reps, 1)), np.tile(sn, (reps, 1))
    c["cosA"], c["sinA"] = tables(32, 4)
    c["cosB"], c["sinB"] = tables(64, 2)
    return {k: np.ascontiguousarray(v, dtype=np.float32) for k, v in c.items()}


def _fm(v, nch):
    return np.asarray(v, np.float32).reshape(nch, 128).T


def _host_vecs(inp):
    vf = np.zeros((128, DEPTH * NF), np.float32)
    vr = np.zeros((1, DEPTH * NR), np.float32)
    for l in range(DEPTH):
        b = l * NF
        vf[:, b + F_PRE:b + F_PRE + 8] = _fm(inp["attn_pre_g"][l], 8)
        vf[:, b + F_FPRE:b + F_FPRE + 8] = _fm(inp["ffn_pre_g"][l], 8)
        vf[:, b + F_QN:b + F_QN + 3] = _fm(inp["mla_q_norm"][l], 3)
        vf[:, b + F_KVN:b + F_KVN + 2] = _fm(inp["mla_kv_norm"][l], 2)
        vf[:, b + F_GQ] = np.tile(inp["gqa_q_norm"][l], 2)
        vf[:, b + F_GK] = np.tile(inp["gqa_k_norm"][l], 2)
        for j in range(3):
            vf[:, b + F_CONV + j * 12:b + F_CONV + j * 12 + 12] = _fm(inp["rwkv_conv"][l, j], 12)
        for d in range(2):
            vf[:, b + F_W0 + d * 4:b + F_W0 + d * 4 + 4] = _fm(inp["rwkv_w0"][l, d], 4)
            vf[:, b + F_A0 + d * 4:b + F_A0 + d * 4 + 4] = _fm(inp["rwkv_a0"][l, d], 4)
        vf[:, b + F_KK:b + F_KK + 4] = _fm(inp["rwkv_k_k"][l], 4)
        vf[:, b + F_KA:b + F_KA + 4] = _fm(inp["rwkv_k_a"][l], 4)
        vf[:, b + F_RK:b + F_RK + 4] = _fm(inp["rwkv_r_k"][l].reshape(-1), 4)
        vf[:, b + F_LNG:b + F_LNG + 4] = _fm(inp["rwkv_ln_g"][l], 4)
        vf[:, b + F_LNB:b + F_LNB + 4] = _fm(inp["rwkv_ln_b"][l], 4)
        rb = l * NR
        vr[0, rb + R_ADAB:rb + R_ADAB + 6144] = inp["ada_b"][l]
        vr[0, rb + R_POST:rb + R_POST + 1024] = inp["attn_post_g"][l]
        vr[0, rb + R_FPOST:rb + R_FPOST + 1024] = inp["ffn_post_g"][l]
    return vf, vr


WEIGHTS = {
    "ada_w": [DEPTH, D, 6 * D], "w_in": [DEPTH, D, IN_W], "mla_w_uq": [DEPTH, 384, 768],
    "mla_w_ukv": [DEPTH, 256, 1024], "rwkv_w2": [DEPTH, 2, 64, 512], "rwkv_a2": [DEPTH, 2, 64, 512],
    "rwkv_g2": [DEPTH, 128, 512], "w_branch": [DEPTH, 3, 512, D], "w_out": [DEPTH, D, D],
    "ffn_w_in": [DEPTH, D, 2 * FFH], "ffn_w_out": [DEPTH, FFH, D],
}
CONST_SHAPES = {
    "ident": [128, 128], "ones": [128, 128], "bones": [128, 128], "perm": [128, 128], "stackI": [128, 64],
    "m_sl": [128, 128], "m_sl_T": [128, 128], "m_il": [128, 128], "m_il_T": [128, 128],
    "rmask_f": [128, 512], "rmask_b": [128, 512],
    "cosA": [128, T], "sinA": [128, T], "cosB": [128, T], "sinB": [128, T],
}


def build_program(upto="all", debug=False, nlayers=DEPTH, skip=()):
    nc = bass.Bass("TRN2", target_bir_lowering=False)
    kindS = "ExternalOutput" if debug else "Internal"

    def din(name, shape, dt=F32):
        return nc.dram_tensor(name, list(shape), dt, kind="ExternalInput").ap()

    def dsc(name, shape, dt=F32):
        return nc.dram_tensor(name, list(shape), dt, kind=kindS).ap()

    xin = din("xin", [T, D])
    csil = din("csil", [128, 16])
    vecF_d = din("vecF", [128, DEPTH * NF])
    vecR_d = din("vecR", [1, DEPTH * NR])
    Wd = {k: din(k, s) for k, s in WEIGHTS.items()}
    Cd = {k: din(k, s) for k, s in CONST_SHAPES.items()}
    out_d = nc.dram_tensor("out", [NLAT, D], F32, kind="ExternalOutput").ap()

    xres = dsc("xres", [T, D])
    cqT = dsc("cqT", [384, T], BF16)
    ckvT = dsc("ckvT", [256, T], BF16)
    krT = dsc("krT", [64, T], BF16)
    gqT = dsc("gqT", [512, T], BF16)
    gkT = dsc("gkT", [128, T], BF16)
    gvS = dsc("gvS", [T, 128], BF16)
    rkvT = dsc("rkvT", [1536, T], F32)
    loT = dsc("loT", [384, T], F32)
    gateT = dsc("gateT", [3072, T], BF16)
    QaT = dsc("QaT", [8 * 96, T], BF16)
    KaT = dsc("KaT", [8 * 96, T], BF16)
    VaS = dsc("VaS", [T, 512], BF16)
    QbT = dsc("QbT", [512, T], BF16)
    KbT = dsc("KbT", [128, T], BF16)
    oaT = dsc("oaT", [512, T], BF16)
    obT = dsc("obT", [512, T], BF16)
    ocT = dsc("ocT", [512, T], BF16)
    rwA = dsc("rwA", [2, 512, T], RW_DT); rwB = dsc("rwB", [2, 512, T], RW_DT); rwK = dsc("rwK", [2, 512, T], RW_DT)
    rwR = dsc("rwR", [2, 512, T], RW_DT); rwE = dsc("rwE", [2, 512, T])
    rwV = dsc("rwV", [512, T], RW_DT); rwBon = dsc("rwBon", [512, T])
    rwY = dsc("rwY", [2, T, 512])

    with ExitStack() as top:
        import os as _os
        P = Prog(nc, top, same_engine_sync=bool(int(_os.environ.get('SES', '0'))))

        uid = {"n": 0}

        def sbt(st, name, shape, dt=F32):
            uid["n"] += 1
            return st.enter_context(nc.sbuf_tensor(f"{name}_s{uid['n']}", list(shape), dt))

        def pst(st, name, shape, dt=F32):
            uid["n"] += 1
            return st.enter_context(nc.psum_tensor(f"{name}_p{uid['n']}", list(shape), dt))

        ident_bf = sbt(top, "ident_bf", [128, 128], BF16)
        ones_bf = sbt(top, "ones_bf", [128, 128], BF16)
        bones_bf = sbt(top, "bones_bf", [128, 128], BF16)
        perm_bf = sbt(top, "perm_bf", [128, 128], BF16)
        ones_f = sbt(top, "ones_f", [128, 128])
        bones_f = sbt(top, "bones_f", [128, 128])
        ident_f = sbt(top, "ident_f", [128, 128])
        stackI = sbt(top, "stackI", [128, 64])
        m_sl = sbt(top, "m_sl", [128, 128]); m_slT = sbt(top, "m_slT", [128, 128]); m_il = sbt(top, "m_il", [128, 128])
        vecF = sbt(top, "vecF_sb", [128, DEPTH * NF])
        colF = sbt(top, "colF", [128, 64])
        GB = [[sbt(top, f"GB{j}{i}", [128, D]) for i in range(2)] for j in range(2)]
        for name, tl in (("ident", ident_bf), ("ones", ones_bf), ("bones", bones_bf), ("perm", perm_bf)):
            P.dma("pool", tl[:], Cd[name], w=[tl.name])
        for name, tl in (("ones", ones_f), ("bones", bones_f), ("ident", ident_f), ("stackI", stackI), ("m_sl", m_sl),
                         ("m_sl_T", m_slT), ("m_il", m_il)):
            P.dma("sp", tl[:], Cd[name], w=[tl.name])
        P.dma("sp", vecF[:], vecF_d, w=["vecF"])
        P.barrier()

        rr = {"ev": 0}

        def evac(out, in_, r, w, func=None, scale=None, bias=None, eng=None):
            if func is None and scale is None and bias is None:
                if eng is None:
                    eng = "act" if rr["ev"] % 2 else "dve"
                    rr["ev"] += 1
                if eng == "act":
                    P.op("act", lambda e: e.copy(out, in_), r=r, w=w)
                else:
                    P.op(eng, lambda e: e.tensor_copy(out, in_), r=r, w=w)
            else:
                kw = {}
                if scale is not None:
                    kw["scale"] = scale
                if bias is not None:
                    kw["bias"] = bias
                f = func if func is not None else AF.Identity
                P.op("act", lambda e: e.activation(out, in_, f, **kw), r=r, w=w)

        def vF(l, off, n=1):
            return vecF[:, l * NF + off:l * NF + off + n]

        def phase_mod(l):
            with ExitStack() as ph:
                adaw = sbt(ph, "adaw", [128, 8, 3072], BF16)
                vecR = sbt(ph, "vecR", [1, NR])
                P.dma("sp", vecR[:], vecR_d[0:1, l * NR:(l + 1) * NR], w=["vecR"])
                cs = sbt(ph, "cs", [128, 16]); csb = sbt(ph, "csb", [128, 16], BF16)
                modrow = [sbt(ph, f"modrow{j}", [1, 6144]) for j in range(2)]
                rowtmp = sbt(ph, "rowtmp", [1, D])
                colraw = sbt(ph, "colraw", [128, 64])
                pss = [pst(ph, f"mps{i}", [128, 512]) for i in range(4)]
                P.dma("sp", cs[:], csil, w=["cs"])
                P.op("act", lambda e: e.activation(csb[:], cs[:], AF.Silu), r=["cs"], w=["csb"])
                n = 0
                for hf in range(2):
                    for k in range(8):
                        P.dma("pool", adaw[:, k, :], Wd["ada_w"][l, k * 128:(k + 1) * 128, hf * 3072:(hf + 1) * 3072], w=[f"adaw{k}"])
                    for j in range(2):
                        for gg in range(6):
                            g = hf * 6 + gg
                            ps = pss[n % 4]; pk = f"mps{n % 4}"; n += 1
                            for k in range(8):
                                P.op("pe", lambda e, ps=ps, k=k, j=j, gg=gg: e.matmul(
                                    ps[0:1, :], csb[:, k * 2 + j:k * 2 + j + 1], adaw[:, k, gg * 512:(gg + 1) * 512],
                                    start=(k == 0), stop=(k == 7)), r=["csb", f"adaw{k}"], w=[pk])
                            P.op("dve", lambda e, ps=ps, j=j, g=g: e.tensor_tensor(
                                modrow[j][0:1, g * 512:(g + 1) * 512], ps[0:1, :],
                                vecR[0:1, R_ADAB + g * 512:R_ADAB + (g + 1) * 512], ALU.add),
                                r=[pk, "vecR"], w=[f"modrow{j}"])
                psc = pss[0]
                segs = (0, 1, 3, 4)
                for j in range(2):
                    for si, seg in enumerate(segs):
                        for k in range(8):
                            cidx = (j * 4 + si) * 8 + k
                            P.op("pe", lambda e, j=j, seg=seg, k=k, cidx=cidx: e.matmul(
                                psc[:, cidx:cidx + 1], modrow[j][0:1, seg * 1024 + k * 128:seg * 1024 + (k + 1) * 128],
                                ones_f[0:1, 0:1], start=True, stop=True), r=[f"modrow{j}"], w=["mps0"])
                P.op("dve", lambda e: e.tensor_copy(colraw[:], psc[:, 0:64]), r=["mps0"], w=["colraw"])
                for j in range(2):
                    b = j * 32
                    P.op("dve", lambda e, b=b: e.scalar_tensor_tensor(
                        colF[:, b:b + 8], colraw[:, b + 8:b + 16], 1.0, vF(l, F_PRE, 8), ALU.add, ALU.mult),
                        r=["colraw", "vecF"], w=["colF"])
                    P.op("dve", lambda e, b=b: e.tensor_copy(colF[:, b + 8:b + 16], colraw[:, b:b + 8]), r=["colraw"], w=["colF"])
                    P.op("dve", lambda e, b=b: e.scalar_tensor_tensor(
                        colF[:, b + 16:b + 24], colraw[:, b + 24:b + 32], 1.0, vF(l, F_FPRE, 8), ALU.add, ALU.mult),
                        r=["colraw", "vecF"], w=["colF"])
                    P.op("dve", lambda e, b=b: e.tensor_copy(colF[:, b + 24:b + 32], colraw[:, b + 16:b + 24]), r=["colraw"], w=["colF"])
                n = 1
                for j in range(2):
                    for i, (seg, roff) in enumerate(((2, R_POST), (5, R_FPOST))):
                        P.op("dve", lambda e, j=j, seg=seg, roff=roff: e.tensor_tensor(
                            rowtmp[0:1, :], modrow[j][0:1, seg * 1024:(seg + 1) * 1024],
                            vecR[0:1, roff:roff + 1024], ALU.mult),
                            r=[f"modrow{j}", "vecR"], w=["rowtmp"])
                        for hh in range(2):
                            ps = pss[n % 4]; pk = f"mps{n % 4}"; n += 1
                            P.op("pe", lambda e, ps=ps, hh=hh: e.matmul(
                                ps[:, :], ones_f[0:1, :], rowtmp[0:1, hh * 512:(hh + 1) * 512], start=True, stop=True),
                                r=["rowtmp"], w=[pk])
                            evac(GB[j][i][:, hh * 512:(hh + 1) * 512], ps[:, :], r=[pk], w=[f"GB{j}{i}"])
                P.barrier()

        def norm_modulate_T(ph, l, src, t0, n, kind, hT, tag):
            j = 1 if t0 < NCTX else 0
            cb = (j * 4 + kind * 2) * 8
            nst = n // 128
            for s in range(nst):
                nb_ = len(ph["xt"])
                xt = ph["xt"][s % nb_]; xk = f"xt{s % nb_}"
                P.dma("sp", xt[:], src[t0 + s * 128:t0 + (s + 1) * 128, :], r=[(tag, "x", t0 + s * 128)], w=[xk])
                junk = ph["junk"]; ss = ph["ss"][s % 2]; sk = f"ss{s % 2}"
                P.op("act", lambda e, xt=xt, ss=ss: e.activation(junk[:], xt[:], AF.Square, accum_out=ss[:, 0:1]),
                     r=[xk], w=["junk", sk])
                P.op("act", lambda e, ss=ss: e.activation(ss[:, 1:2], ss[:, 0:1], AF.Ln, scale=1.0 / D, bias=ph["epsc"][:, 0:1]),
                     r=[sk, "epsc"], w=[sk + "b"], ss=True)
                P.op("act", lambda e, ss=ss: e.activation(ss[:, 2:3], ss[:, 1:2], AF.Exp, scale=-0.5), r=[sk + "b"], w=[sk + "c"], ss=True)
                xn = ph["xn"][s % nb_]; nk = f"xn{s % nb_}"
                P.op("dve", lambda e, xt=xt, xn=xn, ss=ss: e.tensor_scalar(xn[:], xt[:], ss[:, 2:3], None, ALU.mult),
                     r=[xk, sk + "c"], w=[nk])
                for half in range(2):
                    tp = ph["tp"][half]; tk = f"tp{half}"
                    for kk in range(4):
                        k = half * 4 + kk
                        P.op("pe", lambda e, tp=tp, kk=kk, k=k, xn=xn: e.transpose(
                            tp[:, kk * 128:(kk + 1) * 128], xn[:, k * 128:(k + 1) * 128], ident_bf[:]), r=[nk], w=[tk])
                    for kk in range(4):
                        k = half * 4 + kk
                        o = hT[:, k, s * 128:(s + 1) * 128]
                        i_ = tp[:, kk * 128:(kk + 1) * 128]
                        if kk % 2 == 0:
                            P.op("act", lambda e, o=o, i_=i_, k=k: e.activation(
                                o, i_, AF.Identity, scale=colF[:, cb + k:cb + k + 1], bias=colF[:, cb + 8 + k:cb + 9 + k]),
                                r=[tk, "colF"], w=[(tag, "hT")])
                        else:
                            P.op("dve", lambda e, o=o, i_=i_, k=k: e.tensor_scalar(
                                o, i_, colF[:, cb + k:cb + k + 1], colF[:, cb + 8 + k:cb + 9 + k], ALU.mult, ALU.add),
                                r=[tk, "colF"], w=[(tag, "hT")])

        def norm_tiles(ph, nbuf=2):
            d = {}
            d["xt"] = [sbt(ph, f"xt{i}", [128, D]) for i in range(nbuf)]
            d["xn"] = [sbt(ph, f"xn{i}", [128, D], BF16) for i in range(nbuf)]
            d["junk"] = sbt(ph, "junk", [128, D], BF16)
            d["ss"] = [sbt(ph, f"ss{i}", [128, 4]) for i in range(2)]
            d["tp"] = [pst(ph, f"tp{i}", [128, 1024], BF16) for i in range(2)]
            d["epsc"] = sbt(ph, "epsc", [128, 1])
            P.op("dve", lambda e: e.memset(d["epsc"][:], EPS), w=["epsc"])
            return d

        def phase_inproj(l):
            src = xin if l == 0 else xres
            with ExitStack() as ph:
                win = sbt(ph, "win", [128, 8, IN_W + 32], BF16)
                nt = norm_tiles(ph)
                hT = [sbt(ph, f"hT{i}", [128, 8, 512], BF16) for i in range(2)]
                stg_b = [sbt(ph, f"stgb{i}", [128, 512], BF16) for i in range(3)]
                stg_f = [sbt(ph, f"stgf{i}", [128, 512]) for i in range(3)]
                pss = [pst(ph, f"ips{i}", [128, 512]) for i in range(4)]
                for k in range(8):
                    P.dma("pool", win[:, k, 0:IN_W], Wd["w_in"][l, k * 128:(k + 1) * 128, :], w=[f"win{k}"])
                for k in range(8):
                    P.op("dve", lambda e, k=k: e.tensor_scalar(
                        win[:, k, IN_W:IN_W + 16], win[:, k, O_KR + 16:O_KR + 32], -1.0, None, ALU.mult), r=[f"win{k}"], w=[f"winR{k}"])
                    P.op("dve", lambda e, k=k: e.tensor_copy(
                        win[:, k, IN_W + 16:IN_W + 32], win[:, k, O_KR:O_KR + 16]), r=[f"win{k}"], w=[f"winR{k}"])
                wkeys = [f"win{k}" for k in range(8)] + [f"winR{k}" for k in range(8)]
                chunks = []
                for c in range(3):
                    chunks.append((O_CQ + c * 128, 128, cqT, c * 128, BF16, None))
                for c in range(2):
                    chunks.append((O_CKV + c * 128, 128, ckvT, c * 128, BF16, None))
                chunks.append((O_KR, 32, krT, 0, BF16, None))
                chunks.append((IN_W, 32, krT, 32, BF16, None))
                for c in range(4):
                    chunks.append((O_GQ + c * 128, 128, gqT, c * 128, BF16, None))
                chunks.append((O_GK, 128, gkT, 0, BF16, None))
                for c in range(12):
                    chunks.append((O_RKV + c * 128, 128, rkvT, c * 128, F32, None))
                chunks.append((O_ALO, 128, loT, 128, F32, None))
                chunks.append((O_WLO, 128, loT, 0, F32, AF.Tanh))
                chunks.append((O_GLO, 128, loT, 256, F32, AF.Sigmoid))
                for c in range(24):
                    chunks.append((O_GATE + c * 128, 128, gateT, c * 128, BF16, AF.Sigmoid))
                n = 0
                nsb = 0
                nsf = 0
                for bi, (t0, nn) in enumerate(SPANS):
                    h = hT[bi % 2]; hk = ("in", "hT", bi % 2)
                    norm_modulate_T(nt, l, src, t0, nn, 0, h, ("in", bi % 2))
                    hkey = (("in", bi % 2), "hT")
                    for (c0, m, dst, r0, dt, func) in chunks:
                        ps = pss[n % 4]; pk = f"ips{n % 4}"; n += 1
                        for k in range(8):
                            P.op("pe", lambda e, ps=ps, k=k, c0=c0, m=m, h=h, nn=nn: e.matmul(
                                ps[0:m, 0:nn], win[:, k, c0:c0 + m], h[:, k, 0:nn], start=(k == 0), stop=(k == 7)),
                                r=[hkey] + wkeys, w=[pk])
                        if dt == BF16:
                            sg = stg_b[nsb % 3]; sk = f"stgb{nsb % 3}"; nsb += 1
                        else:
                            sg = stg_f[nsf % 3]; sk = f"stgf{nsf % 3}"; nsf += 1
                        evac(sg[0:m, 0:nn], ps[0:m, 0:nn], r=[pk], w=[sk], func=func)
                        P.dma("sp", dst[r0:r0 + m, t0:t0 + nn], sg[0:m, 0:nn], r=[sk], w=[(dst.tensor.name, t0)])
                    for s in range(nn // 128):
                        ps = pss[n % 4]; pk = f"ips{n % 4}"; n += 1
                        for k in range(8):
                            P.op("pe", lambda e, ps=ps, k=k, h=h, s=s: e.matmul(
                                ps[:, 0:128], h[:, k, s * 128:(s + 1) * 128], win[:, k, O_GV:O_GV + 128],
                                start=(k == 0), stop=(k == 7)), r=[hkey] + wkeys, w=[pk])
                        sg = stg_b[nsb % 3]; sk = f"stgb{nsb % 3}"; nsb += 1
                        evac(sg[:, 0:128], ps[:, 0:128], r=[pk], w=[sk])
                        P.dma("sp", gvS[t0 + s * 128:t0 + (s + 1) * 128, :], sg[:, 0:128], r=[sk], w=[("gvS", t0)])
                P.barrier()

        def rms_feat(ph, x, nk, nn, ones_t, inv_n, gcol0, l, xn_out, tag):
            sq = ph["sq"]; pss = ph["pss"]; lnv = ph["lnv"]; rstd = ph["rstd"]
            P.op("dve", lambda e: e.tensor_tensor(sq[:, 0:nk, 0:nn], x[:, 0:nk, 0:nn], x[:, 0:nk, 0:nn], ALU.mult),
                 r=[tag + "x"], w=["sq"])
            for k in range(nk):
                P.op("pe", lambda e, k=k: e.matmul(pss[:, 0:nn], ones_t[:], sq[:, k, 0:nn], start=(k == 0), stop=(k == nk - 1)),
                     r=["sq"], w=["pss"])
            P.op("act", lambda e: e.activation(lnv[:, 0:nn], pss[:, 0:nn], AF.Ln, scale=inv_n, bias=ph["epsc"][:, 0:1]),
                 r=["pss", "epsc"], w=["lnv"], ss=True)
            P.op("act", lambda e: e.activation(rstd[:, 0:nn], lnv[:, 0:nn], AF.Exp, scale=-0.5), r=["lnv"], w=["rstd"], ss=True)
            for k in range(nk):
                P.op("dve", lambda e, k=k: e.scalar_tensor_tensor(
                    xn_out[:, k, 0:nn], x[:, k, 0:nn], vF(l, gcol0 + k), rstd[:, 0:nn], ALU.mult, ALU.mult),
                    r=[tag + "x", "rstd", "vecF"], w=[tag + "xn"])

        def phase_mla(l):
            with ExitStack() as ph:
                wq_n = sbt(ph, "wq_n", [128, 3, 512], BF16); wq_r = sbt(ph, "wq_r", [128, 3, 256], BF16)
                wq_rR = sbt(ph, "wq_rR", [128, 3, 256], BF16)
                wk_n = sbt(ph, "wk_n", [128, 2, 512], BF16); wk_v = sbt(ph, "wk_v", [128, 2, 512], BF16)
                t = {}
                t["sq"] = sbt(ph, "sq", [128, 3, 512], BF16); t["lnv"] = sbt(ph, "lnv", [128, 512]); t["rstd"] = sbt(ph, "rstd", [128, 512])
                t["pss"] = pst(ph, "pss", [128, 512]); t["epsc"] = sbt(ph, "epsc", [128, 1])
                P.op("dve", lambda e: e.memset(t["epsc"][:], EPS), w=["epsc"])
                cq = sbt(ph, "cq", [128, 3, 512], BF16); cqn = sbt(ph, "cqn", [128, 3, 512], BF16)
                ckv = sbt(ph, "ckv", [128, 2, 512], BF16); ckvn = sbt(ph, "ckvn", [128, 2, 512], BF16)
                krA = sbt(ph, "krA", [32, 512], BF16); krB = sbt(ph, "krB", [32, 512], BF16)
                cA = sbt(ph, "cA", [128, 512]); sA = sbt(ph, "sA", [128, 512])
                t1 = sbt(ph, "t1", [128, 512]); t2 = sbt(ph, "t2", [128, 512])
                stg = [sbt(ph, f"stg{i}", [128, 512], BF16) for i in range(4)]
                pq = [pst(ph, f"pq{i}", [128, 512]) for i in range(4)]
                for kc in range(3):
                    src = Wd["mla_w_uq"][l, kc * 128:(kc + 1) * 128, :].rearrange("p (h d) -> p h d", d=96)
                    P.dma("pool", wq_n[:, kc, :].rearrange("p (h d) -> p h d", d=64), src[:, :, 0:64], w=["wq_n"])
                    P.dma("pool", wq_r[:, kc, :].rearrange("p (h d) -> p h d", d=32), src[:, :, 64:96], w=["wq_r"])
                    rv = wq_r[:, kc, :].rearrange("p (h d) -> p h d", d=32)
                    rRv = wq_rR[:, kc, :].rearrange("p (h d) -> p h d", d=32)
                    P.op("dve", lambda e, rv=rv, rRv=rRv: e.tensor_scalar(rRv[:, :, 0:16], rv[:, :, 16:32], -1.0, None, ALU.mult),
                         r=["wq_r"], w=["wq_rR"])
                    P.op("dve", lambda e, rv=rv, rRv=rRv: e.tensor_copy(rRv[:, :, 16:32], rv[:, :, 0:16]), r=["wq_r"], w=["wq_rR"])
                for kc in range(2):
                    src = Wd["mla_w_ukv"][l, kc * 128:(kc + 1) * 128, :].rearrange("p (h d) -> p h d", d=128)
                    P.dma("pool", wk_n[:, kc, :].rearrange("p (h d) -> p h d", d=64), src[:, :, 0:64], w=["wk_n"])
                    P.dma("pool", wk_v[:, kc, :].rearrange("p (h d) -> p h d", d=64), src[:, :, 64:128], w=["wk_v"])
                ns = 0; npq = 0
                cqT_v = cqT.rearrange("(k p) t -> p k t", p=128)
                ckvT_v = ckvT.rearrange("(k p) t -> p k t", p=128)
                for bi, (t0, nn) in enumerate(SPANS):
                    P.dma("sp", cq[:, :, 0:nn], cqT_v[:, :, t0:t0 + nn], w=["qx"])
                    P.dma("sp", ckv[:, :, 0:nn], ckvT_v[:, :, t0:t0 + nn], w=["kx"])
                    P.dma("sp", krA[:, 0:nn], krT[0:32, t0:t0 + nn], w=["krA"])
                    P.dma("sp", krB[:, 0:nn], krT[32:64, t0:t0 + nn], w=["krB"])
                    P.dma("sp", cA[:, 0:nn], Cd["cosA"][:, t0:t0 + nn], w=["cA"])
                    P.dma("sp", sA[:, 0:nn], Cd["sinA"][:, t0:t0 + nn], w=["sA"])
                    rms_feat(t, cq, 3, nn, ones_bf, 1.0 / 384, F_QN, l, cqn, "q")
                    need_q = not (l == DEPTH - 1 and t0 < NCTX)
                    if need_q:
                        for g in range(4):
                            ps = pq[npq % 4]; pk = f"pq{npq % 4}"; npq += 1
                            for k in range(3):
                                P.op("pe", lambda e, ps=ps, k=k, g=g, nn=nn: e.matmul(ps[:, 0:nn], wq_n[:, k, g * 128:(g + 1) * 128], cqn[:, k, 0:nn],
                                                                                start=(k == 0), stop=(k == 2)), r=["qxn", "wq_n"], w=[pk])
                            sg = stg[ns % 4]; sk = f"stg{ns % 4}"; ns += 1
                            evac(sg[:, 0:nn], ps[:, 0:nn], r=[pk], w=[sk])
                            for hh in range(2):
                                h = 2 * g + hh
                                P.dma("sp", QaT[h * 96:h * 96 + 64, t0:t0 + nn], sg[hh * 64:(hh + 1) * 64, 0:nn], r=[sk], w=[("QaT", t0, h)])
                        for g2 in range(2):
                            p1 = pq[npq % 4]; k1 = f"pq{npq % 4}"; npq += 1
                            p2 = pq[npq % 4]; k2 = f"pq{npq % 4}"; npq += 1
                            for k in range(3):
                                P.op("pe", lambda e, k=k, p1=p1, g2=g2, nn=nn: e.matmul(p1[:, 0:nn], wq_r[:, k, g2 * 128:(g2 + 1) * 128], cqn[:, k, 0:nn],
                                                                           start=(k == 0), stop=(k == 2)), r=["qxn", "wq_r"], w=[k1])
                            for k in range(3):
                                P.op("pe", lambda e, k=k, p2=p2, g2=g2, nn=nn: e.matmul(p2[:, 0:nn], wq_rR[:, k, g2 * 128:(g2 + 1) * 128], cqn[:, k, 0:nn],
                                                                           start=(k == 0), stop=(k == 2)), r=["qxn", "wq_rR"], w=[k2])
                            P.op("dve", lambda e, p1=p1, nn=nn: e.tensor_tensor(t1[:, 0:nn], p1[:, 0:nn], cA[:, 0:nn], ALU.mult), r=[k1, "cA"], w=["t1"])
                            P.op("dve", lambda e, p2=p2, nn=nn: e.tensor_tensor(t2[:, 0:nn], p2[:, 0:nn], sA[:, 0:nn], ALU.mult), r=[k2, "sA"], w=["t2"])
                            sg = stg[ns % 4]; sk = f"stg{ns % 4}"; ns += 1
                            P.op("pool", lambda e, sg=sg, nn=nn: e.tensor_tensor(sg[:, 0:nn], t1[:, 0:nn], t2[:, 0:nn], ALU.add), r=["t1", "t2"], w=[sk])
                            for hh in range(4):
                                h = 4 * g2 + hh
                                P.dma("sp", QaT[h * 96 + 64:h * 96 + 96, t0:t0 + nn], sg[hh * 32:(hh + 1) * 32, 0:nn], r=[sk], w=[("QaT", t0, h, "r")])
                    rms_feat(t, ckv, 2, nn, ones_bf, 1.0 / 256, F_KVN, l, ckvn, "k")
                    for g in range(4):
                        ps = pq[npq % 4]; pk = f"pq{npq % 4}"; npq += 1
                        for k in range(2):
                            P.op("pe", lambda e, ps=ps, k=k, g=g, nn=nn: e.matmul(ps[:, 0:nn], wk_n[:, k, g * 128:(g + 1) * 128], ckvn[:, k, 0:nn],
                                                                            start=(k == 0), stop=(k == 1)), r=["kxn", "wk_n"], w=[pk])
                        sg = stg[ns % 4]; sk = f"stg{ns % 4}"; ns += 1
                        evac(sg[:, 0:nn], ps[:, 0:nn], r=[pk], w=[sk])
                        for hh in range(2):
                            h = 2 * g + hh
                            P.dma("sp", KaT[h * 96:h * 96 + 64, t0:t0 + nn], sg[hh * 64:(hh + 1) * 64, 0:nn], r=[sk], w=[("KaT", t0, h)])
                    for s in range(nn // 128):
                        ps = pq[npq % 4]; pk = f"pq{npq % 4}"; npq += 1
                        for k in range(2):
                            P.op("pe", lambda e, ps=ps, k=k, s=s: e.matmul(ps[:, :], ckvn[:, k, s * 128:(s + 1) * 128], wk_v[:, k, :],
                                                                            start=(k == 0), stop=(k == 1)), r=["kxn", "wk_v"], w=[pk])
                        sg = stg[ns % 4]; sk = f"stg{ns % 4}"; ns += 1
                        evac(sg[:, :], ps[:, :], r=[pk], w=[sk])
                        P.dma("sp", VaS[t0 + s * 128:t0 + (s + 1) * 128, :], sg[:, :], r=[sk], w=[("VaS", t0, s)])
                    P.op("dve", lambda e, nn=nn: e.tensor_tensor(t1[0:32, 0:nn], krA[:, 0:nn], cA[0:32, 0:nn], ALU.mult), r=["krA", "cA"], w=["t1"])
                    P.op("dve", lambda e, nn=nn: e.tensor_tensor(t2[0:32, 0:nn], krB[:, 0:nn], sA[0:32, 0:nn], ALU.mult), r=["krB", "sA"], w=["t2"])
                    sg = stg[ns % 4]; sk = f"stg{ns % 4}"; ns += 1
                    P.op("pool", lambda e, sg=sg, nn=nn: e.tensor_tensor(sg[0:32, 0:nn], t1[0:32, 0:nn], t2[0:32, 0:nn], ALU.add), r=["t1", "t2"], w=[sk])
                    for h in range(8):
                        P.dma("sp", KaT[h * 96 + 64:h * 96 + 96, t0:t0 + nn], sg[0:32, 0:nn], r=[sk], w=[("KaT", t0, h, "r")])
                P.barrier()

        def phase_gqa(l):
            with ExitStack() as ph:
                t = {}
                t["sq"] = sbt(ph, "sq", [128, 1, 512], BF16); t["lnv"] = sbt(ph, "lnv", [128, 512]); t["rstd"] = sbt(ph, "rstd", [128, 512])
                t["pss"] = pst(ph, "pss", [128, 512]); t["epsc"] = sbt(ph, "epsc", [128, 1])
                P.op("dve", lambda e: e.memset(t["epsc"][:], EPS), w=["epsc"])
                x = [sbt(ph, f"gx{i}", [128, 1, 512], BF16) for i in range(2)]
                xn = [sbt(ph, f"gxn{i}", [128, 1, 512], BF16) for i in range(2)]
                cB = sbt(ph, "cB", [128, 512]); sB = sbt(ph, "sB", [128, 512])
                t1 = sbt(ph, "t1", [128, 512]); t2 = sbt(ph, "t2", [128, 512])
                stg = [sbt(ph, f"stg{i}", [128, 512], BF16) for i in range(2)]
                pr = [pst(ph, f"pr{i}", [128, 512]) for i in range(2)]
                n = 0
                for bi, (t0, nn) in enumerate(SPANS):
                    P.dma("sp", cB[:, 0:nn], Cd["cosB"][:, t0:t0 + nn], w=["cB"])
                    P.dma("sp", sB[:, 0:nn], Cd["sinB"][:, t0:t0 + nn], w=["sB"])
                    for c in range(5):
                        if c < 4 and l == DEPTH - 1 and t0 < NCTX:
                            continue
                        src = gqT[c * 128:(c + 1) * 128, t0:t0 + nn] if c < 4 else gkT[:, t0:t0 + nn]
                        dst = QbT[c * 128:(c + 1) * 128, t0:t0 + nn] if c < 4 else KbT[:, t0:t0 + nn]
                        i = n % 2; n += 1
                        P.dma("sp", x[i][:, 0, 0:nn], src, w=[f"g{i}x"])
                        rms_feat(t, x[i], 1, nn, bones_bf, 1.0 / 64, F_GQ if c < 4 else F_GK, l, xn[i], f"g{i}")
                        P.op("pe", lambda e, i=i, nn=nn: e.matmul(pr[i][:, 0:nn], perm_bf[:], xn[i][:, 0, 0:nn], start=True, stop=True),
                             r=[f"g{i}xn"], w=[f"pr{i}"])
                        P.op("pool", lambda e, i=i, nn=nn: e.tensor_tensor(t1[:, 0:nn], xn[i][:, 0, 0:nn], cB[:, 0:nn], ALU.mult), r=[f"g{i}xn", "cB"], w=["t1"])
                        P.op("dve", lambda e, i=i, nn=nn: e.tensor_tensor(t2[:, 0:nn], pr[i][:, 0:nn], sB[:, 0:nn], ALU.mult), r=[f"pr{i}", "sB"], w=["t2"])
                        P.op("pool", lambda e, i=i, nn=nn: e.tensor_tensor(stg[i][:, 0:nn], t1[:, 0:nn], t2[:, 0:nn], ALU.add), r=["t1", "t2"], w=[f"stg{i}"])
                        P.dma("sp", dst, stg[i][:, 0:nn], r=[f"stg{i}"], w=[("Qb", t0, c)])
                P.barrier()

        def phase_attn(l, kind):
            if kind == "a":
                QT, KT, VS, d, nkv, G, scale, OT = QaT, KaT, VaS, 96, 8, 1, MLA_SCALE, oaT
            else:
                QT, KT, VS, d, nkv, G, scale, OT = QbT, KbT, gvS, 64, 2, 4, GQA_SCALE, obT
            NKT = T // 128
            with ExitStack() as ph:
                Kt = [sbt(ph, f"Kt{i}", [128, T], BF16) for i in range(2)]
                Va = [sbt(ph, f"Va{i}", [128, NKT, 128], BF16) for i in range(2)]
                Qt = [sbt(ph, f"Qt{i}", [128, 512], BF16) for i in range(2)]
                pt = [sbt(ph, f"pt{i}", [128, 512], BF16) for i in range(4)]
                rc = sbt(ph, "rc", [128, 512]); ot = [sbt(ph, f"ot{i}", [64, 512], BF16) for i in range(2)]
                ps_s = [pst(ph, f"pss{i}", [128, 512]) for i in range(4)]
                ps_o = [pst(ph, f"pso{i}", [128, 512]) for i in range(2)]
                for i in range(2):
                    P.op("pool", lambda e, i=i: e.memset(Va[i][:, :, 64:128], 1.0), w=[f"Va{i}o"])
                VS_v = VS.rearrange("(kt p) c -> p kt c", p=128)
                jobs = []
                nq = 0; nun = 0
                for hk in range(nkv):
                    for gi in range(G):
                        h = hk * G + gi
                        for (t0, nn) in SPANS:
                            if t0 < NCTX and l == DEPTH - 1:
                                continue
                            kts = list(range(NCTX // 128)) if t0 < NCTX else list(range(NKT))
                            qi = nq % 2; nq += 1
                            oi = nun % 2; nun += 1
                            for ki, kt in enumerate(kts):
                                jobs.append(dict(hk=hk, h=h, t0=t0, nn=nn, kt=kt, ki=ki, nk=len(kts), qi=qi, oi=oi, bi=hk % 2,
                                                 first_of_head=(gi == 0 and ki == 0 and t0 == SPANS[1 if l == DEPTH - 1 else 0][0])))
                LOOK = 3

                def front(j, J):
                    bi_, qi, nn, kt, h, t0, hk = J["bi"], J["qi"], J["nn"], J["kt"], J["h"], J["t0"], J["hk"]
                    if J["first_of_head"]:
                        P.dma("sp", Kt[bi_][0:d, :], KT[hk * d:(hk + 1) * d, :], w=[f"Kt{bi_}"])
                        P.dma("sp", Va[bi_][:, :, 0:64], VS_v[:, :, hk * 64:(hk + 1) * 64], w=[f"Va{bi_}"])
                    if J["ki"] == 0:
                        P.dma("sp", Qt[qi][0:d, 0:nn], QT[h * d:(h + 1) * d, t0:t0 + nn], w=[f"Qt{qi}"])
                    si = j % 4
                    P.op("pe", lambda e: e.matmul(ps_s[si][:, 0:nn], Kt[bi_][0:d, kt * 128:(kt + 1) * 128], Qt[qi][0:d, 0:nn], start=True, stop=True),
                         r=[f"Kt{bi_}", f"Qt{qi}"], w=[f"pss{si}"])
                    P.op("act", lambda e: e.activation(pt[si][:, 0:nn], ps_s[si][:, 0:nn], AF.Exp, scale=float(scale)),
                         r=[f"pss{si}"], w=[f"pt{si}"])

                def back(j, J):
                    bi_, nn, kt, h, t0, ki, nk, oi = J["bi"], J["nn"], J["kt"], J["h"], J["t0"], J["ki"], J["nk"], J["oi"]
                    si = j % 4
                    po = ps_o[oi]; pok = f"pso{oi}"
                    P.op("pe", lambda e: e.matmul(po[:, 0:nn], Va[bi_][:, kt, :], pt[si][:, 0:nn], start=(ki == 0), stop=(ki == nk - 1)),
                         r=[f"Va{bi_}", f"Va{bi_}o", f"pt{si}"], w=[pok])
                    if ki == nk - 1:
                        P.op("dve", lambda e: e.reciprocal(rc[64:128, 0:nn], po[64:128, 0:nn]), r=[pok], w=["rc"])
                        P.op("dve", lambda e: e.tensor_tensor(ot[oi][:, 0:nn], po[0:64, 0:nn], rc[64:128, 0:nn], ALU.mult),
                             r=[pok, "rc"], w=[f"ot{oi}"])
                        P.dma("sp", OT[h * 64:(h + 1) * 64, t0:t0 + nn], ot[oi][:, 0:nn], r=[f"ot{oi}"], w=[("OT", h, t0)])

                for j in range(len(jobs) + LOOK):
                    if j < len(jobs):
                        front(j, jobs[j])
                    if j - LOOK >= 0:
                        back(j - LOOK, jobs[j - LOOK])
                P.barrier()

        def phase_rw1(l):
            with ExitStack() as ph:
                def Ft(name, w=512):
                    return sbt(ph, name, [128, w])
                raw = {x: Ft("raw" + x, 514) for x in "rkv"}
                cv = {x: Ft("cv" + x) for x in "rkv"}
                kkr = Ft("kkr"); sqk = Ft("sqk"); nrm = Ft("nrm"); kk = Ft("kk")
                low = Ft("low"); loa = Ft("loa")
                w2t = Ft("w2t"); a2t = Ft("a2t"); omka = sbt(ph, "omka", [128, 4])
                rmf = Ft("rmf"); rmb = Ft("rmb")
                sig = Ft("sig"); av = Ft("av"); cl = Ft("cl"); e2 = Ft("e2"); e3 = Ft("e3")
                tmp = Ft("tmp"); tmp2 = Ft("tmp2"); km = Ft("km")
                NB_ = 4
                At_l = [sbt(ph, f"At{i}", [128, 512], RW_DT) for i in range(NB_)]; Bt_l = [sbt(ph, f"Bt{i}", [128, 512], RW_DT) for i in range(NB_)]
                Kt_l = [sbt(ph, f"Kt_{i}", [128, 512], RW_DT) for i in range(NB_)]; Rt_l = [sbt(ph, f"Rt{i}", [128, 512], RW_DT) for i in range(NB_)]
                e1_l = [sbt(ph, f"e1{i}", [128, 512]) for i in range(NB_)]
                vb_l = [sbt(ph, f"vb{i}", [128, 512], RW_DT) for i in range(2)]; bon_l = [sbt(ph, f"bon{i}", [128, 512]) for i in range(2)]
                nob = {"n": 0, "v": 0}
                rk = Ft("rk")
                pw = pst(ph, "pw", [128, 512]); pa = pst(ph, "pa", [128, 512]); pn = pst(ph, "pn", [128, 512]); pb = pst(ph, "pb", [128, 512])
                P.dma("sp", w2t[:], Wd["rwkv_w2"][l].rearrange("d r c -> (d r) c"), w=["w2t"])
                P.dma("sp", a2t[:], Wd["rwkv_a2"][l].rearrange("d r c -> (d r) c"), w=["a2t"])
                P.dma("sp", rmf[:], Cd["rmask_f"], w=["rmf"])
                P.dma("sp", rmb[:], Cd["rmask_b"], w=["rmb"])
                P.op("dve", lambda e: e.tensor_scalar(omka[:], vF(l, F_KA, 4), -1.0, 1.0, ALU.mult, ALU.add), r=["vecF"], w=["omka"])
                for (t0, nn) in SPANS:
                    seg0, seg1 = (0, NCTX) if t0 < NCTX else (NCTX, T)
                    lo = max(t0 - 1, seg0); hi = min(t0 + nn + 1, seg1)
                    P.dma("sp", low[:, 0:nn], loT[0:128, t0:t0 + nn], w=["low"])
                    P.dma("sp", loa[:, 0:nn], loT[128:256, t0:t0 + nn], w=["loa"])
                    for c in range(4):
                        for xi, x in enumerate("rkv"):
                            rt = raw[x]
                            P.op("pool", lambda e, rt=rt: e.memset(rt[:, 0:1], 0.0), w=["raw" + x])
                            P.op("pool", lambda e, rt=rt, nn=nn: e.memset(rt[:, nn + 1:nn + 2], 0.0), w=["raw" + x])
                            r0 = xi * 512 + c * 128
                            P.dma("sp", rt[:, lo - (t0 - 1):hi - (t0 - 1)], rkvT[r0:r0 + 128, lo:hi], w=["raw" + x])
                            ch = xi * 4 + c
                            tp0, tp1, tp2 = (vF(l, F_CONV + j * 12 + ch) for j in range(3))
                            o = cv[x]
                            P.op("dve", lambda e, o=o, rt=rt, nn=nn, tp1=tp1: e.tensor_scalar(o[:, 0:nn], rt[:, 1:nn + 1], tp1, None, ALU.mult),
                                 r=["raw" + x], w=["cv" + x])
                            P.op("dve", lambda e, o=o, rt=rt, nn=nn, tp0=tp0: e.scalar_tensor_tensor(
                                o[:, 0:nn], rt[:, 0:nn], tp0, o[:, 0:nn], ALU.mult, ALU.add), r=["raw" + x], w=["cv" + x])
                            P.op("dve", lambda e, o=o, rt=rt, nn=nn, tp2=tp2: e.scalar_tensor_tensor(
                                o[:, 0:nn], rt[:, 2:nn + 2], tp2, o[:, 0:nn], ALU.mult, ALU.add), r=["raw" + x], w=["cv" + x])
                        rows = slice(c * 128, (c + 1) * 128)
                        vi_ = nob["v"] % 2; nob["v"] += 1
                        vb = vb_l[vi_]; bon = bon_l[vi_]; vbk = f"vb{vi_}"; bonk = f"bon{vi_}"
                        P.op("act", lambda e, nn=nn, vb=vb: e.copy(vb[:, 0:nn], cv["v"][:, 0:nn]), r=["cvv"], w=[vbk])
                        P.dma("sp", rwV[rows, t0:t0 + nn], vb[:, 0:nn], r=[vbk], w=[("rwV", c, t0)])
                        kkc = vF(l, F_KK + c)
                        P.op("dve", lambda e, nn=nn, kkc=kkc: e.tensor_scalar(kkr[:, 0:nn], cv["k"][:, 0:nn], kkc, None, ALU.mult), r=["cvk"], w=["kkr"])
                        P.op("pool", lambda e, nn=nn: e.tensor_tensor(sqk[:, 0:nn], kkr[:, 0:nn], kkr[:, 0:nn], ALU.mult), r=["kkr"], w=["sqk"])
                        P.op("pe", lambda e, nn=nn: e.matmul(pn[:, 0:nn], bones_f[:], sqk[:, 0:nn], start=True, stop=True), r=["sqk"], w=["pn"])
                        P.op("act", lambda e, nn=nn: e.activation(nrm[:, 0:nn], pn[:, 0:nn], AF.Sqrt), r=["pn"], w=["nrm"])
                        P.op("dve", lambda e, nn=nn: e.tensor_scalar(nrm[:, 0:nn], nrm[:, 0:nn], 1e-12, None, ALU.max), r=["nrm"], w=["nrm"])
                        P.op("dve", lambda e, nn=nn: e.reciprocal(nrm[:, 0:nn], nrm[:, 0:nn]), r=["nrm"], w=["nrm"])
                        P.op("dve", lambda e, nn=nn: e.tensor_tensor(kk[:, 0:nn], kkr[:, 0:nn], nrm[:, 0:nn], ALU.mult), r=["kkr", "nrm"], w=["kk"])
                        for d in range(2):
                            oi_ = nob["n"] % NB_; nob["n"] += 1
                            At = At_l[oi_]; Bt = Bt_l[oi_]; Kt_ = Kt_l[oi_]; Rt = Rt_l[oi_]; e1 = e1_l[oi_]
                            kAt, kBt, kKt, kRt, ke1 = f"At{oi_}", f"Bt{oi_}", f"Kt_{oi_}", f"Rt{oi_}", f"e1{oi_}"
                            ps_ = slice(64 * d, 64 * d + 64)
                            P.op("pe", lambda e, nn=nn, ps_=ps_, c=c: e.matmul(pw[:, 0:nn], w2t[ps_, c * 128:(c + 1) * 128], low[ps_, 0:nn], start=True, stop=True),
                                 r=["w2t", "low"], w=["pw"])
                            w0c = vF(l, F_W0 + d * 4 + c); a0c = vF(l, F_A0 + d * 4 + c)
                            P.op("act", lambda e, nn=nn, w0c=w0c: e.activation(sig[:, 0:nn], pw[:, 0:nn], AF.Sigmoid, bias=w0c), r=["pw", "vecF"], w=["sig"])
                            P.op("pool", lambda e, nn=nn: e.tensor_scalar(sig[:, 0:nn], sig[:, 0:nn], -0.6065306597126334, None, ALU.mult), r=["sig"], w=["sig"])
                            P.op("pe", lambda e, nn=nn, ps_=ps_, c=c: e.matmul(pa[:, 0:nn], a2t[ps_, c * 128:(c + 1) * 128], loa[ps_, 0:nn], start=True, stop=True),
                                 r=["a2t", "loa"], w=["pa"])
                            P.op("act", lambda e, nn=nn, a0c=a0c: e.activation(av[:, 0:nn], pa[:, 0:nn], AF.Sigmoid, bias=a0c), r=["pa", "vecF"], w=["av"])
                            if d == 0:
                                P.op("dve", lambda e, nn=nn: e.tensor_tensor_scan(cl[:, 0:nn], rmf[:, 0:nn], sig[:, 0:nn], 0.0, ALU.mult, ALU.add),
                                     r=["sig", "rmf"], w=["cl"])
                            else:
                                P.op("dve", lambda e, nn=nn: e.tensor_tensor_scan(cl[:, 0:nn][:, ::-1], rmb[:, 0:nn][:, ::-1], sig[:, 0:nn][:, ::-1],
                                                                                  0.0, ALU.mult, ALU.add), r=["sig", "rmb"], w=["cl"])
                            P.op("act", lambda e, nn=nn, e1=e1: e.activation(e1[:, 0:nn], cl[:, 0:nn], AF.Exp), r=["cl"], w=[ke1])
                            P.op("act", lambda e, nn=nn: e.activation(e2[:, 0:nn], cl[:, 0:nn], AF.Exp, scale=-1.0), r=["cl"], w=["e2"])
                            P.op("pool", lambda e, nn=nn: e.tensor_tensor(tmp[:, 0:nn], cl[:, 0:nn], sig[:, 0:nn], ALU.subtract), r=["cl", "sig"], w=["tmp"])
                            P.op("act", lambda e, nn=nn: e.activation(e3[:, 0:nn], tmp[:, 0:nn], AF.Exp), r=["tmp"], w=["e3"])
                            P.op("dve", lambda e, nn=nn, At=At: e.scalar_tensor_tensor(At[:, 0:nn], kk[:, 0:nn], -1.0, e3[:, 0:nn], ALU.mult, ALU.mult), r=["kk", "e3"], w=[kAt])
                            P.op("pool", lambda e, nn=nn: e.tensor_tensor(tmp[:, 0:nn], kk[:, 0:nn], av[:, 0:nn], ALU.mult), r=["kk", "av"], w=["tmp"])
                            P.op("pool", lambda e, nn=nn, Bt=Bt: e.tensor_tensor(Bt[:, 0:nn], tmp[:, 0:nn], e2[:, 0:nn], ALU.mult), r=["tmp", "e2"], w=[kBt])
                            kac = vF(l, F_KA + c); omc = omka[:, c:c + 1]
                            P.op("dve", lambda e, nn=nn, kac=kac, omc=omc: e.tensor_scalar(tmp2[:, 0:nn], av[:, 0:nn], kac, omc, ALU.mult, ALU.add),
                                 r=["av", "omka", "vecF"], w=["tmp2"])
                            P.op("pool", lambda e, nn=nn: e.tensor_tensor(km[:, 0:nn], cv["k"][:, 0:nn], tmp2[:, 0:nn], ALU.mult), r=["cvk", "tmp2"], w=["km"])
                            P.op("pool", lambda e, nn=nn, Kt_=Kt_: e.tensor_tensor(Kt_[:, 0:nn], km[:, 0:nn], e2[:, 0:nn], ALU.mult), r=["km", "e2"], w=[kKt])
                            P.op("dve", lambda e, nn=nn, Rt=Rt, e1=e1: e.tensor_tensor(Rt[:, 0:nn], cv["r"][:, 0:nn], e1[:, 0:nn], ALU.mult), r=["cvr", ke1], w=[kRt])
                            rkc = vF(l, F_RK + c)
                            P.op("dve", lambda e, nn=nn, rkc=rkc: e.scalar_tensor_tensor(rk[:, 0:nn], cv["r"][:, 0:nn], rkc, km[:, 0:nn], ALU.mult, ALU.mult),
                                 r=["cvr", "km", "vecF"], w=["rk"])
                            P.op("pe", lambda e, nn=nn, d=d: e.matmul(pb[:, 0:nn], bones_f[:], rk[:, 0:nn], start=(d == 0), stop=(d == 1)), r=["rk"], w=["pb"])
                            for (tl, dst, nm) in ((At, rwA, kAt), (Bt, rwB, kBt), (Kt_, rwK, kKt), (Rt, rwR, kRt), (e1, rwE, ke1)):
                                P.dma("sp", dst[d, rows, t0:t0 + nn], tl[:, 0:nn], r=[nm], w=[(nm, d, c, t0)])
                        P.op("dve", lambda e, nn=nn, bon=bon: e.tensor_tensor(bon[:, 0:nn], pb[:, 0:nn], cv["v"][:, 0:nn], ALU.mult), r=["pb", "cvv"], w=[bonk])
                        P.dma("sp", rwBon[rows, t0:t0 + nn], bon[:, 0:nn], r=[bonk], w=[("bon", c, t0)])
                P.barrier()

        def phase_rw2(l):
            import os
            with ExitStack() as ph:
                MK = [sbt(ph, f"MK{d}", [128, 256]) for d in range(2)]
                MK3 = [sbt(ph, f"MK3{d}", [128, 384]) for d in range(2)]
                m_ilT = sbt(ph, "m_ilT", [128, 128])
                P.dma("sp", m_ilT[:], Cd["m_il_T"], w=["m_ilT"])
                for d, (a_, b_, c_) in enumerate(((m_sl, m_slT, m_il), (m_slT, m_sl, m_ilT))):
                    P.op("dve", lambda e, d=d, a_=a_: e.tensor_copy(MK[d][:, 0:128], a_[:]), r=["m_ilT"], w=["MK"])
                    P.op("dve", lambda e, d=d, b_=b_: e.tensor_copy(MK[d][:, 128:256], b_[:]), r=["m_ilT"], w=["MK"])
                    P.op("dve", lambda e, d=d, a_=a_: e.tensor_copy(MK3[d][:, 0:128], a_[:]), r=["m_ilT"], w=["MK"])
                    P.op("dve", lambda e, d=d, c_=c_: e.tensor_copy(MK3[d][:, 128:256], c_[:]), r=["m_ilT"], w=["MK"])
                    P.op("dve", lambda e, d=d, c_=c_: e.tensor_copy(MK3[d][:, 256:384], c_[:]), r=["m_ilT"], w=["MK"])
                NCH = 4
                opnd = [[{X: sbt(ph, f"o{X}{ci}{b}", [128, 128], RW_DT) for X in "ABKRV"} for b in range(2)] for ci in range(NCH)]
                wc = [[sbt(ph, f"wc{ci}{b}", [128, 1]) for b in range(2)] for ci in range(NCH)]
                S2 = [[sbt(ph, f"S{d}{ci}", [128, 64]) for ci in range(NCH)] for d in range(2)]
                Sb2 = [[sbt(ph, f"Sb{d}{ci}", [128, 64], RW_DT) for ci in range(NCH)] for d in range(2)]
                S = list(S2[0]); Sb = list(Sb2[0])
                stackI_bf = sbt(ph, "stackI_bf", [128, 64], RW_DT)
                P.op("dve", lambda e: e.tensor_copy(stackI_bf[:], stackI[:]), w=["stackI_bf"])
                Np = [[sbt(ph, f"Np{ci}{b}", [128, 256], RW_DT) for b in range(2)] for ci in range(NCH)]
                NkB = [sbt(ph, f"NkB{ci}", [128, 384], RW_DT) for ci in range(NCH)]
                VBK = [sbt(ph, f"VBK{ci}", [128, 320], RW_DT) for ci in range(NCH)]
                Xs = [sbt(ph, f"Xs{ci}", [128, 64], RW_DT) for ci in range(NCH)]
                Us = [sbt(ph, f"Us{ci}", [128, 64], RW_DT) for ci in range(NCH)]
                Ys = [sbt(ph, f"Ys{ci}", [128, 64]) for ci in range(NCH)]
                tS = [sbt(ph, f"tS{ci}", [128, 64]) for ci in range(NCH)]
                Xps = [pst(ph, f"Xps{ci}", [128, 512]) for ci in range(NCH)]
                tps = [pst(ph, f"tps{i}", [128, 512]) for i in range(4)]
                tn = {"n": 0, "dq": 0, "cp": 0}

                def tnext():
                    i = tn["n"] % 4; tn["n"] += 1
                    return tps[i], f"tps{i}"

                def dq():
                    tn["dq"] += 1
                    return "sp"

                def cpy(out, in_, r, w):
                    tn["cp"] += 1
                    if tn["cp"] % 2:
                        P.op("act", lambda e: e.copy(out, in_), r=r, w=w)
                    else:
                        P.op("dve", lambda e: e.tensor_copy(out, in_), r=r, w=w)

                for ci in range(NCH):
                    for d_ in range(2):
                        P.op("pool", lambda e, t_=S2[d_][ci]: e.memset(t_[:], 0.0), w=[f"S{ci}"])
                        P.op("pool", lambda e, t_=Sb2[d_][ci]: e.memset(t_[:], 0.0), w=[f"Sb{ci}"])
                    for b in range(2):
                        for X in "ABKRV":
                            P.op("pool", lambda e, t_=opnd[ci][b][X]: e.memset(t_[:], 0.0), w=[f"o{X}{ci}{b}"])

                def load(ci, d, c, q, b):
                    ts = q * 64
                    O = opnd[ci][b]
                    for X, src in (("A", rwA[d]), ("B", rwB[d]), ("K", rwK[d]), ("R", rwR[d]), ("V", rwV)):
                        for h in range(2):
                            P.dma(dq(), O[X][h * 64:(h + 1) * 64, h * 64:(h + 1) * 64],
                                  src[c * 128 + h * 64:c * 128 + (h + 1) * 64, ts:ts + 64], w=[f"o{X}{ci}{b}"])
                    tl = ts + 63 if d == 0 else ts
                    if not os.environ.get("NOWC"):
                        P.dma(dq(), wc[ci][b][:, 0:1], rwE[d, c * 128:(c + 1) * 128, tl:tl + 1], w=[f"wc{ci}{b}"], allow_slow_non_contiguous=True)

                def compute(ci, d, c, q, b):
                    ts = q * 64
                    Sc = S[ci]; Sbc = Sb[ci]
                    O = opnd[ci][b]
                    ko = lambda X: f"o{X}{ci}{b}"
                    t1, k1 = tnext()
                    P.op("pe", lambda e: e.matmul(t1[:, 0:128], O["B"][:], O["A"][:], start=True, stop=True), r=[ko("B"), ko("A")], w=[k1])
                    P.op("pe", lambda e: e.matmul(t1[:, 128:256], O["A"][:], O["B"][:], start=True, stop=True), r=[ko("B"), ko("A")], w=[k1])
                    P.op("dve", lambda e: e.tensor_tensor(Np[ci][0][:, :], t1[:, 0:256], MK[d][:, :], ALU.mult), r=[k1, "MK"], w=[f"Np{ci}0N", f"Np{ci}0T"])
                    t2, k2 = tnext()
                    P.op("pe", lambda e: e.matmul(t2[:, 0:128], O["K"][:], O["A"][:], start=True, stop=True), r=[ko("K"), ko("A")], w=[k2])
                    P.op("pe", lambda e: e.matmul(t2[:, 128:256], O["B"][:], O["R"][:], start=True, stop=True), r=[ko("B"), ko("R")], w=[k2])
                    P.op("pe", lambda e: e.matmul(t2[:, 256:384], O["K"][:], O["R"][:], start=True, stop=True), r=[ko("K"), ko("R")], w=[k2])
                    P.op("dve", lambda e: e.tensor_tensor(NkB[ci][:, :], t2[:, 0:384], MK3[d][:, :], ALU.mult), r=[k2, "MK"], w=[f"NkB{ci}"])
                    t3, k3 = tnext()
                    P.op("pe", lambda e: e.matmul(t3[:, 0:64], O["V"][:], stackI_bf[:], start=True, stop=True), r=[ko("V")], w=[k3])
                    if RW_DT == F32:
                        P.op("pe", lambda e: e.transpose(t3[:, 64:192], O["B"][:], ident_f[:]), r=[ko("B")], w=[k3])
                        P.op("pe", lambda e: e.transpose(t3[:, 192:320], O["K"][:], ident_f[:]), r=[ko("K")], w=[k3])
                    else:
                        P.op("pe", lambda e: e.matmul(t3[:, 64:192], O["B"][:], ident_bf[:], start=True, stop=True), r=[ko("B")], w=[k3])
                        P.op("pe", lambda e: e.matmul(t3[:, 192:320], O["K"][:], ident_bf[:], start=True, stop=True), r=[ko("K")], w=[k3])
                    P.op("act", lambda e: e.copy(VBK[ci][:, :], t3[:, 0:320]), r=[k3], w=[f"VBK{ci}"])
                    yield
                    xp = Xps[ci]; xk = f"Xps{ci}"
                    P.op("pe", lambda e: e.matmul(xp[:, 0:64], O["A"][:], Sbc[:], start=True, stop=False), r=[ko("A"), f"Sb{ci}"], w=[xk])
                    P.op("pe", lambda e: e.matmul(xp[:, 0:64], NkB[ci][:, 0:128], VBK[ci][:, 0:64], start=False, stop=False),
                         r=[f"NkB{ci}", f"VBK{ci}"], w=[xk])
                    for k in range(6):
                        cur = Np[ci][k % 2]; ckN = f"Np{ci}{k % 2}N"; ckT = f"Np{ci}{k % 2}T"
                        cpy(Xs[ci][:, :], xp[:, 0:64], r=[xk], w=[f"Xs{ci}"])
                        yield
                        P.op("pe", lambda e, cur=cur, k=k: e.matmul(xp[:, 0:64], cur[:, 0:128], Xs[ci][:, :], start=False, stop=(k == 5)),
                             r=[ckN, f"Xs{ci}"], w=[xk])
                        if k < 5:
                            nxt = Np[ci][(k + 1) % 2]; nkN = f"Np{ci}{(k + 1) % 2}N"; nkT = f"Np{ci}{(k + 1) % 2}T"
                            tt, kt = tnext()
                            P.op("pe", lambda e, cur=cur, tt=tt: e.matmul(tt[:, 0:128], cur[:, 128:256], cur[:, 0:128], start=True, stop=True), r=[ckN, ckT], w=[kt])
                            cpy(nxt[:, 0:128], tt[:, 0:128], r=[kt], w=[nkN])
                            if k < 4:
                                tt2, kt2 = tnext()
                                P.op("pe", lambda e, nxt=nxt, tt2=tt2: e.transpose(tt2[:, 0:128], nxt[:, 0:128], ident_f[:]), r=[nkN], w=[kt2])
                                cpy(nxt[:, 128:256], tt2[:, 0:128], r=[kt2], w=[nkT])
                        yield
                    cpy(Us[ci][:, :], xp[:, 0:64], r=[xk], w=[f"Us{ci}"])
                    yield
                    ty, ky = tnext()
                    P.op("pe", lambda e: e.matmul(ty[:, 0:64], O["R"][:], Sbc[:], start=True, stop=False), r=[ko("R"), f"Sb{ci}"], w=[ky])
                    P.op("pe", lambda e: e.matmul(ty[:, 0:64], NkB[ci][:, 128:256], Us[ci][:, :], start=False, stop=False), r=[f"NkB{ci}", f"Us{ci}"], w=[ky])
                    P.op("pe", lambda e: e.matmul(ty[:, 0:64], NkB[ci][:, 256:384], VBK[ci][:, 0:64], start=False, stop=True), r=[f"NkB{ci}", f"VBK{ci}"], w=[ky])
                    P.op("pe", lambda e: e.matmul(ty[:, 64:128], VBK[ci][:, 64:192], Us[ci][:, :], start=True, stop=False), r=[f"VBK{ci}", f"Us{ci}"], w=[ky])
                    P.op("pe", lambda e: e.matmul(ty[:, 64:128], VBK[ci][:, 192:320], VBK[ci][:, 0:64], start=False, stop=True), r=[f"VBK{ci}"], w=[ky])
                    P.op("act", lambda e: e.copy(Ys[ci][:, :], ty[:, 0:64]), r=[ky], w=[f"Ys{ci}"])
                    P.op("dve", lambda e: e.tensor_tensor(tS[ci][:, :], ty[:, 64:128], Sc[:, :], ALU.add), r=[ky, f"S{ci}"], w=[f"tS{ci}"])
                    P.op("dve", lambda e: e.tensor_scalar(Sc[:, :], tS[ci][:, :], wc[ci][b][:, 0:1], None, ALU.mult),
                         r=[f"tS{ci}", f"wc{ci}{b}"], w=[f"S{ci}"], ss=True)
                    P.op("act", lambda e: e.copy(Sbc[:, :], Sc[:, :]), r=[f"S{ci}"], w=[f"Sb{ci}"])
                    for h in range(2):
                        P.dma(dq(), rwY[d, ts:ts + 64, c * 128 + h * 64:c * 128 + (h + 1) * 64], Ys[ci][h * 64:(h + 1) * 64, :],
                              r=[f"Ys{ci}"], w=[("rwY", d, c, q, h)])
                    yield

                nctx_ch = NCTX // CH
                _lim = int(os.environ.get("RW2_STEPS", "100000"))
                for d in range(int(os.environ.get("RW2_D0", "0")), int(os.environ.get("RW2_DIRS", "2"))):
                    if d == 0:
                        qs = list(range(NCHUNK))
                    else:
                        qs = list(range(nctx_ch - 1, -1, -1)) + list(range(NCHUNK - 1, nctx_ch - 1, -1))
                    for ci in range(NCH):
                        S[ci] = S2[d][ci]; Sb[ci] = Sb2[d][ci]
                        load(ci, d, ci, qs[0], 0)
                    qs = qs[:_lim]
                    for si, q in enumerate(qs):
                        b = si % 2
                        gens = []
                        for ci in range(NCH):
                            if si + 1 < len(qs):
                                load(ci, d, ci, qs[si + 1], 1 - b)
                            gens.append(compute(ci, d, ci, q, b))
                        alive = list(gens)
                        while alive:
                            nxt_alive = []
                            for g in alive:
                                try:
                                    next(g)
                                    nxt_alive.append(g)
                                except StopIteration:
                                    pass
                            alive = nxt_alive
                P.barrier()

        def phase_rw3(l):
            with ExitStack() as ph:
                g2t = sbt(ph, "g2t", [128, 512]); epsg = sbt(ph, "epsg", [128, 1])
                y0 = [sbt(ph, f"y0{i}", [128, 512]) for i in range(2)]; y1 = [sbt(ph, f"y1{i}", [128, 512]) for i in range(2)]
                ys = sbt(ph, "ys", [128, 512]); sq = sbt(ph, "sq", [128, 512]); yn = sbt(ph, "yn", [128, 512])
                st = [sbt(ph, f"st{i}", [128, 32]) for i in range(2)]
                log_ = sbt(ph, "log", [128, 512]); z = sbt(ph, "z", [128, 512]); bn = sbt(ph, "bn", [128, 512])
                og = [sbt(ph, f"og{i}", [128, 512], BF16) for i in range(2)]
                ptr = [pst(ph, f"ptr{c}", [128, 512]) for c in range(4)]
                pg = [pst(ph, f"pg{i}", [128, 512]) for i in range(2)]
                P.dma("sp", g2t[:], Wd["rwkv_g2"][l], w=["g2t"])
                P.op("dve", lambda e: e.memset(epsg[:], GN_EPS), w=["epsg"])
                n = 0; ng = 0
                for (t0, nn) in SPANS:
                    if t0 < NCTX and l == DEPTH - 1:
                        continue
                    for s_ in range(nn // 128):
                        i = n % 2; n += 1
                        r0 = t0 + s_ * 128
                        P.dma("sp", y0[i][:], rwY[0, r0:r0 + 128, :], w=[f"y0{i}"])
                        P.dma("pool", y1[i][:], rwY[1, r0:r0 + 128, :], w=[f"y1{i}"])
                        P.op("pool", lambda e, i=i: e.tensor_tensor(ys[:], y0[i][:], y1[i][:], ALU.add), r=[f"y0{i}", f"y1{i}"], w=["ys"])
                        sti = st[i]; sk = f"st{i}"
                        P.op("dve", lambda e, sti=sti: e.tensor_reduce(sti[:, 0:8], ys[:].rearrange("p (h d) -> p h d", d=64), AX.X, ALU.add),
                             r=["ys"], w=[sk])
                        P.op("dve", lambda e, sti=sti: e.tensor_scalar(sti[:, 8:16], sti[:, 0:8], -1.0 / 64, None, ALU.mult), r=[sk], w=[sk + "m"], ss=True)
                        for h in range(8):
                            P.op("dve", lambda e, sti=sti, h=h: e.tensor_scalar(yn[:, h * 64:(h + 1) * 64], ys[:, h * 64:(h + 1) * 64],
                                                                                 sti[:, 8 + h:9 + h], None, ALU.add), r=["ys", sk + "m"], w=["yc"], ss=True)
                        P.op("pool", lambda e: e.tensor_tensor(sq[:], yn[:], yn[:], ALU.mult), r=["yc"], w=["sq"])
                        P.op("dve", lambda e, sti=sti: e.tensor_reduce(sti[:, 16:24], sq[:].rearrange("p (h d) -> p h d", d=64), AX.X, ALU.add),
                             r=["sq"], w=[sk + "v"])
                        P.op("act", lambda e, sti=sti: e.activation(sti[:, 24:32], sti[:, 16:24], AF.Ln, scale=1.0 / 64, bias=epsg[:, 0:1]),
                             r=[sk + "v", "epsg"], w=[sk + "l"], ss=True)
                        P.op("act", lambda e, sti=sti: e.activation(sti[:, 16:24], sti[:, 24:32], AF.Exp, scale=-0.5), r=[sk + "l"], w=[sk + "r"], ss=True)
                        for h in range(8):
                            P.op("dve", lambda e, sti=sti, h=h: e.tensor_scalar(yn[:, h * 64:(h + 1) * 64], yn[:, h * 64:(h + 1) * 64],
                                                                                 sti[:, 16 + h:17 + h], None, ALU.mult), r=["yc", "sq", sk + "r"], w=["yn"], ss=True)
                        for c in range(4):
                            P.op("pe", lambda e, c=c, s_=s_: e.transpose(ptr[c][:, s_ * 128:(s_ + 1) * 128], yn[:, c * 128:(c + 1) * 128], ident_f[:]),
                                 r=["yn"], w=[f"ptr{c}"])
                    P.dma("sp", log_[:, 0:nn], loT[256:384, t0:t0 + nn], w=["log"])
                    for c in range(4):
                        P.op("act", lambda e, c=c, nn=nn: e.activation(z[:, 0:nn], ptr[c][:, 0:nn], AF.Identity, scale=vF(l, F_LNG + c), bias=vF(l, F_LNB + c)),
                             r=[f"ptr{c}", "vecF"], w=["z"])
                        P.dma("sp", bn[:, 0:nn], rwBon[c * 128:(c + 1) * 128, t0:t0 + nn], w=["bn"])
                        P.op("pool", lambda e, nn=nn: e.tensor_tensor(z[:, 0:nn], z[:, 0:nn], bn[:, 0:nn], ALU.add), r=["bn", "z"], w=["z"])
                        gi = ng % 2; ng += 1
                        P.op("pe", lambda e, c=c, nn=nn, gi=gi: e.matmul(pg[gi][:, 0:nn], g2t[:, c * 128:(c + 1) * 128], log_[:, 0:nn], start=True, stop=True),
                             r=["g2t", "log"], w=[f"pg{gi}"])
                        P.op("dve", lambda e, nn=nn, gi=gi: e.tensor_tensor(og[gi][:, 0:nn], pg[gi][:, 0:nn], z[:, 0:nn], ALU.mult), r=[f"pg{gi}", "z"], w=[f"og{gi}"])
                        P.dma("sp", ocT[c * 128:(c + 1) * 128, t0:t0 + nn], og[gi][:, 0:nn], r=[f"og{gi}"], w=[("ocT", c, t0)])
                P.barrier()

        def post_residual(tl, l, py, j, which, src, r0, dst, d0, tag):
            ysb = tl["ysb"]; ssq = tl["ssq"]; xt2 = tl["xt2"]; tmp = tl["tmp"]; junk = tl["junk2"]
            for hh in range(2):
                P.op("act", lambda e, hh=hh: e.activation(junk[:, 0:512], py[hh][:, :], AF.Square, accum_out=ssq[:, hh:hh + 1]),
                     r=[tag + f"py{hh}"], w=["junk2", f"ssq{hh}"])
                P.op("dve", lambda e, hh=hh: e.tensor_copy(ysb[:, hh * 512:(hh + 1) * 512], py[hh][:, :]), r=[tag + f"py{hh}"], w=["ysb"])
            P.op("dve", lambda e: e.tensor_tensor(ssq[:, 2:3], ssq[:, 0:1], ssq[:, 1:2], ALU.add), r=["ssq0", "ssq1"], w=["ssq2"], ss=True)
            P.op("act", lambda e: e.activation(ssq[:, 3:4], ssq[:, 2:3], AF.Ln, scale=1.0 / D, bias=tl["epsc"][:, 0:1]), r=["ssq2", "epsc"], w=["ssq3"], ss=True)
            P.op("act", lambda e: e.activation(ssq[:, 4:5], ssq[:, 3:4], AF.Exp, scale=-0.5), r=["ssq3"], w=["ssq4"], ss=True)
            P.dma("sp", xt2[:], src[r0:r0 + 128, :], w=["xt2"])
            P.op("pool", lambda e: e.tensor_tensor(tmp[:], ysb[:], GB[j][which][:], ALU.mult), r=["ysb"], w=["tmp"])
            P.op("dve", lambda e: e.scalar_tensor_tensor(tmp[:], tmp[:], ssq[:, 4:5], xt2[:], ALU.mult, ALU.add), r=["tmp", "ssq4", "xt2"], w=["tmp"], ss=True)
            P.dma("sp", dst[d0:d0 + 128, :], tmp[:], r=["tmp"], w=[(tag, "out", d0)])

        def post_tiles(ph, junk=None):
            tl = {}
            tl["ysb"] = sbt(ph, "ysb", [128, D]); tl["ssq"] = sbt(ph, "ssq", [128, 8]); tl["xt2"] = sbt(ph, "xt2", [128, D])
            tl["tmp"] = sbt(ph, "tmp", [128, D]); tl["junk2"] = junk if junk is not None else sbt(ph, "junk2", [128, 512], BF16)
            return tl

        def phase_merge(l):
            src = xin if l == 0 else xres
            with ExitStack() as ph:
                wb = sbt(ph, "wb", [128, 12, D], BF16); wo = sbt(ph, "wo", [128, 8, D], BF16)
                br = sbt(ph, "br", [128, 12, 512], BF16)
                gt = [sbt(ph, f"gt{i}", [128, 512], BF16) for i in range(3)]
                acc = sbt(ph, "acc", [128, 8, 512], BF16)
                ta = sbt(ph, "ta", [128, 512]); tb = sbt(ph, "tb", [128, 512])
                tl = post_tiles(ph)
                tl["epsc"] = sbt(ph, "epsc", [128, 1])
                P.op("dve", lambda e: e.memset(tl["epsc"][:], EPS), w=["epsc"])
                pm = [pst(ph, f"pm{i}", [128, 512]) for i in range(3)]
                py = [pst(ph, f"py{i}", [128, 512]) for i in range(2)]
                for b in range(3):
                    for k in range(4):
                        P.dma("pool", wb[:, b * 4 + k, :], Wd["w_branch"][l, b, k * 128:(k + 1) * 128, :], w=["wb"])
                for k in range(8):
                    P.dma("pool", wo[:, k, :], Wd["w_out"][l, k * 128:(k + 1) * 128, :], w=["wo"])
                n = 0
                for (t0, nn) in SPANS:
                    if t0 < NCTX and l == DEPTH - 1:
                        continue
                    j = 1 if t0 < NCTX else 0
                    for b, oT in enumerate((oaT, obT, ocT)):
                        P.dma("sp", br[:, b * 4:(b + 1) * 4, 0:nn], oT.rearrange("(k p) t -> p k t", p=128)[:, :, t0:t0 + nn], w=["br"])
                    for n_ in range(8):
                        for b in range(3):
                            i = n % 3; n += 1
                            g0 = b * 1024 + n_ * 128
                            P.dma("sp", gt[i][:, 0:nn], gateT[g0:g0 + 128, t0:t0 + nn], w=[f"gt{i}"])
                            for k in range(4):
                                P.op("pe", lambda e, i=i, b=b, k=k, n_=n_, nn=nn: e.matmul(
                                    pm[i][:, 0:nn], wb[:, b * 4 + k, n_ * 128:(n_ + 1) * 128], br[:, b * 4 + k, 0:nn], start=(k == 0), stop=(k == 3)),
                                    r=["wb", "br"], w=[f"pm{i}"])
                            if b == 0:
                                P.op("dve", lambda e, i=i, nn=nn: e.tensor_tensor(ta[:, 0:nn], pm[i][:, 0:nn], gt[i][:, 0:nn], ALU.mult), r=[f"pm{i}", f"gt{i}"], w=["ta"])
                            else:
                                P.op("dve", lambda e, i=i, nn=nn: e.tensor_tensor(tb[:, 0:nn], pm[i][:, 0:nn], gt[i][:, 0:nn], ALU.mult), r=[f"pm{i}", f"gt{i}"], w=["tb"])
                                if b == 1:
                                    P.op("pool", lambda e, nn=nn: e.tensor_tensor(ta[:, 0:nn], ta[:, 0:nn], tb[:, 0:nn], ALU.add), r=["ta", "tb"], w=["ta"])
                                else:
                                    P.op("pool", lambda e, nn=nn, n_=n_: e.tensor_tensor(acc[:, n_, 0:nn], ta[:, 0:nn], tb[:, 0:nn], ALU.add), r=["ta", "tb"], w=["acc"])
                    for s_ in range(nn // 128):
                        for hh in range(2):
                            for k in range(8):
                                P.op("pe", lambda e, hh=hh, k=k, s_=s_: e.matmul(py[hh][:, :], acc[:, k, s_ * 128:(s_ + 1) * 128], wo[:, k, hh * 512:(hh + 1) * 512],
                                                                                start=(k == 0), stop=(k == 7)), r=["acc", "wo"], w=[f"mpy{hh}"])
                        r0 = t0 + s_ * 128
                        post_residual(tl, l, py, j, 0, src, r0, xres, r0, "m")
                P.barrier()

        def phase_ffn(l):
            last = l == DEPTH - 1
            with ExitStack() as ph:
                w1 = sbt(ph, "w1", [128, 8, 2 * FFH], BF16); w2 = sbt(ph, "w2", [128, 22, D], BF16)
                nt = norm_tiles(ph, 1)
                hT = sbt(ph, "hT", [128, 8, 512], BF16)
                actT = sbt(ph, "actT", [128, 22, 512], BF16)
                sil = [sbt(ph, f"sil{i}", [128, 512]) for i in range(2)]
                tl = post_tiles(ph, nt["junk"]); tl["epsc"] = nt["epsc"]
                pgu = [pst(ph, f"pgu{i}", [128, 512]) for i in range(4)]
                py = [pst(ph, f"fpy{i}", [128, 512]) for i in range(2)]
                for k in range(8):
                    P.dma("pool", w1[:, k, :], Wd["ffn_w_in"][l, k * 128:(k + 1) * 128, :], w=["w1"])
                for cc in range(22):
                    P.dma("pool", w2[:, cc, :], Wd["ffn_w_out"][l, cc * 128:(cc + 1) * 128, :], w=["w2"])
                n = 0
                for (t0, nn) in SPANS:
                    if t0 < NCTX and last:
                        continue
                    j = 1 if t0 < NCTX else 0
                    norm_modulate_T(nt, l, xres, t0, nn, 1, hT, "f")
                    for cc in range(22):
                        i = n % 2; n += 1
                        pg_, pu_ = pgu[2 * i], pgu[2 * i + 1]
                        for k in range(8):
                            P.op("pe", lambda e, pg_=pg_, k=k, cc=cc, nn=nn: e.matmul(pg_[:, 0:nn], w1[:, k, cc * 128:(cc + 1) * 128], hT[:, k, 0:nn],
                                                                                      start=(k == 0), stop=(k == 7)), r=["w1", ("f", "hT")], w=[f"pgu{2 * i}"])
                        for k in range(8):
                            P.op("pe", lambda e, pu_=pu_, k=k, cc=cc, nn=nn: e.matmul(pu_[:, 0:nn], w1[:, k, FFH + cc * 128:FFH + (cc + 1) * 128], hT[:, k, 0:nn],
                                                                                      start=(k == 0), stop=(k == 7)), r=["w1", ("f", "hT")], w=[f"pgu{2 * i + 1}"])
                        P.op("act", lambda e, pg_=pg_, i=i, nn=nn: e.activation(sil[i][:, 0:nn], pg_[:, 0:nn], AF.Silu), r=[f"pgu{2 * i}"], w=[f"sil{i}"])
                        P.op("dve", lambda e, pu_=pu_, i=i, nn=nn, cc=cc: e.tensor_tensor(actT[:, cc, 0:nn], pu_[:, 0:nn], sil[i][:, 0:nn], ALU.mult),
                             r=[f"pgu{2 * i + 1}", f"sil{i}"], w=["actT"])
                    for s_ in range(nn // 128):
                        for hh in range(2):
                            for cc in range(22):
                                P.op("pe", lambda e, hh=hh, cc=cc, s_=s_: e.matmul(py[hh][:, :], actT[:, cc, s_ * 128:(s_ + 1) * 128], w2[:, cc, hh * 512:(hh + 1) * 512],
                                                                                  start=(cc == 0), stop=(cc == 21)), r=["actT", "w2"], w=[f"fpy{hh}"])
                        r0 = t0 + s_ * 128
                        if last:
                            post_residual(tl, l, py, j, 1, xres, r0, out_d, r0 - NCTX, "f")
                        else:
                            post_residual(tl, l, py, j, 1, xres, r0, xres, r0, "f")
                P.barrier()

        PH = {"mod": phase_mod, "inproj": phase_inproj, "mla": phase_mla, "gqa": phase_gqa,
              "attna": lambda l: phase_attn(l, "a"), "attnb": lambda l: phase_attn(l, "b"),
              "rw1": phase_rw1, "rw2": phase_rw2, "rw3": phase_rw3, "merge": phase_merge, "ffn": phase_ffn}
        seq = ["mod", "inproj", "rw1", "rw2", "rw3", "mla", "gqa", "attna", "attnb", "merge", "ffn"]
        if skip:
            seq = [x for x in seq if x not in skip]
        order = []
        for l in range(nlayers):
            order += [(x, l) for x in seq]
        for name, l in order:
            PH[name](l)
            if upto == (name, l):
                break
        P.barrier()
        P.emit()
        print("instructions:", P.ninstr, {e: P.cnt[e] for e in ENGS})
    return nc


def make_in_maps(inputs, cores=range(8)):
    inp = {k: np.asarray(v) for k, v in inputs.items()}
    consts = _host_consts()
    vf, vr = _host_vecs(inp)
    maps = []
    for b in cores:
        m = {}
        m["xin"] = np.ascontiguousarray(np.concatenate([inp["ctx"][b], inp["x"][b]], axis=0), dtype=np.float32)
        cc = np.stack([inp["c"][b].reshape(8, 128).T, inp["c_ctx"].reshape(8, 128).T], axis=-1)
        m["csil"] = np.ascontiguousarray(cc.reshape(128, 16), dtype=np.float32)
        m["vecF"] = vf
        m["vecR"] = vr
        for k in WEIGHTS:
            m[k] = np.ascontiguousarray(inp[k], dtype=np.float32)
        for k in CONST_SHAPES:
            m[k] = consts[k]
        maps.append(m)
    return maps


def kernel(**inputs):
    nc = build_program()
    maps = make_in_maps(inputs)
    res = run_bass_kernel_spmd(nc, maps, core_ids=list(range(8)))
    return np.stack([r["out"] for r in res.results], axis=0).astype(np.float32)
```
